# Optimizing a Trainium2 kernel written in Bass

```python
import math
import jax
import jax.numpy as jnp
from jax import lax
import numpy as np

D_MODEL = 1024
BATCH = 4
SEQ = 4096
DEPTH = 2
DEC_BATCH = 32
DEC_SEQ = 64
PAST_LEN = 4096

CHUNK = 64
Q_BLOCK = 128
SB_HEADS = 4
SB_HD = 64
SB_W = SB_HEADS * SB_HD
GLA_HEADS = 4
GLA_DK = 32
GLA_DV = 64
GLA_QK_W = GLA_HEADS * GLA_DK
GLA_W = GLA_HEADS * GLA_DV
GLA_GATE_RANK = 16
GLA_TAU = 16.0
GLA_BLOCK = 16
MLA_HEADS = 8
MLA_Q_LORA = 256
MLA_KV_LORA = 128
MLA_NOPE = 64
MLA_ROPE = 32
MLA_V = 64
MLA_W = MLA_HEADS * MLA_V
ROPE_THETA = 10000.0
MIX_W = SB_W + GLA_W + MLA_W
N_IN = 3 * SB_W + 2 * GLA_QK_W + 2 * GLA_W + GLA_GATE_RANK + MLA_Q_LORA + MLA_KV_LORA + MLA_ROPE
MEM_LEN = 256
MEM_HEADS = 4
MEM_HD = 128
MEM_W = MEM_HEADS * MEM_HD
D_FF = -(-8 * D_MODEL // (3 * 256)) * 256
NORM_EPS = 1e-6

kernel_name = 'hybrid_sb_gla_mla_stream_step'


def rmsnorm(x, g):
    xf = x.astype(jnp.float32)
    y = xf * lax.rsqrt(jnp.mean(xf * xf, axis=-1, keepdims=True) + NORM_EPS)
    return (y * g.astype(jnp.float32)).astype(x.dtype)


def rope(x, pos):
    half = x.shape[-1] // 2
    freqs = ROPE_THETA ** (-jnp.arange(half, dtype=jnp.float32) / half)
    ang = pos.astype(jnp.float32)[:, None] * freqs[None, :]
    shape = (pos.shape[0],) + (1,) * (x.ndim - 3) + (half,)
    cos = jnp.cos(ang).reshape(shape)
    sin = jnp.sin(ang).reshape(shape)
    xf = x.astype(jnp.float32)
    x1, x2 = xf[..., :half], xf[..., half:]
    return jnp.concatenate([x1 * cos - x2 * sin, x1 * sin + x2 * cos], axis=-1).astype(x.dtype)


def to_blocks(a, qb):
    B, T = a.shape[0], a.shape[1]
    return jnp.swapaxes(a.reshape((B, T // qb, qb) + a.shape[2:]), 0, 1)


def from_blocks(a):
    nb, B, qb = a.shape[:3]
    return jnp.swapaxes(a, 0, 1).reshape((B, nb * qb) + a.shape[3:])


def stick_breaking_attention(q, k, v, q_pos, k_pos):
    T = q.shape[1]
    qb = Q_BLOCK if T % Q_BLOCK == 0 else T
    scale = 1.0 / math.sqrt(q.shape[-1])

    def block(args):
        qblk, pblk = args
        z = jnp.einsum('bqhd,bkhd->bhqk', qblk, k).astype(jnp.float32) * scale
        visible = k_pos[None, :] < pblk[:, None]
        log_keep = jnp.where(visible, jax.nn.log_sigmoid(-z), 0.0)
        later = lax.cumsum(log_keep, axis=3, reverse=True) - log_keep
        wts = jnp.where(visible, jnp.exp(jax.nn.log_sigmoid(z) + later), 0.0)
        return jnp.einsum('bhqk,bkhd->bqhd', wts.astype(v.dtype), v)

    out = lax.map(block, (to_blocks(q, qb), q_pos.reshape(-1, qb)))
    return from_blocks(out)


def gla_recurrence(q, k, v, log_a, s0):
    T, dk = q.shape[1], q.shape[-1]
    pad = (-T) % GLA_BLOCK
    scale = dk ** -0.5

    def prep(a):
        a = jnp.pad(a.astype(jnp.float32), ((0, 0), (0, pad), (0, 0), (0, 0)))
        return to_blocks(a, GLA_BLOCK)

    tril = jnp.tril(jnp.ones((GLA_BLOCK, GLA_BLOCK), dtype=bool))

    def step(S, blk):
        qc, kc, vc, ac = blk
        b = jnp.cumsum(ac, axis=1)
        qt = qc * jnp.exp(b) * scale
        kt = kc * jnp.exp(-b)
        att = jnp.where(tril, jnp.einsum('bchk,bshk->bhcs', qt, kt), 0.0)
        o = jnp.einsum('bchk,bhkv->bchv', qt, S) + jnp.einsum('bhcs,bshv->bchv', att, vc)
        b_last = b[:, -1]
        kd = kc * jnp.exp(b_last[:, None] - b)
        S = jnp.exp(b_last)[..., None] * S + jnp.einsum('bchk,bchv->bhkv', kd, vc)
        return S, o

    S, o = lax.scan(step, s0.astype(jnp.float32), (prep(q), prep(k), prep(v), prep(log_a)))
    o = from_blocks(o)[:, :T]
    return o.astype(v.dtype), S.astype(s0.dtype)


def chunk_causal_attention(q_nope, q_rope, k_nope, k_rope, v, q_pos, k_pos):
    T = q_nope.shape[1]
    qb = Q_BLOCK if T % Q_BLOCK == 0 else T
    scale = (q_nope.shape[-1] + q_rope.shape[-1]) ** -0.5
    k_chunk = k_pos // CHUNK

    def block(args):
        qn, qr, pb = args
        s = (jnp.einsum('bqhd,bkhd->bhqk', qn, k_nope)
             + jnp.einsum('bqhr,bkr->bhqk', qr, k_rope)).astype(jnp.float32) * scale
        visible = k_chunk[None, :] <= (pb // CHUNK)[:, None]
        s = jnp.where(visible, s, -1e30)
        p = jax.nn.softmax(s, axis=-1).astype(v.dtype)
        return jnp.einsum('bhqk,bkhd->bqhd', p, v)

    out = lax.map(block, (to_blocks(q_nope, qb), to_blocks(q_rope, qb), q_pos.reshape(-1, qb)))
    return from_blocks(out)


def mla_attention(cq, ckv, kr_raw, lat_past, kr_past, q_pos, k_pos, w):
    B, T = cq.shape[:2]
    q = (rmsnorm(cq, w['g_cq']) @ w['w_uq']).reshape(B, T, MLA_HEADS, MLA_NOPE + MLA_ROPE)
    q_nope = rmsnorm(q[..., :MLA_NOPE], w['g_qn'])
    q_rope = rope(rmsnorm(q[..., MLA_NOPE:], w['g_qr']), q_pos)
    lat = rmsnorm(ckv, w['g_ckv'])
    kr = rope(rmsnorm(kr_raw, w['g_kr']), q_pos)
    lat_all = jnp.concatenate([lat_past, lat], axis=1)
    kr_all = jnp.concatenate([kr_past, kr], axis=1)
    L = lat_all.shape[1]
    kv = (lat_all @ w['w_ukv']).reshape(B, L, MLA_HEADS, MLA_NOPE + MLA_V)
    k_nope = rmsnorm(kv[..., :MLA_NOPE], w['g_kn'])
    v = kv[..., MLA_NOPE:]
    out = chunk_causal_attention(q_nope, q_rope, k_nope, kr_all, v, q_pos, k_pos)
    return out, lat, kr


def memory_kv(mem, w):
    B, M, _ = mem.shape
    h = rmsnorm(mem, w['g_mem_norm'])
    k = rmsnorm((h @ w['w_ck']).reshape(B, M, MEM_HEADS, MEM_HD), w['g_ckn'])
    v = (h @ w['w_cv']).reshape(B, M, MEM_HEADS, MEM_HD)
    return k, v


def memory_cross_attention(h, mem_k, mem_v, w):
    B, T, _ = h.shape
    q = rmsnorm((h @ w['w_cq']).reshape(B, T, MEM_HEADS, MEM_HD), w['g_cqn'])
    s = jnp.einsum('bqhd,bkhd->bhqk', q, mem_k).astype(jnp.float32) * (MEM_HD ** -0.5)
    p = jax.nn.softmax(s, axis=-1).astype(mem_v.dtype)
    o = jnp.einsum('bhqk,bkhd->bqhd', p, mem_v).reshape(B, T, MEM_W)
    return o @ w['w_co']


def trunk_layer(x, sb_k_past, sb_v_past, gla_s0, lat_past, kr_past, mem_k, mem_v, w):
    B, T, _ = x.shape
    P = sb_k_past.shape[1]
    q_pos = P + jnp.arange(T, dtype=jnp.int32)
    k_pos = jnp.arange(P + T, dtype=jnp.int32)
    h = rmsnorm(x, w['g_mix_norm'])
    proj = h @ w['w_in']
    sizes = (SB_W, SB_W, SB_W, GLA_QK_W, GLA_QK_W, GLA_W, GLA_GATE_RANK, GLA_W,
             MLA_Q_LORA, MLA_KV_LORA, MLA_ROPE)
    offsets = []
    acc = 0
    for s in sizes[:-1]:
        acc += s
        offsets.append(acc)
    qa, ka, va, qg, kg, vg, ag, rg, cq, ckv, kr_raw = jnp.split(proj, offsets, axis=-1)
    qa = qa.reshape(B, T, SB_HEADS, SB_HD)
    ka = ka.reshape(B, T, SB_HEADS, SB_HD)
    va = va.reshape(B, T, SB_HEADS, SB_HD)
    o_a = stick_breaking_attention(qa, jnp.concatenate([sb_k_past, ka], axis=1),
                                   jnp.concatenate([sb_v_past, va], axis=1), q_pos, k_pos)
    o_a = rmsnorm(o_a, w['g_sb_out']).reshape(B, T, SB_W)
    log_a = jax.nn.log_sigmoid((ag @ w['w_gla_gate'] + w['b_gla_gate']).astype(jnp.float32)) / GLA_TAU
    o_b, gla_s = gla_recurrence(qg.reshape(B, T, GLA_HEADS, GLA_DK), kg.reshape(B, T, GLA_HEADS, GLA_DK),
                                vg.reshape(B, T, GLA_HEADS, GLA_DV),
                                log_a.reshape(B, T, GLA_HEADS, GLA_DK), gla_s0)
    o_b = rmsnorm(o_b, w['g_gla_out']).reshape(B, T, GLA_W) * jax.nn.silu(rg)
    o_c, lat, kr = mla_attention(cq, ckv, kr_raw, lat_past, kr_past, q_pos, k_pos, w)
    o_c = rmsnorm(o_c, w['g_mla_out']).reshape(B, T, MLA_W)
    x = x + jnp.concatenate([o_a, o_b, o_c], axis=-1) @ w['w_out']
    x = x + memory_cross_attention(rmsnorm(x, w['g_cross_norm']), mem_k, mem_v, w)
    h = rmsnorm(x, w['g_ffn_norm'])
    x = x + (jax.nn.silu(h @ w['w_gate']) * (h @ w['w_up'])) @ w['w_down']
    return x, ka, va, gla_s, lat, kr


def setup_inputs(seed: int = 0) -> dict:
    key = jax.random.key(seed)
    ks = iter(jax.random.split(key, 64))
    f32 = jnp.float32

    def nrm(shape, scale=1.0):
        return jax.random.normal(next(ks), shape, f32) * scale

    def gain(n):
        return 1.0 + 0.02 * jax.random.normal(next(ks), (DEPTH, n), f32)

    return {
        'x_prompt': nrm((BATCH, SEQ, D_MODEL)),
        'x_sample': nrm((DEC_BATCH, DEC_SEQ, D_MODEL)),
        'mem_prompt': nrm((BATCH, MEM_LEN, D_MODEL)),
        'cache_sb_k': nrm((DEPTH, DEC_BATCH, PAST_LEN, SB_HEADS, SB_HD)),
        'cache_sb_v': nrm((DEPTH, DEC_BATCH, PAST_LEN, SB_HEADS, SB_HD)),
        'state_gla': nrm((DEPTH, DEC_BATCH, GLA_HEADS, GLA_DK, GLA_DV), 0.5),
        'cache_mla_latent': nrm((DEPTH, DEC_BATCH, PAST_LEN, MLA_KV_LORA)),
        'cache_mla_krope': nrm((DEPTH, DEC_BATCH, PAST_LEN, MLA_ROPE)),
        'cache_mem_k': nrm((DEPTH, DEC_BATCH, MEM_LEN, MEM_HEADS, MEM_HD)),
        'cache_mem_v': nrm((DEPTH, DEC_BATCH, MEM_LEN, MEM_HEADS, MEM_HD)),
        'g_mix_norm': gain(D_MODEL),
        'w_in': nrm((DEPTH, D_MODEL, N_IN), D_MODEL ** -0.5),
        'w_gla_gate': nrm((DEPTH, GLA_GATE_RANK, GLA_QK_W), GLA_GATE_RANK ** -0.5),
        'b_gla_gate': nrm((DEPTH, GLA_QK_W), 0.1),
        'g_gla_out': gain(GLA_DV),
        'g_sb_out': gain(SB_HD),
        'g_cq': gain(MLA_Q_LORA),
        'w_uq': nrm((DEPTH, MLA_Q_LORA, MLA_HEADS * (MLA_NOPE + MLA_ROPE)), MLA_Q_LORA ** -0.5),
        'g_qn': gain(MLA_NOPE),
        'g_qr': gain(MLA_ROPE),
        'g_kr': gain(MLA_ROPE),
        'g_ckv': gain(MLA_KV_LORA),
        'w_ukv': nrm((DEPTH, MLA_KV_LORA, MLA_HEADS * (MLA_NOPE + MLA_V)), MLA_KV_LORA ** -0.5),
        'g_kn': gain(MLA_NOPE),
        'g_mla_out': gain(MLA_V),
        'w_out': nrm((DEPTH, MIX_W, D_MODEL), MIX_W ** -0.5),
        'g_cross_norm': gain(D_MODEL),
        'g_mem_norm': gain(D_MODEL),
        'w_cq': nrm((DEPTH, D_MODEL, MEM_W), D_MODEL ** -0.5),
        'w_ck': nrm((DEPTH, D_MODEL, MEM_W), D_MODEL ** -0.5),
        'w_cv': nrm((DEPTH, D_MODEL, MEM_W), D_MODEL ** -0.5),
        'g_cqn': gain(MEM_HD),
        'g_ckn': gain(MEM_HD),
        'w_co': nrm((DEPTH, MEM_W, D_MODEL), MEM_W ** -0.5),
        'g_ffn_norm': gain(D_MODEL),
        'w_gate': nrm((DEPTH, D_MODEL, D_FF), D_MODEL ** -0.5),
        'w_up': nrm((DEPTH, D_MODEL, D_FF), D_MODEL ** -0.5),
        'w_down': nrm((DEPTH, D_FF, D_MODEL), D_FF ** -0.5),
    }


def reference(x_prompt, x_sample, mem_prompt, cache_sb_k, cache_sb_v, state_gla, cache_mla_latent,
              cache_mla_krope, cache_mem_k, cache_mem_v, g_mix_norm, w_in, w_gla_gate, b_gla_gate,
              g_gla_out, g_sb_out, g_cq, w_uq, g_qn, g_qr, g_kr, g_ckv, w_ukv, g_kn, g_mla_out, w_out,
              g_cross_norm, g_mem_norm, w_cq, w_ck, w_cv, g_cqn, g_ckn, w_co, g_ffn_norm, w_gate,
              w_up, w_down):
    dt = x_prompt.dtype
    B = x_prompt.shape[0]
    empty_kv = jnp.zeros((B, 0, SB_HEADS, SB_HD), dt)
    empty_lat = jnp.zeros((B, 0, MLA_KV_LORA), dt)
    empty_kr = jnp.zeros((B, 0, MLA_ROPE), dt)
    zero_state = jnp.zeros((B, GLA_HEADS, GLA_DK, GLA_DV), dt)
    xp, xs = x_prompt, x_sample
    p_k, p_v, p_s, p_lat, p_kr, p_mk, p_mv = [], [], [], [], [], [], []
    s_k, s_v, s_s, s_lat, s_kr = [], [], [], [], []
    for l in range(DEPTH):
        w = {
            'g_mix_norm': g_mix_norm[l], 'w_in': w_in[l], 'w_gla_gate': w_gla_gate[l],
            'b_gla_gate': b_gla_gate[l], 'g_gla_out': g_gla_out[l], 'g_sb_out': g_sb_out[l],
            'g_cq': g_cq[l], 'w_uq': w_uq[l], 'g_qn': g_qn[l], 'g_qr': g_qr[l], 'g_kr': g_kr[l],
            'g_ckv': g_ckv[l], 'w_ukv': w_ukv[l], 'g_kn': g_kn[l], 'g_mla_out': g_mla_out[l],
            'w_out': w_out[l], 'g_cross_norm': g_cross_norm[l], 'g_mem_norm': g_mem_norm[l],
            'w_cq': w_cq[l], 'w_ck': w_ck[l], 'w_cv': w_cv[l], 'g_cqn': g_cqn[l], 'g_ckn': g_ckn[l],
            'w_co': w_co[l], 'g_ffn_norm': g_ffn_norm[l], 'w_gate': w_gate[l], 'w_up': w_up[l],
            'w_down': w_down[l],
        }
        mk, mv = memory_kv(mem_prompt, w)
        xp, ka, va, st, lat, kr = trunk_layer(xp, empty_kv, empty_kv, zero_state, empty_lat, empty_kr, mk, mv, w)
        p_k.append(ka)
        p_v.append(va)
        p_s.append(st)
        p_lat.append(lat)
        p_kr.append(kr)
        p_mk.append(mk)
        p_mv.append(mv)
        xs, ka, va, st, lat, kr = trunk_layer(xs, cache_sb_k[l], cache_sb_v[l], state_gla[l],
                                              cache_mla_latent[l], cache_mla_krope[l],
                                              cache_mem_k[l], cache_mem_v[l], w)
        s_k.append(ka)
        s_v.append(va)
        s_s.append(st)
        s_lat.append(lat)
        s_kr.append(kr)
    new_sb_k_prompt = jnp.stack(p_k)
    new_sb_v_prompt = jnp.stack(p_v)
    new_state_gla_prompt = jnp.stack(p_s)
    new_mla_latent_prompt = jnp.stack(p_lat)
    new_mla_krope_prompt = jnp.stack(p_kr)
    new_mem_k_prompt = jnp.stack(p_mk)
    new_mem_v_prompt = jnp.stack(p_mv)
    new_sb_k_sample = jnp.stack(s_k)
    new_sb_v_sample = jnp.stack(s_v)
    new_state_gla_sample = jnp.stack(s_s)
    new_mla_latent_sample = jnp.stack(s_lat)
    new_mla_krope_sample = jnp.stack(s_kr)
    return (xp, xs, new_sb_k_prompt, new_sb_v_prompt, new_state_gla_prompt, new_mla_latent_prompt,
            new_mla_krope_prompt, new_mem_k_prompt, new_mem_v_prompt, new_sb_k_sample, new_sb_v_sample,
            new_state_gla_sample, new_mla_latent_sample, new_mla_krope_sample)
```

```python
from contextlib import ExitStack
import numpy as np
import concourse.bass as bass
import concourse.mybir as mybir
from concourse.bass_utils import run_bass_kernel_spmd

F32 = mybir.dt.float32
BF = mybir.dt.bfloat16
AF = mybir.ActivationFunctionType
ALU = mybir.AluOpType
AX = mybir.AxisListType

D = 1024
NIN = 1968
DFF = 2816
EPS = 1e-6
GL = 48


class Cfg:
    def __init__(self, SEQ=4096, PAST=4096, NSS=4):
        self.SEQ = SEQ
        self.PAST = PAST
        self.NSS = NSS
        self.NG = SEQ // 512
        self.NK = max(SEQ, PAST + 512)
        self.NTOT = PAST + 64
        self.NPOS = max(SEQ, PAST + 64)


class Buf:
    __slots__ = ("w", "r", "name")

    def __init__(self, name=""):
        self.w = None
        self.r = {}
        self.name = name


class Tl:
    __slots__ = ("t", "b")

    def __init__(self, t, b):
        self.t = t
        self.b = b


class Sched:
    ROT = 30000
    KD = 8

    def __init__(self, nc, es):
        self.nc = nc
        self.es = es
        self.eng = {}
        for name, kind in (("pe", "c"), ("act", "c"), ("dve", "c"), ("pool", "d"), ("sp", "d")):
            self.eng[name] = dict(ops=[], n=0, kind=kind, known={})
        self.nt = 0

    def sb(self, shape, dt, name=None):
        self.nt += 1
        name = name or ("t%d" % self.nt)
        t = self.es.enter_context(self.nc.sbuf_tensor(name, list(shape), dt))
        return Tl(t, Buf(name))

    def ps(self, name):
        t = self.es.enter_context(self.nc.psum_tensor(name, [128, 512], F32))
        return Tl(t, Buf(name))

    def dram(self, name, shape, dt):
        return self.nc.dram_tensor(name, list(shape), dt, kind="Internal").ap()

    def _kv(self, dep):
        en, seq = dep
        if self.eng[en]["kind"] == "c":
            return (en, (seq - 1) // self.ROT), (seq - 1) % self.ROT + 1
        k = seq - 1
        return (en, k % self.KD), 16 * (k // self.KD + 1)

    def op(self, en, fn, r=(), w=()):
        e = self.eng[en]
        deps = set()
        for b in r:
            b = b.b if isinstance(b, Tl) else b
            if b.w is not None:
                deps.add(b.w)
        for b in w:
            b = b.b if isinstance(b, Tl) else b
            if b.w is not None:
                deps.add(b.w)
            for x in b.r.values():
                deps.update(x)
        e["n"] += 1
        seq = e["n"]
        me = (en, seq)
        if e["kind"] == "d" and seq > self.KD:
            deps.add((en, seq - self.KD))
        waits = []
        for d in deps:
            if d[0] == en and en == "pe":
                continue
            key, val = self._kv(d)
            if e["known"].get(key, 0) >= val:
                continue
            e["known"][key] = val
            waits.append((key, val))
        e["ops"].append((waits, fn, self._kv(me)))
        for b in r:
            b = b.b if isinstance(b, Tl) else b
            lst = b.r.setdefault(en, [])
            if e["kind"] == "c":
                lst[:] = [me]
            else:
                lst.append(me)
        for b in w:
            b = b.b if isinstance(b, Tl) else b
            b.w = me
            b.r = {}

    def finish(self):
        nc = self.nc
        fin = []
        for en in ("pool", "sp"):
            n = self.eng[en]["n"]
            for s in range(max(1, n - self.KD + 1), n + 1):
                fin.append(self._kv((en, s)))
        for en in ("pe", "act", "dve"):
            n = self.eng[en]["n"]
            if n:
                fin.append(self._kv((en, n)))
        keys = set()
        for e in self.eng.values():
            for waits, fn, kv in e["ops"]:
                keys.add(kv[0])
        sems = {}
        for k in sorted(keys):
            sems[k] = self.es.enter_context(nc.semaphore("s_%s_%d" % k))
        block = self.es.enter_context(nc.Block())
        engs = self.eng

        def replay(en, eobj, extra=()):
            inc = 1 if engs[en]["kind"] == "c" else 16
            for waits, fn, kv in engs[en]["ops"]:
                for k, v in waits:
                    eobj.wait_ge(sems[k], v)
                fn(eobj).then_inc(sems[kv[0]], inc)
            for k, v in extra:
                eobj.wait_ge(sems[k], v)

        @block.tensor
        def _(t):
            replay("pe", t)

        @block.scalar
        def _(a):
            replay("act", a)

        @block.vector
        def _(v):
            replay("dve", v)

        @block.gpsimd
        def _(g):
            replay("pool", g)

        @block.sync
        def _(s):
            replay("sp", s, fin)


class Pool:
    def __init__(self, S, shape, dt, n, name):
        self.tiles = [S.sb(shape, dt, "%s%d" % (name, i)) for i in range(n)]
        self.i = 0

    def get(self):
        t = self.tiles[self.i % len(self.tiles)]
        self.i += 1
        return t


def run_interleaved(gens, width, gap=0):
    active = []
    it = iter(gens)
    done = False
    since = gap
    while True:
        while not done and len(active) < width and (since >= gap or not active):
            g = next(it, None)
            if g is None:
                done = True
                break
            active.append(g)
            since = 0
            if gap > 0:
                break
        since += 1
        if not active:
            break
        for g in list(active):
            try:
                next(g)
            except StopIteration:
                active.remove(g)


def build(cfg):
    nc = bass.Bass("TRN2", target_bir_lowering=False)
    SEQ, PAST, NSS, NG, NK = cfg.SEQ, cfg.PAST, cfg.NSS, cfg.NG, cfg.NK
    NSEQ = 1 + NSS
    NTS = NSS * 64
    NKBS = PAST // 512 + 1

    def din(name, shape):
        return nc.dram_tensor(name, list(shape), F32, kind="ExternalInput").ap()

    def dout(name, shape):
        return nc.dram_tensor(name, list(shape), F32, kind="ExternalOutput").ap()

    xp = din("xp", [SEQ, D])
    xs = din("xs", [NTS, D])
    memp = din("memp", [256, D])
    csk = din("csk", [2, NSS, PAST, 256])
    csv = din("csv", [2, NSS, PAST, 256])
    sgla = din("sgla", [2, NSS, 4, 32, 64])
    clat = din("clat", [2, NSS, PAST, 128])
    ckr = din("ckr", [2, NSS, PAST, 32])
    cmk = din("cmk", [2, NSS, 256, 512])
    cmv = din("cmv", [2, NSS, 256, 512])
    w_in = din("w_in", [2, D, NIN])
    w_gg = din("w_gla_gate", [2, 16, 128])
    w_uq = din("w_uq", [2, 256, 768])
    w_ukv = din("w_ukv", [2, 128, 1024])
    w_out = din("w_out", [2, D, D])
    w_cq = din("w_cq", [2, D, 512])
    w_ck = din("w_ck", [2, D, 512])
    w_cv = din("w_cv", [2, D, 512])
    w_co = din("w_co", [2, 512, D])
    w_gate = din("w_gate", [2, D, DFF])
    w_up = din("w_up", [2, D, DFF])
    w_down = din("w_down", [2, DFF, D])
    gtab = din("gtab", [128, 2 * GL])
    cmat = din("cmat", [128, 13 * 128])
    ropec = din("ropec", [96, cfg.NPOS])
    ropes = din("ropes", [96, cfg.NPOS])
    mdiag = din("mdiag", [128, 4 * 512])

    yp = dout("yp", [SEQ, D])
    ys = dout("ys", [NTS, D])
    o_sbk_p = dout("sbk_p", [2, SEQ, 256])
    o_sbv_p = dout("sbv_p", [2, SEQ, 256])
    o_gla_p = dout("gla_p", [2, 4, 32, 64])
    o_lat_p = dout("lat_p", [2, SEQ, 128])
    o_kr_p = dout("kr_p", [2, SEQ, 32])
    o_mk_p = dout("mk_p", [2, 256, 512])
    o_mv_p = dout("mv_p", [2, 256, 512])
    o_sbk_s = dout("sbk_s", [2, NTS, 256])
    o_sbv_s = dout("sbv_s", [2, NTS, 256])
    o_gla_s = dout("gla_s", [2, NSS, 4, 32, 64])
    o_lat_s = dout("lat_s", [2, NTS, 128])
    o_kr_s = dout("kr_s", [2, NTS, 32])

    es = ExitStack()
    S = Sched(nc, es)
    c_sbk = S.dram("c_sbk", [2, NSEQ, 64, 4, NK], BF)
    c_sbv = S.dram("c_sbv", [2, NSEQ, NK, 256], BF)
    c_mk = S.dram("c_mk", [2, NSEQ, 96, 8, NK], BF)
    c_mv = S.dram("c_mv", [2, NSEQ, 8, NK, 65], BF)
    c_memk = S.dram("c_memk", [2, NSEQ, 128, 4, 256], BF)
    c_memv = S.dram("c_memv", [2, NSEQ, 256, 512], BF)
    cbuf = {}

    def CB(l, s, kb):
        k = (l, s, kb)
        if k not in cbuf:
            cbuf[k] = Buf("cb%d_%d_%d" % k)
        return cbuf[k]

    membuf = {(l, s): Buf("mem%d_%d" % (l, s)) for l in range(2) for s in range(NSEQ)}

    PSR = [S.ps("psr%d" % i) for i in range(6)]
    PSL = [S.ps("psl%d" % i) for i in range(2)]
    psi = [0, 0]

    def psr():
        psi[0] += 1
        return PSR[psi[0] % 6]

    def psl():
        psi[1] += 1
        return PSL[psi[1] % 2]

    GT = S.sb([128, 2 * GL], F32, "GT")
    NEGB = S.sb([128, 2], F32, "NEGB")
    CMf = S.sb([128, 13 * 128], F32, "CMf")
    CMb = S.sb([128, 13 * 128], BF, "CMb")
    MDb = S.sb([128, 4 * 512], BF, "MDb")
    ONESF = S.sb([128, 512], F32, "ONESF")
    WGG = S.sb([16, 2, 128], BF, "WGG")
    def cmf(i):
        return CMf.t[:, 128 * i:128 * (i + 1)]

    def cmb(i):
        return CMb.t[:, 128 * i:128 * (i + 1)]

    XT = S.sb([128, 8, 512], F32, "XT")
    HT = S.sb([128, 8, 512], BF, "HT")
    WPn = 3
    WP = [S.sb([128, 4224], BF, "WP%d" % i) for i in range(WPn)]
    wpi = [0]
    f2 = Pool(S, [128, 512], F32, 5, "f2_")
    b1 = Pool(S, [128, 512], BF, 5, "b1_")
    XIN = Pool(S, [128, 1024], F32, 2, "xin")
    QA = S.sb([64, 4, 512], BF, "QA")
    KA = S.sb([64, 4, 512], BF, "KA")
    VBF = S.sb([128, 4, 256], BF, "VBF")
    KBK = Pool(S, [64, 4, 512], BF, 2, "kbk")
    VBK = Pool(S, [128, 4, 256], BF, 2, "vbk")
    OSBT = S.sb([128, 1024], F32, "OSBT")
    OSB = Tl(OSBT.t[:, :].rearrange("p (a b) -> p a b", a=4), OSBT.b)
    OBT = Tl(OSBT.t[:, :].rearrange("p (a b) -> p a b", a=2), OSBT.b)
    NEGC = S.sb([128, 16], F32, "NEGC")
    NEGCB = [Buf("negc%d" % i) for i in range(16)]
    import os as _os2
    SBW = int(_os2.environ.get("KSBW", "4"))
    MLW = int(_os2.environ.get("KMLW", "4"))
    SBG = int(_os2.environ.get("KSBG", "2"))
    LQ = _os2.environ.get("KLQ", "pool")
    FT = Pool(S, [128, 513], F32, 4, "ft")
    QG = S.sb([128, 512], F32, "QG")
    KG = S.sb([128, 512], F32, "KG")
    AG = S.sb([16, 512], BF, "AG")
    SPG = S.sb([128, 512], F32, "SPG")
    CS = S.sb([128, 512], F32, "CS")
    EBt = S.sb([128, 512], F32, "EB")
    NBL = S.sb([128, 8], F32, "NBL")
    QTg = S.sb([128, 512], BF, "QTg")
    KTg = S.sb([128, 512], BF, "KTg")
    KDg = S.sb([128, 512], F32, "KDg")
    KDt = Pool(S, [128, 128], BF, 2, "kdt")
    VG = S.sb([128, 4, 256], BF, "VG")
    SALL = [S.sb([128, 256], F32, "SALL%d" % i) for i in range(2)]
    SS = S.sb([128, 256], F32, "SS")
    SBF16 = S.sb([128, 256], BF, "SBF16")
    ATM = Pool(S, [128, 128], BF, 3, "atm")
    SRG = S.sb([128, 2, 512], BF, "SRG")
    CQ = OBT
    LAT = S.sb([128, 512], F32, "LAT")
    LATB = S.sb([128, 512], BF, "LATB")
    KRN = S.sb([96, 512], F32, "KRN")
    ROT = Pool(S, [96, 512], F32, 2, "rot")
    KRB = S.sb([96, 512], BF, "KRB")
    QN = Pool(S, [96, 512], F32, 1, "qn")
    QTm = S.sb([96, 8, 512], BF, "QTm")
    KTn = Pool(S, [96, 512], BF, 2, "ktn")
    VAUG = S.sb([128, 4, 8 * 65], BF, "VAUG")
    COS = S.sb([96, 512], F32, "COS")
    SIN = S.sb([96, 512], F32, "SIN")
    MKB = Pool(S, [96, 512], BF, 2, "mkb")
    MVB = Pool(S, [128, 4, 65], BF, 2, "mvb")
    MEMK = Pool(S, [128, 4, 256], BF, 1, "memk")
    MEMV = Pool(S, [128, 2, 512], BF, 1, "memv")
    AT = S.sb([128, 22, 512], BF, "AT")
    SMALL = Pool(S, [128, 16], F32, 6, "small")
    OUTT = XIN

    class _V:
        def __init__(self, ap):
            self.t = ap
            self.b = AT.b
    OCR = Tl(AT.t[:, 0:4, :], AT.b)
    QCT = Tl(AT.t[:, 4:8, :], AT.b)
    OCT = Tl(AT.t[0:64, 8:16, :], AT.b)
    OAT = Tl(AT.t[:, 16:18, :], AT.b)
    OBB = Tl(AT.t[:, 18:20, :], AT.b)
    CQN = Tl(AT.t[:, 20:22, :], AT.b)

    def mm(out, lhsT, rhs, start, stop, r, w):
        S.op("pe", lambda e: e.matmul(out, lhsT=lhsT, rhs=rhs, start=start, stop=stop), r=r, w=w)

    def tr(out, in_, ident, r, w):
        S.op("pe", lambda e: e.transpose(out, in_, ident), r=r, w=w)

    def act(out, in_, func, r, w, bias=None, scale=None):
        kw = {}
        if bias is not None:
            kw["bias"] = bias
        if scale is not None:
            kw["scale"] = scale
        S.op("act", lambda e: e.activation(out, in_, func, **kw), r=r, w=w)

    def tt(out, in0, in1, op, r, w):
        S.op("dve", lambda e: e.tensor_tensor(out, in0, in1, op), r=r, w=w)

    def ts(out, in0, s1, s2, op0, op1, r, w):
        if op1 is None:
            S.op("dve", lambda e: e.tensor_scalar(out, in0, s1, None, op0), r=r, w=w)
        else:
            S.op("dve", lambda e: e.tensor_scalar(out, in0, s1, s2, op0, op1), r=r, w=w)

    def stt(out, in0, sc, in1, op0, op1, r, w):
        S.op("dve", lambda e: e.scalar_tensor_tensor(out, in0, sc, in1, op0, op1), r=r, w=w)

    def cp(out, in_, r, w, eng="dve"):
        if eng == "dve":
            S.op("dve", lambda e: e.tensor_copy(out, in_), r=r, w=w)
        else:
            S.op("act", lambda e: e.activation(out, in_, AF.Copy), r=r, w=w)

    def dma(q, out, in_, r, w):
        S.op(q, lambda e: e.dma_start(out=out, in_=in_), r=r, w=w)

    def memset(ap, val, w):
        S.op("dve", lambda e: e.memset(ap, val), r=(), w=w)

    def rstd_from(ps_ap, pb, P0, P1, N, eps=EPS):
        t1 = f2.get()
        act(t1.t[P0:P1, 0:N], ps_ap, AF.Ln, r=[pb], w=[t1], bias=EPSC[P0:P1, 0:1] if eps else None)
        t2 = f2.get()
        act(t2.t[P0:P1, 0:N], t1.t[P0:P1, 0:N], AF.Exp, r=[t1], w=[t2], scale=-0.5)
        return t2

    dma("sp", GT.t[:], gtab[:, :], r=[], w=[GT])
    dma("sp", CMf.t[:], cmat[:, :], r=[], w=[CMf])
    dma("pool", MDb.t[:], mdiag[:, :], r=[], w=[MDb])
    dma("pool", CMb.t[:], cmat[:, :], r=[], w=[CMb])
    memset(ONESF.t[:], 1.0, w=[ONESF])
    EPSt = S.sb([128, 1], F32, "EPSC")
    EPSC = EPSt.t
    memset(EPSC[:], EPS, w=[EPSt])
    for l in range(2):
        ts(NEGB.t[:, l:l + 1], GT.t[:, GL * l + 42:GL * l + 43], -1.0, None, ALU.mult, None, r=[GT], w=[NEGB])
        dma("pool", WGG.t[:, l, :], w_gg[l], r=[], w=[WGG])
    for t in FT.tiles:
        memset(t.t[:, 0:1], 0.0, w=[t])
    memset(VAUG.t[:], 1.0, w=[VAUG])
    IDF = cmf(0)
    ONES1024, ONES256, ONES128, BD64, BD96, W65 = cmb(1), cmb(2), cmb(3), cmb(4), cmb(5), cmb(6)
    MASKL01, MASKLNEG, MASKU01 = cmf(7), cmf(8), cmf(9)
    CONST = [CMf, CMb, EPSt]

    def gcol(l, c, P0=0, P1=128):
        return GT.t[P0:P1, GL * l + c:GL * l + c + 1]

    def wload(src_ap, nelem):
        wpi[0] += 1
        t = WP[wpi[0] % WPn]
        return t

    def wpiece(w2d, r0, nkc, c0, ncol, p=128):
        wpi[0] += 1
        t = WP[wpi[0] % WPn]
        view = t.t[0:p, 0:nkc * ncol].rearrange("p (a b) -> p a b", a=nkc)
        src = w2d[r0:r0 + nkc * p, c0:c0 + ncol].rearrange("(kc p) n -> p kc n", p=p)
        dma("pool", view, src, r=[], w=[t])
        return t, view

    def norm_x(l, gbase, NT):
        ps = psr()
        for c in range(8):
            sq = b1.get()
            act(sq.t[:, 0:NT], XT.t[:, c, 0:NT], AF.Square, r=[XT], w=[sq])
            mm(ps.t[:, 0:NT], ONES1024, sq.t[:, 0:NT], c == 0, c == 7, r=[sq, CMb], w=[ps])
        rs = rstd_from(ps.t[:, 0:NT], ps, 0, 128, NT)
        for c in range(8):
            stt(HT.t[:, c, 0:NT], XT.t[:, c, 0:NT], gcol(l, gbase + c), rs.t[:, 0:NT], ALU.mult, ALU.mult,
                r=[XT, rs, GT], w=[HT])

    def proj_fm(ps, pr0, M, wt, wv, col0, NT, KC=8, rhs=None, rb=None):
        rhs = rhs if rhs is not None else HT
        for kc in range(KC):
            mm(ps.t[pr0:pr0 + M, 0:NT], wv[:, kc, col0:col0 + M], rhs.t[:, kc, 0:NT], kc == 0, kc == KC - 1,
               r=[wt, rhs], w=[ps])

    def mla_kside(l, seq, k0, NT, latb, krb, wukv_t, wukv_v):
        kb = k0 // 512

        def head(h):
            ps = psr()
            mm(ps.t[0:64, 0:NT], wukv_v[:, 0, 128 * h:128 * h + 64], latb.t[:, 0:NT], True, True,
               r=[wukv_t, latb], w=[ps])
            yield
            sq = b1.get()
            act(sq.t[0:64, 0:NT], ps.t[0:64, 0:NT], AF.Square, r=[ps], w=[sq])
            yield
            ps2 = psr()
            mm(ps2.t[0:64, 0:NT], BD64[0:64, 0:64], sq.t[0:64, 0:NT], True, True, r=[sq, CMb], w=[ps2])
            yield
            t1 = f2.get()
            act(t1.t[0:64, 0:NT], ps2.t[0:64, 0:NT], AF.Ln, r=[ps2], w=[t1], bias=EPSC[0:64, 0:1])
            yield
            act(t1.t[0:64, 0:NT], t1.t[0:64, 0:NT], AF.Exp, r=[t1], w=[t1], scale=-0.5)
            yield
            kt = KTn.get()
            stt(kt.t[0:64, 0:NT], ps.t[0:64, 0:NT], gcol(l, 36, 0, 64), t1.t[0:64, 0:NT], ALU.mult, ALU.mult,
                r=[ps, t1, GT], w=[kt])
            cp(kt.t[64:96, 0:NT], krb.t[64:96, 0:NT], r=[krb], w=[kt])
            yield
            dma("sp", c_mk[l, seq, :, h, k0:k0 + NT], kt.t[0:96, 0:NT], r=[kt], w=[CB(l, seq, kb)])

        run_interleaved((head(h) for h in range(8)), 2, 3)
        ntt = (NT + 127) // 128
        for t in range(ntt):
            w_ = min(128, NT - 128 * t)
            ps = psr()
            vcols = wukv_v[:, 0, :].rearrange("p (h c) -> p h c", h=8)[:, :, 64:128]
            mm(ps.t[0:w_, 0:512].rearrange("p (h c) -> p h c", h=8), latb.t[:, 128 * t:128 * t + w_], vcols, True, True,
               r=[wukv_t, latb], w=[ps])
            cp(VAUG.t[0:w_, t, :].rearrange("p (h c) -> p h c", h=8)[:, :, 0:64],
               ps.t[0:w_, 0:512].rearrange("p (h c) -> p h c", h=8), r=[ps], w=[VAUG], eng="act")
        for t in range(ntt):
            w_ = min(128, NT - 128 * t)
            dma("sp", c_mv[l, seq, :, k0 + 128 * t:k0 + 128 * t + w_, :].rearrange("h p c -> p h c"),
                VAUG.t[0:w_, t, :].rearrange("p (h c) -> p h c", h=8), r=[VAUG], w=[CB(l, seq, kb)])

    def load_wukv(l):
        return wpiece(w_ukv[l], 0, 1, 0, 1024)

    def prep_mem_prompt():
        for t in range(2):
            xin = XIN.get()
            dma("sp", xin.t[:], memp[128 * t:128 * t + 128, :], r=[], w=[xin])
            for half in range(2):
                ps = psr()
                for c4 in range(4):
                    c = 4 * half + c4
                    tr(ps.t[:, 128 * c4:128 * c4 + 128], xin.t[:, 128 * c:128 * c + 128], IDF, r=[xin, CMf], w=[ps])
                cp(XT.t[:, 4 * half:4 * half + 4, 128 * t:128 * t + 128],
                   ps.t[:, :].rearrange("p (a b) -> p a b", a=4), r=[ps], w=[XT])
        import os as _os
        _sub = int(_os.environ.get("KSUB", "9"))
        if _sub < 1:
            return
        for l in range(2):
            norm_x(l, 24, 256)
            if _sub < 2:
                continue
            wt, wv = wpiece(w_ck[l], 0, 8, 0, 512)
            for h in range(4 if _sub in (3, 4, 9) else 0):
                ps = psr()
                proj_fm(ps, 0, 128, wt, wv, 128 * h, 256)
                sq = b1.get()
                act(sq.t[:, 0:256], ps.t[:, 0:256], AF.Square, r=[ps], w=[sq])
                ps2 = psr()
                mm(ps2.t[:, 0:256], ONES128, sq.t[:, 0:256], True, True, r=[sq, CMb], w=[ps2])
                rs = rstd_from(ps2.t[:, 0:256], ps2, 0, 128, 256)
                kf = f2.get()
                stt(kf.t[:, 0:256], ps.t[:, 0:256], gcol(l, 41), rs.t[:, 0:256], ALU.mult, ALU.mult,
                    r=[ps, rs, GT], w=[kf])
                kb_ = b1.get()
                cp(kb_.t[:, 0:256], kf.t[:, 0:256], r=[kf], w=[kb_])
                dma("sp", c_memk[l, 0, :, h, :], kb_.t[:, 0:256], r=[kb_], w=[membuf[(l, 0)]])
                ps3 = psr()
                for t in range(2):
                    tr(ps3.t[:, 128 * t:128 * t + 128], kf.t[:, 128 * t:128 * t + 128], IDF, r=[kf, CMf], w=[ps3])
                ko = f2.get()
                cp(ko.t[:, 0:256], ps3.t[:, 0:256], r=[ps3], w=[ko], eng="act")
                dma("sp", o_mk_p[l, :, 128 * h:128 * h + 128].rearrange("(t p) c -> p t c", p=128),
                    ko.t[:, 0:256].rearrange("p (t c) -> p t c", t=2), r=[ko], w=[])
            wt, wv = wpiece(w_cv[l], 0, 8, 0, 512)
            for t in range(2 if _sub >= 4 else 0):
                ps = psr()
                for kc in range(8):
                    mm(ps.t[:, 0:512], HT.t[:, kc, 128 * t:128 * t + 128], wv[:, kc, :], kc == 0, kc == 7,
                       r=[HT, wt], w=[ps])
                vo = f2.get()
                cp(vo.t[:, :], ps.t[:, :], r=[ps], w=[vo])
                dma("sp", o_mv_p[l, 128 * t:128 * t + 128, :], vo.t[:, :], r=[vo], w=[])
                vb = b1.get()
                cp(vb.t[:, :], vo.t[:, :], r=[vo], w=[vb], eng="act")
                dma("sp", c_memv[l, 0, 128 * t:128 * t + 128, :], vb.t[:, :], r=[vb], w=[membuf[(l, 0)]])

    def prep_sample_caches():
        for l in range(2):
            wukv_t, wukv_v = load_wukv(l)
            for j in range(NSS):
                seq = 1 + j
                dma("pool", c_sbv[l, seq, 0:PAST, :], csv[l, j], r=[], w=[CB(l, seq, kb) for kb in range(PAST // 512)])
                dma("pool", c_memv[l, seq], cmv[l, j], r=[], w=[membuf[(l, seq)]])
                for t in range(2):
                    xin = XIN.get()
                    dma("sp", xin.t[:, 0:512], cmk[l, j, 128 * t:128 * t + 128, :], r=[], w=[xin])
                    ps = psr()
                    for h in range(4):
                        tr(ps.t[:, 128 * h:128 * h + 128], xin.t[:, 128 * h:128 * h + 128], IDF, r=[xin, CMf], w=[ps])
                    kb_ = b1.get()
                    cp(kb_.t[:, :], ps.t[:, :], r=[ps], w=[kb_])
                    dma("sp", c_memk[l, seq, :, :, 128 * t:128 * t + 128],
                        kb_.t[:, :].rearrange("p (h k) -> p h k", h=4), r=[kb_], w=[membuf[(l, seq)]])
                for kb in range(PAST // 512):
                    k0 = 512 * kb
                    xin = XIN.get()
                    dma("sp", xin.t[:, :].rearrange("p (t c) -> p t c", t=4),
                        csk[l, j, k0:k0 + 512, :].rearrange("(t p) c -> p t c", p=128), r=[], w=[xin])
                    for hp in range(2):
                        ps = psr()
                        for t in range(4):
                            tr(ps.t[:, 128 * t:128 * t + 128], xin.t[:, 256 * t + 128 * hp:256 * t + 128 * hp + 128],
                               IDF, r=[xin, CMf], w=[ps])
                        kb_ = b1.get()
                        cp(kb_.t[:, :], ps.t[:, :], r=[ps], w=[kb_], eng=("act" if hp else "dve"))
                        for hh in range(2):
                            dma("sp", c_sbk[l, seq, :, 2 * hp + hh, k0:k0 + 512], kb_.t[64 * hh:64 * hh + 64, :],
                                r=[kb_], w=[CB(l, seq, kb)])
                    xin = XIN.get()
                    dma("sp", xin.t[:, 0:512].rearrange("p (t c) -> p t c", t=4),
                        clat[l, j, k0:k0 + 512, :].rearrange("(t p) c -> p t c", p=128), r=[], w=[xin])
                    dma("sp", xin.t[:, 512:640].rearrange("p (t c) -> p t c", t=4),
                        ckr[l, j, k0:k0 + 512, :].rearrange("(t p) c -> p t c", p=128), r=[], w=[xin])
                    ps = psr()
                    for t in range(4):
                        tr(ps.t[:, 128 * t:128 * t + 128], xin.t[:, 128 * t:128 * t + 128], IDF, r=[xin, CMf], w=[ps])
                    cp(LATB.t[:, :], ps.t[:, :], r=[ps], w=[LATB])
                    ps = psr()
                    for t in range(4):
                        tr(ps.t[0:32, 128 * t:128 * t + 128], xin.t[:, 512 + 32 * t:512 + 32 * t + 32], IDF,
                           r=[xin, CMf], w=[ps])
                    kr32 = b1.get()
                    cp(kr32.t[0:32, :], ps.t[0:32, :], r=[ps], w=[kr32], eng="act")
                    dma("sp", KRB.t[64:96, :], kr32.t[0:32, :], r=[kr32], w=[KRB])
                    mla_kside(l, seq, k0, 512, LATB, KRB, wukv_t, wukv_v)

    MARKS = []

    def mark(lbl):
        MARKS.append((lbl, S.eng["pe"]["n"], S.eng["act"]["n"], S.eng["dve"]["n"], S.eng["sp"]["n"]))
    nc._marks = MARKS

    def group(kind, g):
        mark("group_%s%d" % (kind, g))
        if kind == "p":
            NT, NTT = 512, 4
            xsrc, ydst = xp, yp
            t0 = 512 * g
            pos0 = t0
            segs = [(0, 0, 512)]
        else:
            NT, NTT = NTS, NTS // 128
            xsrc, ydst = xs, ys
            t0 = 0
            pos0 = PAST
        for t in range(NTT):
            xin = XIN.get()
            dma("sp", xin.t[:], xsrc[t0 + 128 * t:t0 + 128 * t + 128, :], r=[], w=[xin])
            for half in range(2):
                ps = psr()
                for c4 in range(4):
                    c = 4 * half + c4
                    tr(ps.t[:, 128 * c4:128 * c4 + 128], xin.t[:, 128 * c:128 * c + 128], IDF, r=[xin, CMf], w=[ps])
                cp(XT.t[:, 4 * half:4 * half + 4, 128 * t:128 * t + 128],
                   ps.t[:, :].rearrange("p (a b) -> p a b", a=4), r=[ps], w=[XT], eng=("act" if half else "dve"))
        if kind == "p":
            dma("sp", COS.t[:, 0:NT], ropec[:, pos0:pos0 + NT], r=[], w=[COS])
            dma("sp", SIN.t[:, 0:NT], ropes[:, pos0:pos0 + NT], r=[], w=[SIN])
        else:
            for j in range(NSS):
                dma("sp", COS.t[:, 64 * j:64 * j + 64], ropec[:, PAST:PAST + 64], r=[], w=[COS])
                dma("sp", SIN.t[:, 64 * j:64 * j + 64], ropes[:, PAST:PAST + 64], r=[], w=[SIN])

        for l in range(2):
            layer(kind, g, l, NT, NTT, t0)

        for t in range(NTT):
            ot = OUTT.get()
            for half in range(2):
                ps = psr()
                for c4 in range(4):
                    c = 4 * half + c4
                    tr(ps.t[:, 128 * c4:128 * c4 + 128], XT.t[:, c, 128 * t:128 * t + 128], IDF, r=[XT, CMf], w=[ps])
                cp(ot.t[:, 512 * half:512 * half + 512], ps.t[:, :], r=[ps], w=[ot], eng=("act" if half else "dve"))
            dma("sp", ydst[t0 + 128 * t:t0 + 128 * t + 128, :], ot.t[:], r=[ot], w=[])

    def layer(kind, g, l, NT, NTT, t0):
        isp = kind == "p"
        if isp:
            o_sbk, o_sbv, o_lat, o_kr = o_sbk_p, o_sbv_p, o_lat_p, o_kr_p
            k0new = 512 * g
            seqs = [0]
        else:
            o_sbk, o_sbv, o_lat, o_kr = o_sbk_s, o_sbv_s, o_lat_s, o_kr_s
            k0new = PAST
            seqs = list(range(1, 1 + NSS))
        mark("%s%d_l%d_A" % (kind, g, l))
        norm_x(l, 0, NT)
        wt1, wv1 = wpiece(w_in[l], 0, 8, 0, 512)
        for h in range(4):
            ps = psr()
            proj_fm(ps, 0, 64, wt1, wv1, 64 * h, NT)
            cp(QA.t[:, h, 0:NT], ps.t[0:64, 0:NT], r=[ps], w=[QA], eng=("act" if h % 2 else "dve"))
        for h in range(4):
            ps = psr()
            proj_fm(ps, 0, 64, wt1, wv1, 256 + 64 * h, NT)
            cp(KA.t[:, h, 0:NT], ps.t[0:64, 0:NT], r=[ps], w=[KA], eng=("act" if h % 2 else "dve"))
        wt2, wv2 = wpiece(w_in[l], 0, 8, 512, 512)
        for t in range(NTT):
            psk = psr()
            for kc in range(8):
                mm(psk.t[:, 0:256], HT.t[:, kc, 128 * t:128 * t + 128], wv1[:, kc, 256:512], kc == 0, kc == 7,
                   r=[HT, wt1], w=[psk])
            psv = psr()
            for kc in range(8):
                mm(psv.t[:, 0:256], HT.t[:, kc, 128 * t:128 * t + 128], wv2[:, kc, 0:256], kc == 0, kc == 7,
                   r=[HT, wt2], w=[psv])
            kv = f2.get()
            cp(kv.t[:, 0:256], psk.t[:, 0:256], r=[psk], w=[kv])
            cp(kv.t[:, 256:512], psv.t[:, 0:256], r=[psv], w=[kv], eng="act")
            cp(VBF.t[:, t, :], kv.t[:, 256:512], r=[kv], w=[VBF])
            dma("sp", o_sbk[l, t0 + 128 * t:t0 + 128 * t + 128, :], kv.t[:, 0:256], r=[kv], w=[])
            dma("sp", o_sbv[l, t0 + 128 * t:t0 + 128 * t + 128, :], kv.t[:, 256:512], r=[kv], w=[])
        if isp:
            dma("sp", c_sbk[l, 0, :, :, k0new:k0new + 512], KA.t[:, :, :], r=[KA], w=[CB(l, 0, g)])
            dma("sp", c_sbv[l, 0, k0new:k0new + 512, :].rearrange("(t p) c -> p t c", p=128), VBF.t[:, :, :],
                r=[VBF], w=[CB(l, 0, g)])
        else:
            for j in range(NSS):
                dma("sp", c_sbk[l, 1 + j, :, :, PAST:PAST + 64], KA.t[:, :, 64 * j:64 * j + 64], r=[KA],
                    w=[CB(l, 1 + j, PAST // 512)])
                tt_, po = (64 * j) // 128, (64 * j) % 128
                dma("sp", c_sbv[l, 1 + j, PAST:PAST + 64, :], VBF.t[po:po + 64, tt_, :], r=[VBF],
                    w=[CB(l, 1 + j, PAST // 512)])
        ps = psr()
        proj_fm(ps, 0, 128, wt2, wv2, 256, NT)
        cp(QG.t[:, 0:NT], ps.t[:, 0:NT], r=[ps], w=[QG])
        ps = psr()
        proj_fm(ps, 0, 128, wt2, wv2, 384, NT)
        cp(KG.t[:, 0:NT], ps.t[:, 0:NT], r=[ps], w=[KG], eng="act")
        wt3, wv3 = wpiece(w_in[l], 0, 8, 1024, 528)
        for t in range(NTT):
            ps = psr()
            for kc in range(8):
                mm(ps.t[:, 0:256], HT.t[:, kc, 128 * t:128 * t + 128], wv3[:, kc, 0:256], kc == 0, kc == 7,
                   r=[HT, wt3], w=[ps])
            cp(VG.t[:, t, :], ps.t[:, 0:256], r=[ps], w=[VG], eng=("act" if t % 2 else "dve"))
        ps = psr()
        proj_fm(ps, 0, 16, wt3, wv3, 256, NT)
        cp(AG.t[:, 0:NT], ps.t[0:16, 0:NT], r=[ps], w=[AG])
        for c in range(2):
            ps = psr()
            proj_fm(ps, 0, 128, wt3, wv3, 272 + 128 * c, NT)
            act(SRG.t[:, c, 0:NT], ps.t[:, 0:NT], AF.Silu, r=[ps], w=[SRG])
        wt4, wv4 = wpiece(w_in[l], 0, 8, 1552, 416)
        psq = psr()
        for c in range(2):
            ps = psr()
            proj_fm(ps, 0, 128, wt4, wv4, 128 * c, NT)
            cp(CQ.t[:, c, 0:NT], ps.t[:, 0:NT], r=[ps], w=[CQ])
            sq = b1.get()
            act(sq.t[:, 0:NT], CQ.t[:, c, 0:NT], AF.Square, r=[CQ], w=[sq])
            mm(psq.t[:, 0:NT], ONES256, sq.t[:, 0:NT], c == 0, c == 1, r=[sq, CMb], w=[psq])
        rs = rstd_from(psq.t[:, 0:NT], psq, 0, 128, NT)
        for c in range(2):
            stt(CQN.t[:, c, 0:NT], CQ.t[:, c, 0:NT], gcol(l, 32 + c), rs.t[:, 0:NT], ALU.mult, ALU.mult,
                r=[CQ, rs, GT], w=[CQN])
        ps = psr()
        proj_fm(ps, 0, 128, wt4, wv4, 256, NT)
        sq = b1.get()
        act(sq.t[:, 0:NT], ps.t[:, 0:NT], AF.Square, r=[ps], w=[sq])
        ps2 = psr()
        mm(ps2.t[:, 0:NT], ONES128, sq.t[:, 0:NT], True, True, r=[sq, CMb], w=[ps2])
        rs = rstd_from(ps2.t[:, 0:NT], ps2, 0, 128, NT)
        stt(LAT.t[:, 0:NT], ps.t[:, 0:NT], gcol(l, 34), rs.t[:, 0:NT], ALU.mult, ALU.mult, r=[ps, rs, GT], w=[LAT])
        cp(LATB.t[:, 0:NT], LAT.t[:, 0:NT], r=[LAT], w=[LATB], eng="act")
        ps3 = psr()
        for t in range(NTT):
            tr(ps3.t[:, 128 * t:128 * t + 128], LAT.t[:, 128 * t:128 * t + 128], IDF, r=[LAT, CMf], w=[ps3])
        lo = f2.get()
        cp(lo.t[:, 0:NT], ps3.t[:, 0:NT], r=[ps3], w=[lo])
        dma("sp", o_lat[l, t0:t0 + NT, :].rearrange("(t p) c -> p t c", p=128),
            lo.t[:, 0:NT].rearrange("p (t c) -> p t c", t=NTT), r=[lo], w=[])
        ps = psr()
        proj_fm(ps, 64, 32, wt4, wv4, 384, NT)
        sq = b1.get()
        act(sq.t[64:96, 0:NT], ps.t[64:96, 0:NT], AF.Square, r=[ps], w=[sq])
        ps2 = psr()
        mm(ps2.t[64:96, 0:NT], BD96[64:96, 64:96], sq.t[64:96, 0:NT], True, True, r=[sq, CMb], w=[ps2])
        rs = rstd_from(ps2.t[64:96, 0:NT], ps2, 64, 96, NT)
        stt(KRN.t[64:96, 0:NT], ps.t[64:96, 0:NT], gcol(l, 36, 64, 96), rs.t[64:96, 0:NT], ALU.mult, ALU.mult,
            r=[ps, rs, GT], w=[KRN])
        rot = ROT.get()
        dma("sp", rot.t[64:80, 0:NT], KRN.t[80:96, 0:NT], r=[KRN], w=[rot])
        dma("sp", rot.t[80:96, 0:NT], KRN.t[64:80, 0:NT], r=[KRN], w=[rot])
        t1 = f2.get()
        tt(t1.t[64:96, 0:NT], KRN.t[64:96, 0:NT], COS.t[64:96, 0:NT], ALU.mult, r=[KRN, COS], w=[t1])
        t2 = f2.get()
        tt(t2.t[64:96, 0:NT], rot.t[64:96, 0:NT], SIN.t[64:96, 0:NT], ALU.mult, r=[rot, SIN], w=[t2])
        tt(KRN.t[64:96, 0:NT], t1.t[64:96, 0:NT], t2.t[64:96, 0:NT], ALU.add, r=[t1, t2], w=[KRN])
        cp(KRB.t[64:96, 0:NT], KRN.t[64:96, 0:NT], r=[KRN], w=[KRB], eng="act")
        kr0 = f2.get()
        dma("sp", kr0.t[0:32, 0:NT], KRN.t[64:96, 0:NT], r=[KRN], w=[kr0])
        ps3 = psr()
        for t in range(NTT):
            tr(ps3.t[:, 32 * t:32 * t + 32], kr0.t[0:32, 128 * t:128 * t + 128], IDF[0:32, 0:32], r=[kr0, CMf], w=[ps3])
        ko = f2.get()
        cp(ko.t[:, 0:32 * NTT], ps3.t[:, 0:32 * NTT], r=[ps3], w=[ko])
        dma("sp", o_kr[l, t0:t0 + NT, :].rearrange("(t p) c -> p t c", p=128),
            ko.t[:, 0:32 * NTT].rearrange("p (t c) -> p t c", t=NTT), r=[ko], w=[])
        mark("%s%d_l%d_mlaq" % (kind, g, l))
        wtq, wvq = wpiece(w_uq[l], 0, 2, 0, 768)
        for h in range(8):
            ps = psr()
            proj_fm(ps, 0, 96, wtq, wvq, 96 * h, NT, KC=2, rhs=CQN)
            sq = b1.get()
            act(sq.t[0:96, 0:NT], ps.t[0:96, 0:NT], AF.Square, r=[ps], w=[sq])
            ps2 = psr()
            mm(ps2.t[0:96, 0:NT], BD96[0:96, 0:96], sq.t[0:96, 0:NT], True, True, r=[sq, CMb], w=[ps2])
            rs = rstd_from(ps2.t[0:96, 0:NT], ps2, 0, 96, NT)
            stt(QTm.t[0:64, h, 0:NT], ps.t[0:64, 0:NT], gcol(l, 35, 0, 64), rs.t[0:64, 0:NT], ALU.mult, ALU.mult,
                r=[ps, rs, GT], w=[QTm])
            qn = QN.get()
            stt(qn.t[64:96, 0:NT], ps.t[64:96, 0:NT], gcol(l, 35, 64, 96), rs.t[64:96, 0:NT], ALU.mult, ALU.mult,
                r=[ps, rs, GT], w=[qn])
            rot = ROT.get()
            dma("sp", rot.t[64:80, 0:NT], qn.t[80:96, 0:NT], r=[qn], w=[rot])
            dma("sp", rot.t[80:96, 0:NT], qn.t[64:80, 0:NT], r=[qn], w=[rot])
            t1 = f2.get()
            tt(t1.t[64:96, 0:NT], qn.t[64:96, 0:NT], COS.t[64:96, 0:NT], ALU.mult, r=[qn, COS], w=[t1])
            t2 = f2.get()
            tt(t2.t[64:96, 0:NT], rot.t[64:96, 0:NT], SIN.t[64:96, 0:NT], ALU.mult, r=[rot, SIN], w=[t2])
            tt(QTm.t[64:96, h, 0:NT], t1.t[64:96, 0:NT], t2.t[64:96, 0:NT], ALU.add, r=[t1, t2], w=[QTm])
        wukv_t, wukv_v = load_wukv(l)
        if isp:
            mla_kside(l, 0, k0new, 512, LATB, KRB, wukv_t, wukv_v)
        else:
            for j in range(NSS):
                mla_kside_cols(l, 1 + j, PAST, 64 * j, 64, wukv_t, wukv_v)
        import os as _os
        _cut = int(_os.environ.get("KCUT", "99"))
        if _cut <= 1:
            return
        mark("%s%d_l%d_SB" % (kind, g, l))
        sb_attention(kind, g, l, NT)
        if _cut <= 2:
            return
        mark("%s%d_l%d_GLA" % (kind, g, l))
        gla(kind, g, l, NT, NTT)
        if _cut <= 3:
            return
        mark("%s%d_l%d_MLA" % (kind, g, l))
        mla_attention(kind, g, l, NT)
        mark("%s%d_l%d_Wout" % (kind, g, l))
        if _cut <= 4:
            return
        for half in range(2):
            wta, wva = wpiece(w_out[l], 0, 4, 512 * half, 512)
            wtm, wvm = wpiece(w_out[l], 512, 8, 512 * half, 512, p=64)
            for o4 in range(4):
                oc = 4 * half + o4
                ps = psr()
                cols = slice(128 * o4, 128 * o4 + 128)
                for c in range(2):
                    mm(ps.t[:, 0:NT], wva[:, c, cols], OAT.t[:, c, 0:NT], c == 0, False, r=[wta, OAT], w=[ps])
                for c in range(2):
                    mm(ps.t[:, 0:NT], wva[:, 2 + c, cols], OBB.t[:, c, 0:NT], False, False, r=[wta, OBB], w=[ps])
                for h in range(8):
                    mm(ps.t[:, 0:NT], wvm[:, h, cols], OCT.t[:, h, 0:NT], False, h == 7, r=[wtm, OCT], w=[ps])
                tt(XT.t[:, oc, 0:NT], ps.t[:, 0:NT], XT.t[:, oc, 0:NT], ALU.add, r=[ps, XT], w=[XT])
        mark("%s%d_l%d_cross" % (kind, g, l))
        norm_x(l, 8, NT)
        wt, wv = wpiece(w_cq[l], 0, 8, 0, 512)
        for h in range(4):
            ps = psr()
            proj_fm(ps, 0, 128, wt, wv, 128 * h, NT)
            sq = b1.get()
            act(sq.t[:, 0:NT], ps.t[:, 0:NT], AF.Square, r=[ps], w=[sq])
            ps2 = psr()
            mm(ps2.t[:, 0:NT], ONES128, sq.t[:, 0:NT], True, True, r=[sq, CMb], w=[ps2])
            rs = rstd_from(ps2.t[:, 0:NT], ps2, 0, 128, NT)
            stt(QCT.t[:, h, 0:NT], ps.t[:, 0:NT], gcol(l, 40), rs.t[:, 0:NT], ALU.mult, ALU.mult,
                r=[ps, rs, GT], w=[QCT])
        csegs = [(0, 0, NT)] if isp else [(1 + j, 64 * j, 64) for j in range(NSS)]
        for (seq, c0, ncl) in csegs:
            mk = MEMK.get()
            mv = MEMV.get()
            dma("sp", mk.t[:], c_memk[l, seq], r=[membuf[(l, seq)]], w=[mk])
            dma("sp", mv.t[:], c_memv[l, seq].rearrange("(t p) c -> p t c", p=128), r=[membuf[(l, seq)]], w=[mv])
            for h in range(4):
                po = psl()
                pd = psr()
                for t in range(2):
                    ps = psr()
                    mm(ps.t[:, 0:ncl], mk.t[:, h, 128 * t:128 * t + 128], QCT.t[:, h, c0:c0 + ncl], True, True,
                       r=[mk, QCT], w=[ps])
                    pt = b1.get()
                    act(pt.t[:, 0:ncl], ps.t[:, 0:ncl], AF.Exp, r=[ps], w=[pt], scale=128.0 ** -0.5)
                    mm(po.t[:, 0:ncl], mv.t[:, t, 128 * h:128 * h + 128], pt.t[:, 0:ncl], t == 0, t == 1,
                       r=[mv, pt], w=[po])
                    mm(pd.t[:, 0:ncl], cmb(10), pt.t[:, 0:ncl], t == 0, t == 1, r=[CMb, pt], w=[pd])
                rd = f2.get()
                S.op("dve", lambda e, o=rd.t[:, 0:ncl], i=pd.t[:, 0:ncl]: e.reciprocal(o, i), r=[pd], w=[rd])
                tt(OCR.t[:, h, c0:c0 + ncl], po.t[:, 0:ncl], rd.t[:, 0:ncl], ALU.mult, r=[po, rd], w=[OCR])
        wt, wv = wpiece(w_co[l], 0, 4, 0, 1024)
        for oc in range(8):
            ps = psr()
            for h in range(4):
                mm(ps.t[:, 0:NT], wv[:, h, 128 * oc:128 * oc + 128], OCR.t[:, h, 0:NT], h == 0, h == 3,
                   r=[wt, OCR], w=[ps])
            tt(XT.t[:, oc, 0:NT], ps.t[:, 0:NT], XT.t[:, oc, 0:NT], ALU.add, r=[ps, XT], w=[XT])
        mark("%s%d_l%d_FFN" % (kind, g, l))
        norm_x(l, 16, NT)
        for pc in range(6):
            nf = 4 if pc < 5 else 2
            wtg, wvg = wpiece(w_gate[l], 0, 8, 512 * pc, 128 * nf)
            wtu, wvu = wpiece(w_up[l], 0, 8, 512 * pc, 128 * nf)
            for fi in range(nf):
                f = 4 * pc + fi
                pg = psr()
                proj_fm(pg, 0, 128, wtg, wvg, 128 * fi, NT)
                pu = psr()
                proj_fm(pu, 0, 128, wtu, wvu, 128 * fi, NT)
                sg = f2.get()
                act(sg.t[:, 0:NT], pg.t[:, 0:NT], AF.Silu, r=[pg], w=[sg])
                tt(AT.t[:, f, 0:NT], pu.t[:, 0:NT], sg.t[:, 0:NT], ALU.mult, r=[pu, sg], w=[AT])
        for oc in range(8):
            wt, wv = wpiece(w_down[l], 0, 22, 128 * oc, 128)
            ps = psr()
            for f in range(22):
                mm(ps.t[:, 0:NT], wv[:, f, :], AT.t[:, f, 0:NT], f == 0, f == 21, r=[wt, AT], w=[ps])
            tt(XT.t[:, oc, 0:NT], ps.t[:, 0:NT], XT.t[:, oc, 0:NT], ALU.add, r=[ps, XT], w=[XT])

    def mla_kside_cols(l, seq, k0, c0, n, wukv_t, wukv_v):
        lb = LATC
        cp(lb.t[:, 0:n], LATB.t[:, c0:c0 + n], r=[LATB], w=[lb])
        kb2 = KRC
        cp(kb2.t[64:96, 0:n], KRB.t[64:96, c0:c0 + n], r=[KRB], w=[kb2], eng="act")
        mla_kside(l, seq, k0, n, lb, kb2, wukv_t, wukv_v)

    def sb_attention(kind, g, l, NT):
        isp = kind == "p"
        if isp:
            units = [(0, 128 * j, 128, j) for j in range(4)]
            nkb = g + 1
        else:
            units = [(1 + j, 64 * j, 64, j) for j in range(NSS)]
            nkb = NKBS
        memset(NEGC.t[:], 0.0, w=[NEGC] + NEGCB)
        seq_list = [0] if isp else list(range(1, 1 + NSS))
        zpi = [0, 0]
        ZPOOL = [PSR[0], PSR[1], PSR[2], PSL[0]]
        TPOOL = [PSR[3], PSR[4], PSR[5], PSL[1]]

        def unit(kbk, vbk, qc0, nq, slot, h, N, diag, last):
            col = 4 * slot + h
            ncb = NEGCB[col]
            zpi[0] += 1
            ps = ZPOOL[zpi[0] % 4]
            mm(ps.t[0:nq, 0:N], QA.t[:, h, qc0:qc0 + nq], kbk.t[:, h, 0:N], True, True, r=[QA, kbk], w=[ps])
            yield
            e1 = f2.get()
            act(e1.t[0:nq, 0:N], ps.t[0:nq, 0:N], AF.Exp, r=[ps], w=[e1], scale=0.125)
            yield
            act(e1.t[0:nq, 0:N], e1.t[0:nq, 0:N], AF.Ln, r=[e1], w=[e1], bias=ONESF.t[0:nq, 0:1])
            yield
            if diag:
                tt(e1.t[0:nq, N - nq:N], e1.t[0:nq, N - nq:N], MASKL01[0:nq, 0:nq], ALU.mult, r=[e1, CMf], w=[e1])
            ft = FT.get()
            S.op("dve", lambda e, o=ft.t[0:nq, 1:N + 1], d0=ONESF.t[0:nq, 0:N], d1=e1.t[0:nq, 0:N]:
                 e.tensor_tensor_scan(out=o, data0=d0, data1=d1, initial=0.0, op0=ALU.mult, op1=ALU.add),
                 r=[e1, ONESF], w=[ft])
            tt(NEGC.t[0:nq, col:col + 1], NEGC.t[0:nq, col:col + 1], ft.t[0:nq, N:N + 1], ALU.subtract,
               r=[ncb, ft], w=[ncb])
            yield
            stt(e1.t[0:nq, 0:N], ps.t[0:nq, 0:N], 0.125, ft.t[0:nq, 0:N], ALU.mult, ALU.add, r=[ps, ft], w=[e1])
            if diag:
                tt(e1.t[0:nq, N - nq:N], e1.t[0:nq, N - nq:N], MASKLNEG[0:nq, 0:nq], ALU.add, r=[e1, CMf], w=[e1])
            yield
            act(e1.t[0:nq, 0:N], e1.t[0:nq, 0:N], AF.Exp, r=[e1, ncb], w=[e1], bias=NEGC.t[0:nq, col:col + 1])
            yield
            nsub = (N + 127) // 128
            zpi[1] += 1
            pst = TPOOL[zpi[1] % 4]
            for i in range(nsub):
                w_ = min(128, N - 128 * i)
                tr(pst.t[0:w_, 128 * i:128 * i + nq], e1.t[0:nq, 128 * i:128 * i + w_], IDF[0:nq, 0:nq],
                   r=[e1, CMf], w=[pst])
            yield
            wT = b1.get()
            wmax = min(128, N)
            ceng = "act" if (zpi[1] % 4) != 0 else "dve"
            if nq == 128:
                cp(wT.t[0:wmax, 0:128 * nsub], pst.t[0:wmax, 0:128 * nsub], r=[pst], w=[wT], eng=ceng)
            else:
                cp(wT.t[0:wmax, 0:128 * nsub].rearrange("p (a b) -> p a b", a=nsub)[:, :, 0:nq],
                   pst.t[0:wmax, 0:128 * nsub].rearrange("p (a b) -> p a b", a=nsub)[:, :, 0:nq], r=[pst], w=[wT],
                   eng=ceng)
            yield
            po = ps
            for i in range(nsub):
                w_ = min(128, N - 128 * i)
                mm(po.t[0:nq, 0:64], wT.t[0:w_, 128 * i:128 * i + nq], vbk.t[0:w_, i, 64 * h:64 * h + 64],
                   i == 0, i == nsub - 1, r=[wT, vbk], w=[po])
            yield
            if last:
                cp(OSB.t[0:nq, slot, 64 * h:64 * h + 64], po.t[0:nq, 0:64], r=[po], w=[OSB])
            else:
                tt(OSB.t[0:nq, slot, 64 * h:64 * h + 64], po.t[0:nq, 0:64],
                   OSB.t[0:nq, slot, 64 * h:64 * h + 64], ALU.add, r=[po, OSB], w=[OSB])

        def all_units():
            for seq in seq_list:
                us = [u for u in units if u[0] == seq]
                for kbi in range(nkb - 1, -1, -1):
                    kbk = KBK.get()
                    vbk = VBK.get()
                    last = kbi == nkb - 1
                    nkeys = 512 if (isp or not last) else 64
                    dma(LQ, kbk.t[:, :, 0:nkeys], c_sbk[l, seq, :, :, 512 * kbi:512 * kbi + nkeys],
                        r=[CB(l, seq, kbi)], w=[kbk])
                    if nkeys == 512:
                        dma(LQ, vbk.t[:], c_sbv[l, seq, 512 * kbi:512 * kbi + 512, :].rearrange(
                            "(t p) c -> p t c", p=128), r=[CB(l, seq, kbi)], w=[vbk])
                    else:
                        dma(LQ, vbk.t[0:64, 0, :], c_sbv[l, seq, 512 * kbi:512 * kbi + 64, :],
                            r=[CB(l, seq, kbi)], w=[vbk])
                    for (sq_, qc0, nq, slot) in us:
                        N = 128 * (slot + 1) if (isp and last) else nkeys
                        for h in range(4):
                            yield unit(kbk, vbk, qc0, nq, slot, h, N, last, last)

        run_interleaved(all_units(), SBW, SBG)
        for (sq_, qc0, nq, slot) in units:
            sqt = f2.get()
            act(sqt.t[0:nq, 0:256], OSB.t[0:nq, slot, :], AF.Square, r=[OSB], w=[sqt])
            ms = SMALL.get()
            S.op("dve", lambda e, o=ms.t[0:nq, 0:4], i=sqt.t[0:nq, 0:256].rearrange("p (h c) -> p h c", h=4):
                 e.tensor_reduce(out=o, in_=i, axis=AX.X, op=ALU.add), r=[sqt], w=[ms])
            l1 = SMALL.get()
            act(l1.t[0:nq, 0:4], ms.t[0:nq, 0:4], AF.Ln, r=[ms], w=[l1], bias=EPSC[0:nq, 0:1], scale=1.0 / 64)
            r1 = SMALL.get()
            act(r1.t[0:nq, 0:4], l1.t[0:nq, 0:4], AF.Exp, r=[l1], w=[r1], scale=-0.5)
            on = f2.get()
            for h in range(4):
                ts(on.t[0:nq, 64 * h:64 * h + 64], OSB.t[0:nq, slot, 64 * h:64 * h + 64], r1.t[0:nq, h:h + 1], None,
                   ALU.mult, None, r=[OSB, r1], w=[on])
            for c in range(2):
                ps = psr()
                tr(ps.t[:, 0:nq], on.t[0:nq, 128 * c:128 * c + 128], IDF[0:nq, 0:nq], r=[on, CMf], w=[ps])
                ts(OAT.t[:, c, qc0:qc0 + nq], ps.t[:, 0:nq], gcol(l, 37), None, ALU.mult, None, r=[ps, GT], w=[OAT])

    def gla(kind, g, l, NT, NTT):
        isp = kind == "p"
        C = 128 if isp else 64
        nch = NT // C
        ps = psr()
        mm(ps.t[:, 0:NT], WGG.t[:, l, :], AG.t[:, 0:NT], True, True, r=[WGG, AG], w=[ps])
        e1 = f2.get()
        act(e1.t[:, 0:NT], ps.t[:, 0:NT], AF.Exp, r=[ps, NEGB], w=[e1], scale=-1.0, bias=NEGB.t[:, l:l + 1])
        act(SPG.t[:, 0:NT], e1.t[:, 0:NT], AF.Ln, r=[e1], w=[SPG], bias=ONESF.t[:, 0:1])
        for c in range(nch):
            S.op("dve", lambda e, o=CS.t[:, C * c:C * c + C], d0=ONESF.t[:, 0:C], d1=SPG.t[:, C * c:C * c + C]:
                 e.tensor_tensor_scan(out=o, data0=d0, data1=d1, initial=0.0, op0=ALU.mult, op1=ALU.add),
                 r=[SPG, ONESF], w=[CS])
            ts(NBL.t[:, c:c + 1], CS.t[:, C * c + C - 1:C * c + C], -1.0 / 16, None, ALU.mult, None, r=[CS], w=[NBL])
        act(EBt.t[:, 0:NT], CS.t[:, 0:NT], AF.Exp, r=[CS], w=[EBt], scale=-1.0 / 16)
        ebi = f2.get()
        act(ebi.t[:, 0:NT], CS.t[:, 0:NT], AF.Exp, r=[CS], w=[ebi], scale=1.0 / 16)
        stt(QTg.t[:, 0:NT], QG.t[:, 0:NT], 32.0 ** -0.5, EBt.t[:, 0:NT], ALU.mult, ALU.mult, r=[QG, EBt], w=[QTg])
        tt(KTg.t[:, 0:NT], KG.t[:, 0:NT], ebi.t[:, 0:NT], ALU.mult, r=[KG, ebi], w=[KTg])
        KTm = Tl(AT.t[:, 0:4, :], AT.b)
        for h in range(4):
            ts(KTm.t[:, h, 0:NT], KTg.t[:, 0:NT], GT.t[:, 44 + h:45 + h], None, ALU.mult, None, r=[KTg, GT], w=[KTm])
        ekd = f2.get()
        for c in range(nch):
            act(ekd.t[:, C * c:C * c + C], CS.t[:, C * c:C * c + C], AF.Exp, r=[CS, NBL], w=[ekd], scale=1.0 / 16,
                bias=NBL.t[:, c:c + 1])
        tt(KDg.t[:, 0:NT], KG.t[:, 0:NT], ekd.t[:, 0:NT], ALU.mult, r=[KG, ekd], w=[KDg])
        st = SALL[l] if isp else SS
        if isp and g == 0:
            memset(st.t[:], 0.0, w=[st])
        BMASK = CMf.t[:, 11 * 128:13 * 128]
        for c in range(nch):
            if not isp:
                memset(st.t[:], 0.0, w=[st])
                for h in range(4):
                    dma("sp", st.t[32 * h:32 * h + 32, 64 * h:64 * h + 64], sgla[l, c, h], r=[], w=[st])
            tt(SBF16.t[:], st.t[:], BMASK, ALU.mult, r=[st, CMf], w=[SBF16])
            cs_ = slice(C * c, C * c + C)
            pk = psr()
            tr(pk.t[0:C, 0:128], KDg.t[:, cs_], IDF, r=[KDg, CMf], w=[pk])
            kdt = KDt.get()
            cp(kdt.t[0:C, :], pk.t[0:C, 0:128], r=[pk], w=[kdt])
            tt_, po_ = (C * c) // 128, (C * c) % 128
            if isp:
                vsrc, vt = VG, VG.t[:, tt_, :]
            else:
                dma("sp", VGLO.t[0:C, :], VG.t[po_:po_ + C, tt_, :], r=[VG], w=[VGLO])
                vsrc, vt = VGLO, VGLO.t[:, :]
            for h in range(4):
                pa = psr()
                mm(pa.t[0:C, 0:C], KTm.t[:, h, cs_], QTg.t[:, cs_], True, True, r=[KTm, QTg], w=[pa])
                am = ATM.get()
                tt(am.t[0:C, 0:C], pa.t[0:C, 0:C], MASKU01[0:C, 0:C], ALU.mult, r=[pa, CMf], w=[am])
                po = psr()
                pr = slice(64 * (h % 2), 64 * (h % 2) + 64)
                mm(po.t[pr, 0:C], vt[:, 64 * h:64 * h + 64], am.t[:, 0:C], True, False, r=[vsrc, am], w=[po])
                mm(po.t[pr, 0:C], SBF16.t[:, 64 * h:64 * h + 64], QTg.t[:, cs_], False, True, r=[SBF16, QTg], w=[po])
                cp(OBT.t[pr, h // 2, cs_], po.t[pr, 0:C], r=[po], w=[OBT], eng=("act" if h % 2 else "dve"))
            pu = psr()
            mm(pu.t[:, 0:256], kdt.t[:, :], vt, True, True, r=[kdt, vsrc], w=[pu])
            stt(st.t[:], st.t[:], EBt.t[:, C * c + C - 1:C * c + C], pu.t[:, 0:256], ALU.mult, ALU.add,
                r=[st, EBt, pu], w=[st])
            if not isp:
                for h in range(4):
                    dma("sp", o_gla_s[l, c, h], st.t[32 * h:32 * h + 32, 64 * h:64 * h + 64], r=[st], w=[])
        if isp and g == NG - 1:
            for h in range(4):
                dma("sp", o_gla_p[l, h], st.t[32 * h:32 * h + 32, 64 * h:64 * h + 64], r=[st], w=[])
        for c2 in range(2):
            sq = b1.get()
            act(sq.t[:, 0:NT], OBT.t[:, c2, 0:NT], AF.Square, r=[OBT], w=[sq])
            ps2 = psr()
            mm(ps2.t[:, 0:NT], BD64, sq.t[:, 0:NT], True, True, r=[sq, CMb], w=[ps2])
            rs = rstd_from(ps2.t[:, 0:NT], ps2, 0, 128, NT)
            tmp = f2.get()
            stt(tmp.t[:, 0:NT], OBT.t[:, c2, 0:NT], gcol(l, 38), rs.t[:, 0:NT], ALU.mult, ALU.mult,
                r=[OBT, rs, GT], w=[tmp])
            tt(OBB.t[:, c2, 0:NT], tmp.t[:, 0:NT], SRG.t[:, c2, 0:NT], ALU.mult, r=[tmp, SRG], w=[OBB])

    VGLO = S.sb([128, 256], BF, "VGLO")
    memset(VGLO.t[:], 0.0, w=[VGLO])
    for _t in KDt.tiles + ATM.tiles:
        memset(_t.t[:], 0.0, w=[_t])
    LATC = S.sb([128, 64], BF, "LATC")
    KRC = S.sb([96, 64], BF, "KRC")

    def mla_attention(kind, g, l, NT):
        isp = kind == "p"
        SC = 96.0 ** -0.5
        segs = [(0, 0, 512, g + 1)] if isp else [(1 + j, 64 * j, 64, NKBS) for j in range(NSS)]
        for (seq, c0, ncl, nkb) in segs:
            for h in range(8):
                po = psl()

                def step(kb_, vb_, i, w_, first, final, dmask):
                    ps = psr()
                    mm(ps.t[0:w_, 0:ncl], kb_.t[:, 128 * i:128 * i + w_], QTm.t[:, h, c0:c0 + ncl], True, True,
                       r=[kb_, QTm], w=[ps])
                    yield
                    pt = b1.get()
                    act(pt.t[0:w_, 0:ncl], ps.t[0:w_, 0:ncl], AF.Exp, r=[ps, NEG4], w=[pt], scale=SC,
                        bias=NEG4.t[0:w_, 0:1])
                    if dmask:
                        tt(pt.t[:, 0:512], pt.t[:, 0:512], MDb.t[:, 512 * i:512 * i + 512], ALU.mult,
                           r=[pt, MDb], w=[pt])
                    yield
                    mm(po.t[0:65, 0:ncl], vb_.t[0:w_, i, :], pt.t[0:w_, 0:ncl], first, final, r=[vb_, pt], w=[po])

                def step_blk(kb_, vb_, nsub, w_, first, final):
                    ps = psr()
                    for i in range(nsub):
                        mm(ps.t[0:w_, ncl * i:ncl * i + ncl], kb_.t[:, 128 * i:128 * i + w_], QTm.t[:, h, c0:c0 + ncl],
                           True, True, r=[kb_, QTm], w=[ps])
                    yield
                    pt = b1.get()
                    act(pt.t[0:w_, 0:ncl * nsub], ps.t[0:w_, 0:ncl * nsub], AF.Exp, r=[ps, NEG4], w=[pt], scale=SC,
                        bias=NEG4.t[0:w_, 0:1])
                    yield
                    for i in range(nsub):
                        mm(po.t[0:65, 0:ncl], vb_.t[0:w_, i, :], pt.t[0:w_, ncl * i:ncl * i + ncl],
                           first and i == 0, final and i == nsub - 1, r=[vb_, pt], w=[po])

                def all_steps():
                    for kbi in range(nkb):
                        last = kbi == nkb - 1
                        nkeys = 512 if (isp or not last) else 64
                        kb_ = MKB.get()
                        vb_ = MVB.get()
                        dma(LQ, kb_.t[:, 0:nkeys], c_mk[l, seq, :, h, 512 * kbi:512 * kbi + nkeys],
                            r=[CB(l, seq, kbi)], w=[kb_])
                        if nkeys == 512:
                            dma(LQ, vb_.t[:], c_mv[l, seq, h, 512 * kbi:512 * kbi + 512, :].rearrange(
                                "(t p) c -> p t c", p=128), r=[CB(l, seq, kbi)], w=[vb_])
                        else:
                            dma(LQ, vb_.t[0:64, 0, :], c_mv[l, seq, h, 512 * kbi:512 * kbi + 64, :],
                                r=[CB(l, seq, kbi)], w=[vb_])
                        nsub = (nkeys + 127) // 128
                        if not isp:
                            yield step_blk(kb_, vb_, nsub, min(128, nkeys), kbi == 0, last)
                            continue
                        for i in range(nsub):
                            w_ = min(128, nkeys - 128 * i)
                            yield step(kb_, vb_, i, w_, kbi == 0 and i == 0, last and i == nsub - 1, isp and last)

                run_interleaved(all_steps(), MLW if isp else 2, 1 if isp else 0)
                sq = b1.get()
                act(sq.t[0:65, 0:ncl], po.t[0:65, 0:ncl], AF.Square, r=[po], w=[sq])
                ps2 = psr()
                mm(ps2.t[0:64, 0:ncl], W65[0:65, 0:64], sq.t[0:65, 0:ncl], True, True, r=[sq, CMb], w=[ps2])
                t1 = f2.get()
                act(t1.t[0:64, 0:ncl], ps2.t[0:64, 0:ncl], AF.Ln, r=[ps2], w=[t1])
                t2 = f2.get()
                act(t2.t[0:64, 0:ncl], t1.t[0:64, 0:ncl], AF.Exp, r=[t1], w=[t2], scale=-0.5)
                stt(OCT.t[:, h, c0:c0 + ncl], po.t[0:64, 0:ncl], gcol(l, 39, 0, 64), t2.t[0:64, 0:ncl], ALU.mult,
                    ALU.mult, r=[po, t2, GT], w=[OCT])

    NEG4 = S.sb([128, 1], F32, "NEG4")
    memset(NEG4.t[:], -4.0, w=[NEG4])

    import os as _os
    _ph = _os.environ.get("KPHASES", "abcd")
    mark("prep_mem")
    if "a" in _ph:
        prep_mem_prompt()
    mark("prep_sample")
    if "b" in _ph:
        prep_sample_caches()
    if "c" in _ph:
        group("s", 0)
    if "d" in _ph:
        for g in range(NG):
            group("p", g)
    mark("end")
    S.finish()
    es.close()
    return nc


def host_consts(cfg):
    cm = np.zeros((128, 13, 128), np.float32)
    cm[:, 0] = np.eye(128)
    cm[:, 1] = 1.0 / 1024
    cm[:, 2] = 1.0 / 256
    cm[:, 3] = 1.0 / 128
    cm[0:64, 4, 0:64] = 1.0 / 64
    cm[64:128, 4, 64:128] = 1.0 / 64
    cm[0:64, 5, 0:64] = 1.0 / 64
    cm[64:96, 5, 64:96] = 1.0 / 32
    cm[0:64, 6, 0:64] = 1.0 / 64
    cm[64, 6, 0:64] = EPS
    q = np.arange(128)[:, None]
    k = np.arange(128)[None, :]
    cm[:, 7] = (k < q)
    cm[:, 8] = np.where(k < q, 0.0, -30000.0)
    cm[:, 9] = (q <= k)
    cm[:, 10] = 1.0
    bm = (np.arange(128)[:, None] // 32 == np.arange(256)[None, :] // 64).astype(np.float32)
    cm[:, 11] = bm[:, 0:128]
    cm[:, 12] = bm[:, 128:256]
    cmat = cm.reshape(128, 13 * 128)
    half = 16
    freqs = (10000.0 ** (-np.arange(half, dtype=np.float32) / half)).astype(np.float32)
    pos = np.arange(cfg.NPOS, dtype=np.float32)
    ang = (pos[None, :] * freqs[:, None]).astype(np.float32)
    c = np.cos(ang).astype(np.float32)
    s = np.sin(ang).astype(np.float32)
    ropec = np.ones((96, cfg.NPOS), np.float32)
    ropes = np.zeros((96, cfg.NPOS), np.float32)
    ropec[64:80] = c
    ropec[80:96] = c
    ropes[64:80] = -s
    ropes[80:96] = s
    md = np.zeros((128, 4, 512), np.float32)
    kk = np.arange(128)[:, None]
    qq = np.arange(512)[None, :]
    for i in range(4):
        md[:, i, :] = ((128 * i + kk) // 64 <= qq // 64)
    return cmat, ropec, ropes, md.reshape(128, 2048)


def host_gtab(inp):
    gt = np.ones((128, 2 * GL), np.float32)
    for l in range(2):
        b = GL * l
        gt[:, b + 0:b + 8] = inp["g_mix_norm"][l].reshape(8, 128).T
        gt[:, b + 8:b + 16] = inp["g_cross_norm"][l].reshape(8, 128).T
        gt[:, b + 16:b + 24] = inp["g_ffn_norm"][l].reshape(8, 128).T
        gt[:, b + 24:b + 32] = inp["g_mem_norm"][l].reshape(8, 128).T
        gt[:, b + 32:b + 34] = inp["g_cq"][l].reshape(2, 128).T
        gt[:, b + 34] = inp["g_ckv"][l]
        gt[0:64, b + 35] = inp["g_qn"][l]
        gt[64:96, b + 35] = inp["g_qr"][l]
        gt[0:64, b + 36] = inp["g_kn"][l]
        gt[64:96, b + 36] = inp["g_kr"][l]
        gt[0:64, b + 37] = inp["g_sb_out"][l]
        gt[64:128, b + 37] = inp["g_sb_out"][l]
        gt[0:64, b + 38] = inp["g_gla_out"][l]
        gt[64:128, b + 38] = inp["g_gla_out"][l]
        gt[0:64, b + 39] = inp["g_mla_out"][l]
        gt[:, b + 40] = inp["g_cqn"][l]
        gt[:, b + 41] = inp["g_ckn"][l]
        gt[:, b + 42] = inp["b_gla_gate"][l]
    for h in range(4):
        gt[:, 44 + h] = 0.0
        gt[32 * h:32 * h + 32, 44 + h] = 1.0
    return gt


_NC_CACHE = {}


def run(inp, cfg):
    key = (cfg.SEQ, cfg.PAST, cfg.NSS)
    if key not in _NC_CACHE:
        _NC_CACHE[key] = build(cfg)
    nc = _NC_CACHE[key]
    f = lambda a: np.ascontiguousarray(np.asarray(a, dtype=np.float32))
    cmat, ropec, ropes, md = host_consts(cfg)
    gt = host_gtab({k: np.asarray(v) for k, v in inp.items()})
    NSS = cfg.NSS
    shared = {k: f(inp[k]) for k in ("w_in", "w_gla_gate", "w_uq", "w_ukv", "w_out", "w_cq", "w_ck", "w_cv", "w_co",
                                     "w_gate", "w_up", "w_down")}
    shared.update(gtab=gt, cmat=cmat, ropec=ropec, ropes=ropes, mdiag=md)
    in_maps = []
    for c in range(8):
        sl = slice(NSS * c, NSS * c + NSS)
        m = dict(shared)
        m["xp"] = f(inp["x_prompt"][c // 2])
        m["xs"] = f(inp["x_sample"][sl]).reshape(NSS * 64, D)
        m["memp"] = f(inp["mem_prompt"][c // 2])
        m["csk"] = f(inp["cache_sb_k"][:, sl]).reshape(2, NSS, cfg.PAST, 256)
        m["csv"] = f(inp["cache_sb_v"][:, sl]).reshape(2, NSS, cfg.PAST, 256)
        m["sgla"] = f(inp["state_gla"][:, sl])
        m["clat"] = f(inp["cache_mla_latent"][:, sl])
        m["ckr"] = f(inp["cache_mla_krope"][:, sl])
        m["cmk"] = f(inp["cache_mem_k"][:, sl]).reshape(2, NSS, 256, 512)
        m["cmv"] = f(inp["cache_mem_v"][:, sl]).reshape(2, NSS, 256, 512)
        in_maps.append(m)
    res = run_bass_kernel_spmd(nc, in_maps, core_ids=list(range(8)))
    R = res.results
    B = 4
    SEQ = cfg.SEQ
    DB = 8 * NSS
    cat_p = lambda name, shp: np.stack([np.asarray(R[2 * b][name]).reshape(shp) for b in range(B)])
    y_p = cat_p("yp", (SEQ, D))
    y_s = np.concatenate([np.asarray(R[c]["ys"]).reshape(NSS, 64, D) for c in range(8)], 0)

    def pl(name, shp):
        return np.stack([np.asarray(R[2 * b][name]).reshape((2,) + shp) for b in range(B)], 1)

    def sl_(name, shp):
        return np.concatenate([np.asarray(R[c][name]).reshape((2, NSS) + shp) for c in range(8)], 1)

    outs = (y_p, y_s,
            pl("sbk_p", (SEQ, 4, 64)), pl("sbv_p", (SEQ, 4, 64)), pl("gla_p", (4, 32, 64)),
            pl("lat_p", (SEQ, 128)), pl("kr_p", (SEQ, 32)), pl("mk_p", (256, 4, 128)), pl("mv_p", (256, 4, 128)),
            sl_("sbk_s", (64, 4, 64)), sl_("sbv_s", (64, 4, 64)), sl_("gla_s", (4, 32, 64)),
            sl_("lat_s", (64, 128)), sl_("kr_s", (64, 32)))
    return tuple(np.ascontiguousarray(o.astype(np.float32)) for o in outs)


def kernel(**inputs):
    return run(inputs, Cfg())
```

```python
from contextlib import ExitStack
import numpy as np
import concourse.bass as bass
import concourse.mybir as mybir
from concourse.bass_utils import run_bass_kernel_spmd

F32 = mybir.dt.float32
BF = mybir.dt.bfloat16
AF = mybir.ActivationFunctionType
ALU = mybir.AluOpType
AX = mybir.AxisListType

D = 1024
NIN = 1968
DFF = 2816
EPS = 1e-6
GL = 48


class Cfg:
    def __init__(self, SEQ=4096, PAST=4096, NSS=4):
        self.SEQ = SEQ
        self.PAST = PAST
        self.NSS = NSS
        self.NG = SEQ // 512
        self.NK = max(SEQ, PAST + 512)
        self.NTOT = PAST + 64
        self.NPOS = max(SEQ, PAST + 64)


class Buf:
    __slots__ = ("w", "r", "name")

    def __init__(self, name=""):
        self.w = None
        self.r = {}
        self.name = name


class Tl:
    __slots__ = ("t", "b")

    def __init__(self, t, b):
        self.t = t
        self.b = b


class Sched:
    ROT = 30000
    KD = 8

    def __init__(self, nc, es):
        self.nc = nc
        self.es = es
        self.eng = {}
        for name, kind in (("pe", "c"), ("act", "c"), ("dve", "c"), ("pool", "d"), ("sp", "d")):
            self.eng[name] = dict(ops=[], n=0, kind=kind, known={})
        self.nt = 0

    def sb(self, shape, dt, name=None):
        self.nt += 1
        name = name or ("t%d" % self.nt)
        t = self.es.enter_context(self.nc.sbuf_tensor(name, list(shape), dt))
        return Tl(t, Buf(name))

    def ps(self, name):
        t = self.es.enter_context(self.nc.psum_tensor(name, [128, 512], F32))
        return Tl(t, Buf(name))

    def dram(self, name, shape, dt):
        return self.nc.dram_tensor(name, list(shape), dt, kind="Internal").ap()

    def _kv(self, dep):
        en, seq = dep
        if self.eng[en]["kind"] == "c":
            return (en, (seq - 1) // self.ROT), (seq - 1) % self.ROT + 1
        k = seq - 1
        return (en, k % self.KD), 16 * (k // self.KD + 1)

    def op(self, en, fn, r=(), w=()):
        e = self.eng[en]
        deps = set()
        for b in r:
            b = b.b if isinstance(b, Tl) else b
            if b.w is not None:
                deps.add(b.w)
        for b in w:
            b = b.b if isinstance(b, Tl) else b
            if b.w is not None:
                deps.add(b.w)
            for x in b.r.values():
                deps.update(x)
        e["n"] += 1
        seq = e["n"]
        me = (en, seq)
        if e["kind"] == "d" and seq > self.KD:
            deps.add((en, seq - self.KD))
        waits = []
        for d in deps:
            if d[0] == en and en == "pe":
                continue
            key, val = self._kv(d)
            if e["known"].get(key, 0) >= val:
                continue
            e["known"][key] = val
            waits.append((key, val))
        e["ops"].append((waits, fn, self._kv(me)))
        for b in r:
            b = b.b if isinstance(b, Tl) else b
            lst = b.r.setdefault(en, [])
            if e["kind"] == "c":
                lst[:] = [me]
            else:
                lst.append(me)
        for b in w:
            b = b.b if isinstance(b, Tl) else b
            b.w = me
            b.r = {}

    def finish(self):
        nc = self.nc
        fin = []
        for en in ("pool", "sp"):
            n = self.eng[en]["n"]
            for s in range(max(1, n - self.KD + 1), n + 1):
                fin.append(self._kv((en, s)))
        for en in ("pe", "act", "dve"):
            n = self.eng[en]["n"]
            if n:
                fin.append(self._kv((en, n)))
        keys = set()
        for e in self.eng.values():
            for waits, fn, kv in e["ops"]:
                keys.add(kv[0])
        sems = {}
        for k in sorted(keys):
            sems[k] = self.es.enter_context(nc.semaphore("s_%s_%d" % k))
        block = self.es.enter_context(nc.Block())
        engs = self.eng

        def replay(en, eobj, extra=()):
            inc = 1 if engs[en]["kind"] == "c" else 16
            for waits, fn, kv in engs[en]["ops"]:
                for k, v in waits:
                    eobj.wait_ge(sems[k], v)
                fn(eobj).then_inc(sems[kv[0]], inc)
            for k, v in extra:
                eobj.wait_ge(sems[k], v)

        @block.tensor
        def _(t):
            replay("pe", t)

        @block.scalar
        def _(a):
            replay("act", a)

        @block.vector
        def _(v):
            replay("dve", v)

        @block.gpsimd
        def _(g):
            replay("pool", g)

        @block.sync
        def _(s):
            replay("sp", s, fin)


class Pool:
    def __init__(self, S, shape, dt, n, name):
        self.tiles = [S.sb(shape, dt, "%s%d" % (name, i)) for i in range(n)]
        self.i = 0

    def get(self):
        t = self.tiles[self.i % len(self.tiles)]
        self.i += 1
        return t


def run_interleaved(gens, width, gap=0):
    active = []
    it = iter(gens)
    done = False
    since = gap
    while True:
        while not done and len(active) < width and (since >= gap or not active):
            g = next(it, None)
            if g is None:
                done = True
                break
            active.append(g)
            since = 0
            if gap > 0:
                break
        since += 1
        if not active:
            break
        for g in list(active):
            try:
                next(g)
            except StopIteration:
                active.remove(g)


def build(cfg):
    nc = bass.Bass("TRN2", target_bir_lowering=False)
    SEQ, PAST, NSS, NG, NK = cfg.SEQ, cfg.PAST, cfg.NSS, cfg.NG, cfg.NK
    NSEQ = 1 + NSS
    NTS = NSS * 64
    NKBS = PAST // 512 + 1

    def din(name, shape):
        return nc.dram_tensor(name, list(shape), F32, kind="ExternalInput").ap()

    def dout(name, shape):
        return nc.dram_tensor(name, list(shape), F32, kind="ExternalOutput").ap()

    xp = din("xp", [SEQ, D])
    xs = din("xs", [NTS, D])
    memp = din("memp", [256, D])
    csk = din("csk", [2, NSS, PAST, 256])
    csv = din("csv", [2, NSS, PAST, 256])
    sgla = din("sgla", [2, NSS, 4, 32, 64])
    clat = din("clat", [2, NSS, PAST, 128])
    ckr = din("ckr", [2, NSS, PAST, 32])
    cmk = din("cmk", [2, NSS, 256, 512])
    cmv = din("cmv", [2, NSS, 256, 512])
    w_in = din("w_in", [2, D, NIN])
    w_gg = din("w_gla_gate", [2, 16, 128])
    w_uq = din("w_uq", [2, 256, 768])
    w_ukv = din("w_ukv", [2, 128, 1024])
    w_out = din("w_out", [2, D, D])
    w_cq = din("w_cq", [2, D, 512])
    w_ck = din("w_ck", [2, D, 512])
    w_cv = din("w_cv", [2, D, 512])
    w_co = din("w_co", [2, 512, D])
    w_gate = din("w_gate", [2, D, DFF])
    w_up = din("w_up", [2, D, DFF])
    w_down = din("w_down", [2, DFF, D])
    gtab = din("gtab", [128, 2 * GL])
    cmat = din("cmat", [128, 13 * 128])
    ropec = din("ropec", [96, cfg.NPOS])
    ropes = din("ropes", [96, cfg.NPOS])
    mdiag = din("mdiag", [128, 4 * 512])

    yp = dout("yp", [SEQ, D])
    ys = dout("ys", [NTS, D])
    o_sbk_p = dout("sbk_p", [2, SEQ, 256])
    o_sbv_p = dout("sbv_p", [2, SEQ, 256])
    o_gla_p = dout("gla_p", [2, 4, 32, 64])
    o_lat_p = dout("lat_p", [2, SEQ, 128])
    o_kr_p = dout("kr_p", [2, SEQ, 32])
    o_mk_p = dout("mk_p", [2, 256, 512])
    o_mv_p = dout("mv_p", [2, 256, 512])
    o_sbk_s = dout("sbk_s", [2, NTS, 256])
    o_sbv_s = dout("sbv_s", [2, NTS, 256])
    o_gla_s = dout("gla_s", [2, NSS, 4, 32, 64])
    o_lat_s = dout("lat_s", [2, NTS, 128])
    o_kr_s = dout("kr_s", [2, NTS, 32])

    es = ExitStack()
    S = Sched(nc, es)
    c_sbk = S.dram("c_sbk", [2, NSEQ, 64, 4, NK], BF)
    c_sbv = S.dram("c_sbv", [2, NSEQ, NK, 256], BF)
    c_mk = S.dram("c_mk", [2, NSEQ, 96, 8, NK], BF)
    c_mv = S.dram("c_mv", [2, NSEQ, 8, NK, 65], BF)
    c_memk = S.dram("c_memk", [2, NSEQ, 128, 4, 256], BF)
    c_memv = S.dram("c_memv", [2, NSEQ, 256, 512], BF)
    cbuf = {}

    def CB(l, s, kb):
        k = (l, s, kb)
        if k not in cbuf:
            cbuf[k] = Buf("cb%d_%d_%d" % k)
        return cbuf[k]

    membuf = {(l, s): Buf("mem%d_%d" % (l, s)) for l in range(2) for s in range(NSEQ)}

    PSR = [S.ps("psr%d" % i) for i in range(6)]
    PSL = [S.ps("psl%d" % i) for i in range(2)]
    psi = [0, 0]

    def psr():
        psi[0] += 1
        return PSR[psi[0] % 6]

    def psl():
        psi[1] += 1
        return PSL[psi[1] % 2]

    GT = S.sb([128, 2 * GL], F32, "GT")
    NEGB = S.sb([128, 2], F32, "NEGB")
    CMf = S.sb([128, 13 * 128], F32, "CMf")
    CMb = S.sb([128, 13 * 128], BF, "CMb")
    MDb = S.sb([128, 4 * 512], BF, "MDb")
    ONESF = S.sb([128, 512], F32, "ONESF")
    WGG = S.sb([16, 2, 128], BF, "WGG")
    def cmf(i):
        return CMf.t[:, 128 * i:128 * (i + 1)]

    def cmb(i):
        return CMb.t[:, 128 * i:128 * (i + 1)]

    XT = S.sb([128, 8, 512], F32, "XT")
    HT = S.sb([128, 8, 512], BF, "HT")
    WPn = 3
    WP = [S.sb([128, 4224], BF, "WP%d" % i) for i in range(WPn)]
    wpi = [0]
    f2 = Pool(S, [128, 512], F32, 5, "f2_")
    b1 = Pool(S, [128, 512], BF, 5, "b1_")
    XIN = Pool(S, [128, 1024], F32, 2, "xin")
    QA = S.sb([64, 4, 512], BF, "QA")
    KA = S.sb([64, 4, 512], BF, "KA")
    VBF = S.sb([128, 4, 256], BF, "VBF")
    KBK = Pool(S, [64, 4, 512], BF, 2, "kbk")
    VBK = Pool(S, [128, 4, 256], BF, 2, "vbk")
    OSBT = S.sb([128, 1024], F32, "OSBT")
    OSB = Tl(OSBT.t[:, :].rearrange("p (a b) -> p a b", a=4), OSBT.b)
    OBT = Tl(OSBT.t[:, :].rearrange("p (a b) -> p a b", a=2), OSBT.b)
    NEGC = S.sb([128, 16], F32, "NEGC")
    NEGCB = [Buf("negc%d" % i) for i in range(16)]
    import os as _os2
    SBW = int(_os2.environ.get("KSBW", "4"))
    MLW = int(_os2.environ.get("KMLW", "4"))
    SBG = int(_os2.environ.get("KSBG", "2"))
    LQ = _os2.environ.get("KLQ", "pool")
    FT = Pool(S, [128, 513], F32, 4, "ft")
    QG = S.sb([128, 512], F32, "QG")
    KG = S.sb([128, 512], F32, "KG")
    AG = S.sb([16, 512], BF, "AG")
    SPG = S.sb([128, 512], F32, "SPG")
    CS = S.sb([128, 512], F32, "CS")
    EBt = S.sb([128, 512], F32, "EB")
    NBL = S.sb([128, 8], F32, "NBL")
    QTg = S.sb([128, 512], BF, "QTg")
    KTg = S.sb([128, 512], BF, "KTg")
    KDg = S.sb([128, 512], F32, "KDg")
    KDt = Pool(S, [128, 128], BF, 2, "kdt")
    VG = S.sb([128, 4, 256], BF, "VG")
    SALL = [S.sb([128, 256], F32, "SALL%d" % i) for i in range(2)]
    SS = S.sb([128, 256], F32, "SS")
    SBF16 = S.sb([128, 256], BF, "SBF16")
    ATM = Pool(S, [128, 128], BF, 3, "atm")
    SRG = S.sb([128, 2, 512], BF, "SRG")
    CQ = OBT
    LAT = S.sb([128, 512], F32, "LAT")
    LATB = S.sb([128, 512], BF, "LATB")
    KRN = S.sb([96, 512], F32, "KRN")
    ROT = Pool(S, [96, 512], F32, 2, "rot")
    KRB = S.sb([96, 512], BF, "KRB")
    QN = Pool(S, [96, 512], F32, 1, "qn")
    QTm = S.sb([96, 8, 512], BF, "QTm")
    KTn = Pool(S, [96, 512], BF, 2, "ktn")
    VAUG = S.sb([128, 4, 8 * 65], BF, "VAUG")
    COS = S.sb([96, 512], F32, "COS")
    SIN = S.sb([96, 512], F32, "SIN")
    MKB = Pool(S, [96, 512], BF, 2, "mkb")
    MVB = Pool(S, [128, 4, 65], BF, 2, "mvb")
    MEMK = Pool(S, [128, 4, 256], BF, 1, "memk")
    MEMV = Pool(S, [128, 2, 512], BF, 1, "memv")
    AT = S.sb([128, 22, 512], BF, "AT")
    SMALL = Pool(S, [128, 16], F32, 6, "small")
    OUTT = XIN

    class _V:
        def __init__(self, ap):
            self.t = ap
            self.b = AT.b
    OCR = Tl(AT.t[:, 0:4, :], AT.b)
    QCT = Tl(AT.t[:, 4:8, :], AT.b)
    OCT = Tl(AT.t[0:64, 8:16, :], AT.b)
    OAT = Tl(AT.t[:, 16:18, :], AT.b)
    OBB = Tl(AT.t[:, 18:20, :], AT.b)
    CQN = Tl(AT.t[:, 20:22, :], AT.b)

    def mm(out, lhsT, rhs, start, stop, r, w):
        S.op("pe", lambda e: e.matmul(out, lhsT=lhsT, rhs=rhs, start=start, stop=stop), r=r, w=w)

    def tr(out, in_, ident, r, w):
        S.op("pe", lambda e: e.transpose(out, in_, ident), r=r, w=w)

    def act(out, in_, func, r, w, bias=None, scale=None):
        kw = {}
        if bias is not None:
            kw["bias"] = bias
        if scale is not None:
            kw["scale"] = scale
        S.op("act", lambda e: e.activation(out, in_, func, **kw), r=r, w=w)

    def tt(out, in0, in1, op, r, w):
        S.op("dve", lambda e: e.tensor_tensor(out, in0, in1, op), r=r, w=w)

    def ts(out, in0, s1, s2, op0, op1, r, w):
        if op1 is None:
            S.op("dve", lambda e: e.tensor_scalar(out, in0, s1, None, op0), r=r, w=w)
        else:
            S.op("dve", lambda e: e.tensor_scalar(out, in0, s1, s2, op0, op1), r=r, w=w)

    def stt(out, in0, sc, in1, op0, op1, r, w):
        S.op("dve", lambda e: e.scalar_tensor_tensor(out, in0, sc, in1, op0, op1), r=r, w=w)

    def cp(out, in_, r, w, eng="dve"):
        if eng == "dve":
            S.op("dve", lambda e: e.tensor_copy(out, in_), r=r, w=w)
        else:
            S.op("act", lambda e: e.activation(out, in_, AF.Copy), r=r, w=w)

    def dma(q, out, in_, r, w):
        S.op(q, lambda e: e.dma_start(out=out, in_=in_), r=r, w=w)

    def memset(ap, val, w):
        S.op("dve", lambda e: e.memset(ap, val), r=(), w=w)

    def rstd_from(ps_ap, pb, P0, P1, N, eps=EPS):
        t1 = f2.get()
        act(t1.t[P0:P1, 0:N], ps_ap, AF.Ln, r=[pb], w=[t1], bias=EPSC[P0:P1, 0:1] if eps else None)
        t2 = f2.get()
        act(t2.t[P0:P1, 0:N], t1.t[P0:P1, 0:N], AF.Exp, r=[t1], w=[t2], scale=-0.5)
        return t2

    dma("sp", GT.t[:], gtab[:, :], r=[], w=[GT])
    dma("sp", CMf.t[:], cmat[:, :], r=[], w=[CMf])
    dma("pool", MDb.t[:], mdiag[:, :], r=[], w=[MDb])
    dma("pool", CMb.t[:], cmat[:, :], r=[], w=[CMb])
    memset(ONESF.t[:], 1.0, w=[ONESF])
    EPSt = S.sb([128, 1], F32, "EPSC")
    EPSC = EPSt.t
    memset(EPSC[:], EPS, w=[EPSt])
    for l in range(2):
        ts(NEGB.t[:, l:l + 1], GT.t[:, GL * l + 42:GL * l + 43], -1.0, None, ALU.mult, None, r=[GT], w=[NEGB])
        dma("pool", WGG.t[:, l, :], w_gg[l], r=[], w=[WGG])
    for t in FT.tiles:
        memset(t.t[:, 0:1], 0.0, w=[t])
    memset(VAUG.t[:], 1.0, w=[VAUG])
    IDF = cmf(0)
    ONES1024, ONES256, ONES128, BD64, BD96, W65 = cmb(1), cmb(2), cmb(3), cmb(4), cmb(5), cmb(6)
    MASKL01, MASKLNEG, MASKU01 = cmf(7), cmf(8), cmf(9)
    CONST = [CMf, CMb, EPSt]

    def gcol(l, c, P0=0, P1=128):
        return GT.t[P0:P1, GL * l + c:GL * l + c + 1]

    def wload(src_ap, nelem):
        wpi[0] += 1
        t = WP[wpi[0] % WPn]
        return t

    def wpiece(w2d, r0, nkc, c0, ncol, p=128):
        wpi[0] += 1
        t = WP[wpi[0] % WPn]
        view = t.t[0:p, 0:nkc * ncol].rearrange("p (a b) -> p a b", a=nkc)
        src = w2d[r0:r0 + nkc * p, c0:c0 + ncol].rearrange("(kc p) n -> p kc n", p=p)
        dma("pool", view, src, r=[], w=[t])
        return t, view

    def norm_x(l, gbase, NT):
        ps = psr()
        for c in range(8):
            sq = b1.get()
            act(sq.t[:, 0:NT], XT.t[:, c, 0:NT], AF.Square, r=[XT], w=[sq])
            mm(ps.t[:, 0:NT], ONES1024, sq.t[:, 0:NT], c == 0, c == 7, r=[sq, CMb], w=[ps])
        rs = rstd_from(ps.t[:, 0:NT], ps, 0, 128, NT)
        for c in range(8):
            stt(HT.t[:, c, 0:NT], XT.t[:, c, 0:NT], gcol(l, gbase + c), rs.t[:, 0:NT], ALU.mult, ALU.mult,
                r=[XT, rs, GT], w=[HT])

    def proj_fm(ps, pr0, M, wt, wv, col0, NT, KC=8, rhs=None, rb=None):
        rhs = rhs if rhs is not None else HT
        for kc in range(KC):
            mm(ps.t[pr0:pr0 + M, 0:NT], wv[:, kc, col0:col0 + M], rhs.t[:, kc, 0:NT], kc == 0, kc == KC - 1,
               r=[wt, rhs], w=[ps])

    def mla_kside(l, seq, k0, NT, latb, krb, wukv_t, wukv_v):
        kb = k0 // 512

        def head(h):
            ps = psr()
            mm(ps.t[0:64, 0:NT], wukv_v[:, 0, 128 * h:128 * h + 64], latb.t[:, 0:NT], True, True,
               r=[wukv_t, latb], w=[ps])
            yield
            sq = b1.get()
            act(sq.t[0:64, 0:NT], ps.t[0:64, 0:NT], AF.Square, r=[ps], w=[sq])
            yield
            ps2 = psr()
            mm(ps2.t[0:64, 0:NT], BD64[0:64, 0:64], sq.t[0:64, 0:NT], True, True, r=[sq, CMb], w=[ps2])
            yield
            t1 = f2.get()
            act(t1.t[0:64, 0:NT], ps2.t[0:64, 0:NT], AF.Ln, r=[ps2], w=[t1], bias=EPSC[0:64, 0:1])
            yield
            act(t1.t[0:64, 0:NT], t1.t[0:64, 0:NT], AF.Exp, r=[t1], w=[t1], scale=-0.5)
            yield
            kt = KTn.get()
            stt(kt.t[0:64, 0:NT], ps.t[0:64, 0:NT], gcol(l, 36, 0, 64), t1.t[0:64, 0:NT], ALU.mult, ALU.mult,
                r=[ps, t1, GT], w=[kt])
            cp(kt.t[64:96, 0:NT], krb.t[64:96, 0:NT], r=[krb], w=[kt])
            yield
            dma("sp", c_mk[l, seq, :, h, k0:k0 + NT], kt.t[0:96, 0:NT], r=[kt], w=[CB(l, seq, kb)])

        run_interleaved((head(h) for h in range(8)), 3, 2)
        ntt = (NT + 127) // 128
        for t in range(ntt):
            w_ = min(128, NT - 128 * t)
            ps = psr()
            vcols = wukv_v[:, 0, :].rearrange("p (h c) -> p h c", h=8)[:, :, 64:128]
            mm(ps.t[0:w_, 0:512].rearrange("p (h c) -> p h c", h=8), latb.t[:, 128 * t:128 * t + w_], vcols, True, True,
               r=[wukv_t, latb], w=[ps])
            cp(VAUG.t[0:w_, t, :].rearrange("p (h c) -> p h c", h=8)[:, :, 0:64],
               ps.t[0:w_, 0:512].rearrange("p (h c) -> p h c", h=8), r=[ps], w=[VAUG], eng="act")
        for t in range(ntt):
            w_ = min(128, NT - 128 * t)
            dma("sp", c_mv[l, seq, :, k0 + 128 * t:k0 + 128 * t + w_, :].rearrange("h p c -> p h c"),
                VAUG.t[0:w_, t, :].rearrange("p (h c) -> p h c", h=8), r=[VAUG], w=[CB(l, seq, kb)])

    def load_wukv(l):
        return wpiece(w_ukv[l], 0, 1, 0, 1024)

    def prep_mem_prompt():
        for t in range(2):
            xin = XIN.get()
            dma("sp", xin.t[:], memp[128 * t:128 * t + 128, :], r=[], w=[xin])
            for half in range(2):
                ps = psr()
                for c4 in range(4):
                    c = 4 * half + c4
                    tr(ps.t[:, 128 * c4:128 * c4 + 128], xin.t[:, 128 * c:128 * c + 128], IDF, r=[xin, CMf], w=[ps])
                cp(XT.t[:, 4 * half:4 * half + 4, 128 * t:128 * t + 128],
                   ps.t[:, :].rearrange("p (a b) -> p a b", a=4), r=[ps], w=[XT])
        import os as _os
        _sub = int(_os.environ.get("KSUB", "9"))
        if _sub < 1:
            return
        for l in range(2):
            norm_x(l, 24, 256)
            if _sub < 2:
                continue
            wt, wv = wpiece(w_ck[l], 0, 8, 0, 512)
            for h in range(4 if _sub in (3, 4, 9) else 0):
                ps = psr()
                proj_fm(ps, 0, 128, wt, wv, 128 * h, 256)
                sq = b1.get()
                act(sq.t[:, 0:256], ps.t[:, 0:256], AF.Square, r=[ps], w=[sq])
                ps2 = psr()
                mm(ps2.t[:, 0:256], ONES128, sq.t[:, 0:256], True, True, r=[sq, CMb], w=[ps2])
                rs = rstd_from(ps2.t[:, 0:256], ps2, 0, 128, 256)
                kf = f2.get()
                stt(kf.t[:, 0:256], ps.t[:, 0:256], gcol(l, 41), rs.t[:, 0:256], ALU.mult, ALU.mult,
                    r=[ps, rs, GT], w=[kf])
                kb_ = b1.get()
                cp(kb_.t[:, 0:256], kf.t[:, 0:256], r=[kf], w=[kb_])
                dma("sp", c_memk[l, 0, :, h, :], kb_.t[:, 0:256], r=[kb_], w=[membuf[(l, 0)]])
                ps3 = psr()
                for t in range(2):
                    tr(ps3.t[:, 128 * t:128 * t + 128], kf.t[:, 128 * t:128 * t + 128], IDF, r=[kf, CMf], w=[ps3])
                ko = f2.get()
                cp(ko.t[:, 0:256], ps3.t[:, 0:256], r=[ps3], w=[ko], eng="act")
                dma("sp", o_mk_p[l, :, 128 * h:128 * h + 128].rearrange("(t p) c -> p t c", p=128),
                    ko.t[:, 0:256].rearrange("p (t c) -> p t c", t=2), r=[ko], w=[])
            wt, wv = wpiece(w_cv[l], 0, 8, 0, 512)
            for t in range(2 if _sub >= 4 else 0):
                ps = psr()
                for kc in range(8):
                    mm(ps.t[:, 0:512], HT.t[:, kc, 128 * t:128 * t + 128], wv[:, kc, :], kc == 0, kc == 7,
                       r=[HT, wt], w=[ps])
                vo = f2.get()
                cp(vo.t[:, :], ps.t[:, :], r=[ps], w=[vo])
                dma("sp", o_mv_p[l, 128 * t:128 * t + 128, :], vo.t[:, :], r=[vo], w=[])
                vb = b1.get()
                cp(vb.t[:, :], vo.t[:, :], r=[vo], w=[vb], eng="act")
                dma("sp", c_memv[l, 0, 128 * t:128 * t + 128, :], vb.t[:, :], r=[vb], w=[membuf[(l, 0)]])

    def prep_sample_caches():
        for l in range(2):
            wukv_t, wukv_v = load_wukv(l)
            for j in range(NSS):
                seq = 1 + j
                dma("pool", c_sbv[l, seq, 0:PAST, :], csv[l, j], r=[], w=[CB(l, seq, kb) for kb in range(PAST // 512)])
                dma("pool", c_memv[l, seq], cmv[l, j], r=[], w=[membuf[(l, seq)]])
                for t in range(2):
                    xin = XIN.get()
                    dma("sp", xin.t[:, 0:512], cmk[l, j, 128 * t:128 * t + 128, :], r=[], w=[xin])
                    ps = psr()
                    for h in range(4):
                        tr(ps.t[:, 128 * h:128 * h + 128], xin.t[:, 128 * h:128 * h + 128], IDF, r=[xin, CMf], w=[ps])
                    kb_ = b1.get()
                    cp(kb_.t[:, :], ps.t[:, :], r=[ps], w=[kb_])
                    dma("sp", c_memk[l, seq, :, :, 128 * t:128 * t + 128],
                        kb_.t[:, :].rearrange("p (h k) -> p h k", h=4), r=[kb_], w=[membuf[(l, seq)]])
                for kb in range(PAST // 512):
                    k0 = 512 * kb
                    xin = XIN.get()
                    dma("sp", xin.t[:, :].rearrange("p (t c) -> p t c", t=4),
                        csk[l, j, k0:k0 + 512, :].rearrange("(t p) c -> p t c", p=128), r=[], w=[xin])
                    for hp in range(2):
                        ps = psr()
                        for t in range(4):
                            tr(ps.t[:, 128 * t:128 * t + 128], xin.t[:, 256 * t + 128 * hp:256 * t + 128 * hp + 128],
                               IDF, r=[xin, CMf], w=[ps])
                        kb_ = b1.get()
                        cp(kb_.t[:, :], ps.t[:, :], r=[ps], w=[kb_], eng=("act" if hp else "dve"))
                        for hh in range(2):
                            dma("sp", c_sbk[l, seq, :, 2 * hp + hh, k0:k0 + 512], kb_.t[64 * hh:64 * hh + 64, :],
                                r=[kb_], w=[CB(l, seq, kb)])
                    xin = XIN.get()
                    dma("sp", xin.t[:, 0:512].rearrange("p (t c) -> p t c", t=4),
                        clat[l, j, k0:k0 + 512, :].rearrange("(t p) c -> p t c", p=128), r=[], w=[xin])
                    dma("sp", xin.t[:, 512:640].rearrange("p (t c) -> p t c", t=4),
                        ckr[l, j, k0:k0 + 512, :].rearrange("(t p) c -> p t c", p=128), r=[], w=[xin])
                    ps = psr()
                    for t in range(4):
                        tr(ps.t[:, 128 * t:128 * t + 128], xin.t[:, 128 * t:128 * t + 128], IDF, r=[xin, CMf], w=[ps])
                    cp(LATB.t[:, :], ps.t[:, :], r=[ps], w=[LATB])
                    ps = psr()
                    for t in range(4):
                        tr(ps.t[0:32, 128 * t:128 * t + 128], xin.t[:, 512 + 32 * t:512 + 32 * t + 32], IDF,
                           r=[xin, CMf], w=[ps])
                    kr32 = b1.get()
                    cp(kr32.t[0:32, :], ps.t[0:32, :], r=[ps], w=[kr32], eng="act")
                    dma("sp", KRB.t[64:96, :], kr32.t[0:32, :], r=[kr32], w=[KRB])
                    mla_kside(l, seq, k0, 512, LATB, KRB, wukv_t, wukv_v)

    MARKS = []

    def mark(lbl):
        MARKS.append((lbl, S.eng["pe"]["n"], S.eng["act"]["n"], S.eng["dve"]["n"], S.eng["sp"]["n"]))
    nc._marks = MARKS

    def group(kind, g):
        mark("group_%s%d" % (kind, g))
        if kind == "p":
            NT, NTT = 512, 4
            xsrc, ydst = xp, yp
            t0 = 512 * g
            pos0 = t0
            segs = [(0, 0, 512)]
        else:
            NT, NTT = NTS, NTS // 128
            xsrc, ydst = xs, ys
            t0 = 0
            pos0 = PAST
        for t in range(NTT):
            xin = XIN.get()
            dma("sp", xin.t[:], xsrc[t0 + 128 * t:t0 + 128 * t + 128, :], r=[], w=[xin])
            for half in range(2):
                ps = psr()
                for c4 in range(4):
                    c = 4 * half + c4
                    tr(ps.t[:, 128 * c4:128 * c4 + 128], xin.t[:, 128 * c:128 * c + 128], IDF, r=[xin, CMf], w=[ps])
                cp(XT.t[:, 4 * half:4 * half + 4, 128 * t:128 * t + 128],
                   ps.t[:, :].rearrange("p (a b) -> p a b", a=4), r=[ps], w=[XT], eng=("act" if half else "dve"))
        if kind == "p":
            dma("sp", COS.t[:, 0:NT], ropec[:, pos0:pos0 + NT], r=[], w=[COS])
            dma("sp", SIN.t[:, 0:NT], ropes[:, pos0:pos0 + NT], r=[], w=[SIN])
        else:
            for j in range(NSS):
                dma("sp", COS.t[:, 64 * j:64 * j + 64], ropec[:, PAST:PAST + 64], r=[], w=[COS])
                dma("sp", SIN.t[:, 64 * j:64 * j + 64], ropes[:, PAST:PAST + 64], r=[], w=[SIN])

        for l in range(2):
            layer(kind, g, l, NT, NTT, t0)

        for t in range(NTT):
            ot = OUTT.get()
            for half in range(2):
                ps = psr()
                for c4 in range(4):
                    c = 4 * half + c4
                    tr(ps.t[:, 128 * c4:128 * c4 + 128], XT.t[:, c, 128 * t:128 * t + 128], IDF, r=[XT, CMf], w=[ps])
                cp(ot.t[:, 512 * half:512 * half + 512], ps.t[:, :], r=[ps], w=[ot], eng=("act" if half else "dve"))
            dma("sp", ydst[t0 + 128 * t:t0 + 128 * t + 128, :], ot.t[:], r=[ot], w=[])

    def layer(kind, g, l, NT, NTT, t0):
        isp = kind == "p"
        if isp:
            o_sbk, o_sbv, o_lat, o_kr = o_sbk_p, o_sbv_p, o_lat_p, o_kr_p
            k0new = 512 * g
            seqs = [0]
        else:
            o_sbk, o_sbv, o_lat, o_kr = o_sbk_s, o_sbv_s, o_lat_s, o_kr_s
            k0new = PAST
            seqs = list(range(1, 1 + NSS))
        mark("%s%d_l%d_A" % (kind, g, l))
        norm_x(l, 0, NT)
        wt1, wv1 = wpiece(w_in[l], 0, 8, 0, 512)
        for h in range(4):
            ps = psr()
            proj_fm(ps, 0, 64, wt1, wv1, 64 * h, NT)
            cp(QA.t[:, h, 0:NT], ps.t[0:64, 0:NT], r=[ps], w=[QA], eng=("act" if h % 2 else "dve"))
        for h in range(4):
            ps = psr()
            proj_fm(ps, 0, 64, wt1, wv1, 256 + 64 * h, NT)
            cp(KA.t[:, h, 0:NT], ps.t[0:64, 0:NT], r=[ps], w=[KA], eng=("act" if h % 2 else "dve"))
        wt2, wv2 = wpiece(w_in[l], 0, 8, 512, 512)
        for t in range(NTT):
            psk = psr()
            for kc in range(8):
                mm(psk.t[:, 0:256], HT.t[:, kc, 128 * t:128 * t + 128], wv1[:, kc, 256:512], kc == 0, kc == 7,
                   r=[HT, wt1], w=[psk])
            psv = psr()
            for kc in range(8):
                mm(psv.t[:, 0:256], HT.t[:, kc, 128 * t:128 * t + 128], wv2[:, kc, 0:256], kc == 0, kc == 7,
                   r=[HT, wt2], w=[psv])
            kv = f2.get()
            cp(kv.t[:, 0:256], psk.t[:, 0:256], r=[psk], w=[kv])
            cp(kv.t[:, 256:512], psv.t[:, 0:256], r=[psv], w=[kv], eng="act")
            cp(VBF.t[:, t, :], kv.t[:, 256:512], r=[kv], w=[VBF])
            dma("sp", o_sbk[l, t0 + 128 * t:t0 + 128 * t + 128, :], kv.t[:, 0:256], r=[kv], w=[])
            dma("sp", o_sbv[l, t0 + 128 * t:t0 + 128 * t + 128, :], kv.t[:, 256:512], r=[kv], w=[])
        if isp:
            dma("sp", c_sbk[l, 0, :, :, k0new:k0new + 512], KA.t[:, :, :], r=[KA], w=[CB(l, 0, g)])
            dma("sp", c_sbv[l, 0, k0new:k0new + 512, :].rearrange("(t p) c -> p t c", p=128), VBF.t[:, :, :],
                r=[VBF], w=[CB(l, 0, g)])
        else:
            for j in range(NSS):
                dma("sp", c_sbk[l, 1 + j, :, :, PAST:PAST + 64], KA.t[:, :, 64 * j:64 * j + 64], r=[KA],
                    w=[CB(l, 1 + j, PAST // 512)])
                tt_, po = (64 * j) // 128, (64 * j) % 128
                dma("sp", c_sbv[l, 1 + j, PAST:PAST + 64, :], VBF.t[po:po + 64, tt_, :], r=[VBF],
                    w=[CB(l, 1 + j, PAST // 512)])
        ps = psr()
        proj_fm(ps, 0, 128, wt2, wv2, 256, NT)
        cp(QG.t[:, 0:NT], ps.t[:, 0:NT], r=[ps], w=[QG])
        ps = psr()
        proj_fm(ps, 0, 128, wt2, wv2, 384, NT)
        cp(KG.t[:, 0:NT], ps.t[:, 0:NT], r=[ps], w=[KG], eng="act")
        wt3, wv3 = wpiece(w_in[l], 0, 8, 1024, 528)
        for t in range(NTT):
            ps = psr()
            for kc in range(8):
                mm(ps.t[:, 0:256], HT.t[:, kc, 128 * t:128 * t + 128], wv3[:, kc, 0:256], kc == 0, kc == 7,
                   r=[HT, wt3], w=[ps])
            cp(VG.t[:, t, :], ps.t[:, 0:256], r=[ps], w=[VG], eng=("act" if t % 2 else "dve"))
        ps = psr()
        proj_fm(ps, 0, 16, wt3, wv3, 256, NT)
        cp(AG.t[:, 0:NT], ps.t[0:16, 0:NT], r=[ps], w=[AG])
        for c in range(2):
            ps = psr()
            proj_fm(ps, 0, 128, wt3, wv3, 272 + 128 * c, NT)
            act(SRG.t[:, c, 0:NT], ps.t[:, 0:NT], AF.Silu, r=[ps], w=[SRG])
        wt4, wv4 = wpiece(w_in[l], 0, 8, 1552, 416)
        psq = psr()
        for c in range(2):
            ps = psr()
            proj_fm(ps, 0, 128, wt4, wv4, 128 * c, NT)
            cp(CQ.t[:, c, 0:NT], ps.t[:, 0:NT], r=[ps], w=[CQ])
            sq = b1.get()
            act(sq.t[:, 0:NT], CQ.t[:, c, 0:NT], AF.Square, r=[CQ], w=[sq])
            mm(psq.t[:, 0:NT], ONES256, sq.t[:, 0:NT], c == 0, c == 1, r=[sq, CMb], w=[psq])
        rs = rstd_from(psq.t[:, 0:NT], psq, 0, 128, NT)
        for c in range(2):
            stt(CQN.t[:, c, 0:NT], CQ.t[:, c, 0:NT], gcol(l, 32 + c), rs.t[:, 0:NT], ALU.mult, ALU.mult,
                r=[CQ, rs, GT], w=[CQN])
        ps = psr()
        proj_fm(ps, 0, 128, wt4, wv4, 256, NT)
        sq = b1.get()
        act(sq.t[:, 0:NT], ps.t[:, 0:NT], AF.Square, r=[ps], w=[sq])
        ps2 = psr()
        mm(ps2.t[:, 0:NT], ONES128, sq.t[:, 0:NT], True, True, r=[sq, CMb], w=[ps2])
        rs = rstd_from(ps2.t[:, 0:NT], ps2, 0, 128, NT)
        stt(LAT.t[:, 0:NT], ps.t[:, 0:NT], gcol(l, 34), rs.t[:, 0:NT], ALU.mult, ALU.mult, r=[ps, rs, GT], w=[LAT])
        cp(LATB.t[:, 0:NT], LAT.t[:, 0:NT], r=[LAT], w=[LATB], eng="act")
        ps3 = psr()
        for t in range(NTT):
            tr(ps3.t[:, 128 * t:128 * t + 128], LAT.t[:, 128 * t:128 * t + 128], IDF, r=[LAT, CMf], w=[ps3])
        lo = f2.get()
        cp(lo.t[:, 0:NT], ps3.t[:, 0:NT], r=[ps3], w=[lo])
        dma("sp", o_lat[l, t0:t0 + NT, :].rearrange("(t p) c -> p t c", p=128),
            lo.t[:, 0:NT].rearrange("p (t c) -> p t c", t=NTT), r=[lo], w=[])
        ps = psr()
        proj_fm(ps, 64, 32, wt4, wv4, 384, NT)
        sq = b1.get()
        act(sq.t[64:96, 0:NT], ps.t[64:96, 0:NT], AF.Square, r=[ps], w=[sq])
        ps2 = psr()
        mm(ps2.t[64:96, 0:NT], BD96[64:96, 64:96], sq.t[64:96, 0:NT], True, True, r=[sq, CMb], w=[ps2])
        rs = rstd_from(ps2.t[64:96, 0:NT], ps2, 64, 96, NT)
        stt(KRN.t[64:96, 0:NT], ps.t[64:96, 0:NT], gcol(l, 36, 64, 96), rs.t[64:96, 0:NT], ALU.mult, ALU.mult,
            r=[ps, rs, GT], w=[KRN])
        rot = ROT.get()
        dma("sp", rot.t[64:80, 0:NT], KRN.t[80:96, 0:NT], r=[KRN], w=[rot])
        dma("sp", rot.t[80:96, 0:NT], KRN.t[64:80, 0:NT], r=[KRN], w=[rot])
        t1 = f2.get()
        tt(t1.t[64:96, 0:NT], KRN.t[64:96, 0:NT], COS.t[64:96, 0:NT], ALU.mult, r=[KRN, COS], w=[t1])
        t2 = f2.get()
        tt(t2.t[64:96, 0:NT], rot.t[64:96, 0:NT], SIN.t[64:96, 0:NT], ALU.mult, r=[rot, SIN], w=[t2])
        tt(KRN.t[64:96, 0:NT], t1.t[64:96, 0:NT], t2.t[64:96, 0:NT], ALU.add, r=[t1, t2], w=[KRN])
        cp(KRB.t[64:96, 0:NT], KRN.t[64:96, 0:NT], r=[KRN], w=[KRB], eng="act")
        kr0 = f2.get()
        dma("sp", kr0.t[0:32, 0:NT], KRN.t[64:96, 0:NT], r=[KRN], w=[kr0])
        ps3 = psr()
        for t in range(NTT):
            tr(ps3.t[:, 32 * t:32 * t + 32], kr0.t[0:32, 128 * t:128 * t + 128], IDF[0:32, 0:32], r=[kr0, CMf], w=[ps3])
        ko = f2.get()
        cp(ko.t[:, 0:32 * NTT], ps3.t[:, 0:32 * NTT], r=[ps3], w=[ko])
        dma("sp", o_kr[l, t0:t0 + NT, :].rearrange("(t p) c -> p t c", p=128),
            ko.t[:, 0:32 * NTT].rearrange("p (t c) -> p t c", t=NTT), r=[ko], w=[])
        mark("%s%d_l%d_mlaq" % (kind, g, l))
        wtq, wvq = wpiece(w_uq[l], 0, 2, 0, 768)
        def qhead(h):
            ps = psr()
            proj_fm(ps, 0, 96, wtq, wvq, 96 * h, NT, KC=2, rhs=CQN)
            yield
            sq = b1.get()
            act(sq.t[0:96, 0:NT], ps.t[0:96, 0:NT], AF.Square, r=[ps], w=[sq])
            yield
            ps2 = psr()
            mm(ps2.t[0:96, 0:NT], BD96[0:96, 0:96], sq.t[0:96, 0:NT], True, True, r=[sq, CMb], w=[ps2])
            yield
            rs = f2.get()
            act(rs.t[0:96, 0:NT], ps2.t[0:96, 0:NT], AF.Ln, r=[ps2], w=[rs], bias=EPSC[0:96, 0:1])
            yield
            act(rs.t[0:96, 0:NT], rs.t[0:96, 0:NT], AF.Exp, r=[rs], w=[rs], scale=-0.5)
            yield
            stt(QTm.t[0:64, h, 0:NT], ps.t[0:64, 0:NT], gcol(l, 35, 0, 64), rs.t[0:64, 0:NT], ALU.mult, ALU.mult,
                r=[ps, rs, GT], w=[QTm])
            qn = f2.get()
            stt(qn.t[64:96, 0:NT], ps.t[64:96, 0:NT], gcol(l, 35, 64, 96), rs.t[64:96, 0:NT], ALU.mult, ALU.mult,
                r=[ps, rs, GT], w=[qn])
            yield
            rot = ROT.get()
            dma("sp", rot.t[64:80, 0:NT], qn.t[80:96, 0:NT], r=[qn], w=[rot])
            dma("sp", rot.t[80:96, 0:NT], qn.t[64:80, 0:NT], r=[qn], w=[rot])
            yield
            t1 = f2.get()
            tt(t1.t[64:96, 0:NT], qn.t[64:96, 0:NT], COS.t[64:96, 0:NT], ALU.mult, r=[qn, COS], w=[t1])
            t2 = f2.get()
            tt(t2.t[64:96, 0:NT], rot.t[64:96, 0:NT], SIN.t[64:96, 0:NT], ALU.mult, r=[rot, SIN], w=[t2])
            tt(QTm.t[64:96, h, 0:NT], t1.t[64:96, 0:NT], t2.t[64:96, 0:NT], ALU.add, r=[t1, t2], w=[QTm])

        run_interleaved((qhead(h) for h in range(8)), 2, 4)
        wukv_t, wukv_v = load_wukv(l)
        if isp:
            mla_kside(l, 0, k0new, 512, LATB, KRB, wukv_t, wukv_v)
        else:
            for j in range(NSS):
                mla_kside_cols(l, 1 + j, PAST, 64 * j, 64, wukv_t, wukv_v)
        import os as _os
        _cut = int(_os.environ.get("KCUT", "99"))
        if _cut <= 1:
            return
        mark("%s%d_l%d_SB" % (kind, g, l))
        sb_attention(kind, g, l, NT)
        if _cut <= 2:
            return
        mark("%s%d_l%d_GLA" % (kind, g, l))
        gla(kind, g, l, NT, NTT)
        if _cut <= 3:
            return
        mark("%s%d_l%d_MLA" % (kind, g, l))
        mla_attention(kind, g, l, NT)
        mark("%s%d_l%d_Wout" % (kind, g, l))
        if _cut <= 4:
            return
        for half in range(2):
            wta, wva = wpiece(w_out[l], 0, 4, 512 * half, 512)
            wtm, wvm = wpiece(w_out[l], 512, 8, 512 * half, 512, p=64)
            for o4 in range(4):
                oc = 4 * half + o4
                ps = psr()
                cols = slice(128 * o4, 128 * o4 + 128)
                for c in range(2):
                    mm(ps.t[:, 0:NT], wva[:, c, cols], OAT.t[:, c, 0:NT], c == 0, False, r=[wta, OAT], w=[ps])
                for c in range(2):
                    mm(ps.t[:, 0:NT], wva[:, 2 + c, cols], OBB.t[:, c, 0:NT], False, False, r=[wta, OBB], w=[ps])
                for h in range(8):
                    mm(ps.t[:, 0:NT], wvm[:, h, cols], OCT.t[:, h, 0:NT], False, h == 7, r=[wtm, OCT], w=[ps])
                tt(XT.t[:, oc, 0:NT], ps.t[:, 0:NT], XT.t[:, oc, 0:NT], ALU.add, r=[ps, XT], w=[XT])
        mark("%s%d_l%d_cross" % (kind, g, l))
        norm_x(l, 8, NT)
        wt, wv = wpiece(w_cq[l], 0, 8, 0, 512)
        for h in range(4):
            ps = psr()
            proj_fm(ps, 0, 128, wt, wv, 128 * h, NT)
            sq = b1.get()
            act(sq.t[:, 0:NT], ps.t[:, 0:NT], AF.Square, r=[ps], w=[sq])
            ps2 = psr()
            mm(ps2.t[:, 0:NT], ONES128, sq.t[:, 0:NT], True, True, r=[sq, CMb], w=[ps2])
            rs = rstd_from(ps2.t[:, 0:NT], ps2, 0, 128, NT)
            stt(QCT.t[:, h, 0:NT], ps.t[:, 0:NT], gcol(l, 40), rs.t[:, 0:NT], ALU.mult, ALU.mult,
                r=[ps, rs, GT], w=[QCT])
        csegs = [(0, 0, NT)] if isp else [(1 + j, 64 * j, 64) for j in range(NSS)]
        for (seq, c0, ncl) in csegs:
            mk = MEMK.get()
            mv = MEMV.get()
            dma("sp", mk.t[:], c_memk[l, seq], r=[membuf[(l, seq)]], w=[mk])
            dma("sp", mv.t[:], c_memv[l, seq].rearrange("(t p) c -> p t c", p=128), r=[membuf[(l, seq)]], w=[mv])
            for h in range(4):
                po = psl()
                pd = psr()
                for t in range(2):
                    ps = psr()
                    mm(ps.t[:, 0:ncl], mk.t[:, h, 128 * t:128 * t + 128], QCT.t[:, h, c0:c0 + ncl], True, True,
                       r=[mk, QCT], w=[ps])
                    pt = b1.get()
                    act(pt.t[:, 0:ncl], ps.t[:, 0:ncl], AF.Exp, r=[ps], w=[pt], scale=128.0 ** -0.5)
                    mm(po.t[:, 0:ncl], mv.t[:, t, 128 * h:128 * h + 128], pt.t[:, 0:ncl], t == 0, t == 1,
                       r=[mv, pt], w=[po])
                    mm(pd.t[:, 0:ncl], cmb(10), pt.t[:, 0:ncl], t == 0, t == 1, r=[CMb, pt], w=[pd])
                rd = f2.get()
                S.op("dve", lambda e, o=rd.t[:, 0:ncl], i=pd.t[:, 0:ncl]: e.reciprocal(o, i), r=[pd], w=[rd])
                tt(OCR.t[:, h, c0:c0 + ncl], po.t[:, 0:ncl], rd.t[:, 0:ncl], ALU.mult, r=[po, rd], w=[OCR])
        wt, wv = wpiece(w_co[l], 0, 4, 0, 1024)
        for oc in range(8):
            ps = psr()
            for h in range(4):
                mm(ps.t[:, 0:NT], wv[:, h, 128 * oc:128 * oc + 128], OCR.t[:, h, 0:NT], h == 0, h == 3,
                   r=[wt, OCR], w=[ps])
            tt(XT.t[:, oc, 0:NT], ps.t[:, 0:NT], XT.t[:, oc, 0:NT], ALU.add, r=[ps, XT], w=[XT])
        mark("%s%d_l%d_FFN" % (kind, g, l))
        norm_x(l, 16, NT)
        for pc in range(6):
            nf = 4 if pc < 5 else 2
            wtg, wvg = wpiece(w_gate[l], 0, 8, 512 * pc, 128 * nf)
            wtu, wvu = wpiece(w_up[l], 0, 8, 512 * pc, 128 * nf)
            for fi in range(nf):
                f = 4 * pc + fi
                pg = psr()
                proj_fm(pg, 0, 128, wtg, wvg, 128 * fi, NT)
                pu = psr()
                proj_fm(pu, 0, 128, wtu, wvu, 128 * fi, NT)
                sg = f2.get()
                act(sg.t[:, 0:NT], pg.t[:, 0:NT], AF.Silu, r=[pg], w=[sg])
                tt(AT.t[:, f, 0:NT], pu.t[:, 0:NT], sg.t[:, 0:NT], ALU.mult, r=[pu, sg], w=[AT])
        for oc in range(8):
            wt, wv = wpiece(w_down[l], 0, 22, 128 * oc, 128)
            ps = psr()
            for f in range(22):
                mm(ps.t[:, 0:NT], wv[:, f, :], AT.t[:, f, 0:NT], f == 0, f == 21, r=[wt, AT], w=[ps])
            tt(XT.t[:, oc, 0:NT], ps.t[:, 0:NT], XT.t[:, oc, 0:NT], ALU.add, r=[ps, XT], w=[XT])

    def mla_kside_cols(l, seq, k0, c0, n, wukv_t, wukv_v):
        lb = LATC
        cp(lb.t[:, 0:n], LATB.t[:, c0:c0 + n], r=[LATB], w=[lb])
        kb2 = KRC
        cp(kb2.t[64:96, 0:n], KRB.t[64:96, c0:c0 + n], r=[KRB], w=[kb2], eng="act")
        mla_kside(l, seq, k0, n, lb, kb2, wukv_t, wukv_v)

    def sb_attention(kind, g, l, NT):
        isp = kind == "p"
        if isp:
            units = [(0, 128 * j, 128, j) for j in range(4)]
            nkb = g + 1
        else:
            units = [(1 + j, 64 * j, 64, j) for j in range(NSS)]
            nkb = NKBS
        memset(NEGC.t[:], 0.0, w=[NEGC] + NEGCB)
        seq_list = [0] if isp else list(range(1, 1 + NSS))
        zpi = [0, 0]
        ZPOOL = [PSR[0], PSR[1], PSR[2], PSL[0]]
        TPOOL = [PSR[3], PSR[4], PSR[5], PSL[1]]

        def unit(kbk, vbk, qc0, nq, slot, h, N, diag, last):
            col = 4 * slot + h
            ncb = NEGCB[col]
            zpi[0] += 1
            ps = ZPOOL[zpi[0] % 4]
            mm(ps.t[0:nq, 0:N], QA.t[:, h, qc0:qc0 + nq], kbk.t[:, h, 0:N], True, True, r=[QA, kbk], w=[ps])
            yield
            e1 = f2.get()
            act(e1.t[0:nq, 0:N], ps.t[0:nq, 0:N], AF.Exp, r=[ps], w=[e1], scale=0.125)
            yield
            act(e1.t[0:nq, 0:N], e1.t[0:nq, 0:N], AF.Ln, r=[e1], w=[e1], bias=ONESF.t[0:nq, 0:1])
            yield
            if diag:
                tt(e1.t[0:nq, N - nq:N], e1.t[0:nq, N - nq:N], MASKL01[0:nq, 0:nq], ALU.mult, r=[e1, CMf], w=[e1])
            ft = FT.get()
            S.op("dve", lambda e, o=ft.t[0:nq, 1:N + 1], d0=ONESF.t[0:nq, 0:N], d1=e1.t[0:nq, 0:N]:
                 e.tensor_tensor_scan(out=o, data0=d0, data1=d1, initial=0.0, op0=ALU.mult, op1=ALU.add),
                 r=[e1, ONESF], w=[ft])
            tt(NEGC.t[0:nq, col:col + 1], NEGC.t[0:nq, col:col + 1], ft.t[0:nq, N:N + 1], ALU.subtract,
               r=[ncb, ft], w=[ncb])
            yield
            stt(e1.t[0:nq, 0:N], ps.t[0:nq, 0:N], 0.125, ft.t[0:nq, 0:N], ALU.mult, ALU.add, r=[ps, ft], w=[e1])
            if diag:
                tt(e1.t[0:nq, N - nq:N], e1.t[0:nq, N - nq:N], MASKLNEG[0:nq, 0:nq], ALU.add, r=[e1, CMf], w=[e1])
            yield
            act(e1.t[0:nq, 0:N], e1.t[0:nq, 0:N], AF.Exp, r=[e1, ncb], w=[e1], bias=NEGC.t[0:nq, col:col + 1])
            yield
            nsub = (N + 127) // 128
            zpi[1] += 1
            pst = TPOOL[zpi[1] % 4]
            for i in range(nsub):
                w_ = min(128, N - 128 * i)
                tr(pst.t[0:w_, 128 * i:128 * i + nq], e1.t[0:nq, 128 * i:128 * i + w_], IDF[0:nq, 0:nq],
                   r=[e1, CMf], w=[pst])
            yield
            wT = b1.get()
            wmax = min(128, N)
            if nq == 128:
                cp(wT.t[0:wmax, 0:128 * nsub], pst.t[0:wmax, 0:128 * nsub], r=[pst], w=[wT], eng="act")
            else:
                cp(wT.t[0:wmax, 0:128 * nsub].rearrange("p (a b) -> p a b", a=nsub)[:, :, 0:nq],
                   pst.t[0:wmax, 0:128 * nsub].rearrange("p (a b) -> p a b", a=nsub)[:, :, 0:nq], r=[pst], w=[wT],
                   eng="act")
            yield
            po = ps
            for i in range(nsub):
                w_ = min(128, N - 128 * i)
                mm(po.t[0:nq, 0:64], wT.t[0:w_, 128 * i:128 * i + nq], vbk.t[0:w_, i, 64 * h:64 * h + 64],
                   i == 0, i == nsub - 1, r=[wT, vbk], w=[po])
            yield
            if last:
                cp(OSB.t[0:nq, slot, 64 * h:64 * h + 64], po.t[0:nq, 0:64], r=[po], w=[OSB])
            else:
                tt(OSB.t[0:nq, slot, 64 * h:64 * h + 64], po.t[0:nq, 0:64],
                   OSB.t[0:nq, slot, 64 * h:64 * h + 64], ALU.add, r=[po, OSB], w=[OSB])

        def all_units():
            for seq in seq_list:
                us = [u for u in units if u[0] == seq]
                for kbi in range(nkb - 1, -1, -1):
                    kbk = KBK.get()
                    vbk = VBK.get()
                    last = kbi == nkb - 1
                    nkeys = 512 if (isp or not last) else 64
                    dma(LQ, kbk.t[:, :, 0:nkeys], c_sbk[l, seq, :, :, 512 * kbi:512 * kbi + nkeys],
                        r=[CB(l, seq, kbi)], w=[kbk])
                    if nkeys == 512:
                        dma(LQ, vbk.t[:], c_sbv[l, seq, 512 * kbi:512 * kbi + 512, :].rearrange(
                            "(t p) c -> p t c", p=128), r=[CB(l, seq, kbi)], w=[vbk])
                    else:
                        dma(LQ, vbk.t[0:64, 0, :], c_sbv[l, seq, 512 * kbi:512 * kbi + 64, :],
                            r=[CB(l, seq, kbi)], w=[vbk])
                    for (sq_, qc0, nq, slot) in us:
                        N = 128 * (slot + 1) if (isp and last) else nkeys
                        for h in range(4):
                            yield unit(kbk, vbk, qc0, nq, slot, h, N, last, last)

        run_interleaved(all_units(), SBW, SBG)
        for (sq_, qc0, nq, slot) in units:
            sqt = f2.get()
            act(sqt.t[0:nq, 0:256], OSB.t[0:nq, slot, :], AF.Square, r=[OSB], w=[sqt])
            ms = SMALL.get()
            S.op("dve", lambda e, o=ms.t[0:nq, 0:4], i=sqt.t[0:nq, 0:256].rearrange("p (h c) -> p h c", h=4):
                 e.tensor_reduce(out=o, in_=i, axis=AX.X, op=ALU.add), r=[sqt], w=[ms])
            l1 = SMALL.get()
            act(l1.t[0:nq, 0:4], ms.t[0:nq, 0:4], AF.Ln, r=[ms], w=[l1], bias=EPSC[0:nq, 0:1], scale=1.0 / 64)
            r1 = SMALL.get()
            act(r1.t[0:nq, 0:4], l1.t[0:nq, 0:4], AF.Exp, r=[l1], w=[r1], scale=-0.5)
            on = f2.get()
            for h in range(4):
                ts(on.t[0:nq, 64 * h:64 * h + 64], OSB.t[0:nq, slot, 64 * h:64 * h + 64], r1.t[0:nq, h:h + 1], None,
                   ALU.mult, None, r=[OSB, r1], w=[on])
            for c in range(2):
                ps = psr()
                tr(ps.t[:, 0:nq], on.t[0:nq, 128 * c:128 * c + 128], IDF[0:nq, 0:nq], r=[on, CMf], w=[ps])
                ts(OAT.t[:, c, qc0:qc0 + nq], ps.t[:, 0:nq], gcol(l, 37), None, ALU.mult, None, r=[ps, GT], w=[OAT])

    def gla(kind, g, l, NT, NTT):
        isp = kind == "p"
        C = 128 if isp else 64
        nch = NT // C
        ps = psr()
        mm(ps.t[:, 0:NT], WGG.t[:, l, :], AG.t[:, 0:NT], True, True, r=[WGG, AG], w=[ps])
        e1 = f2.get()
        act(e1.t[:, 0:NT], ps.t[:, 0:NT], AF.Exp, r=[ps, NEGB], w=[e1], scale=-1.0, bias=NEGB.t[:, l:l + 1])
        act(SPG.t[:, 0:NT], e1.t[:, 0:NT], AF.Ln, r=[e1], w=[SPG], bias=ONESF.t[:, 0:1])
        for c in range(nch):
            S.op("dve", lambda e, o=CS.t[:, C * c:C * c + C], d0=ONESF.t[:, 0:C], d1=SPG.t[:, C * c:C * c + C]:
                 e.tensor_tensor_scan(out=o, data0=d0, data1=d1, initial=0.0, op0=ALU.mult, op1=ALU.add),
                 r=[SPG, ONESF], w=[CS])
            ts(NBL.t[:, c:c + 1], CS.t[:, C * c + C - 1:C * c + C], -1.0 / 16, None, ALU.mult, None, r=[CS], w=[NBL])
        act(EBt.t[:, 0:NT], CS.t[:, 0:NT], AF.Exp, r=[CS], w=[EBt], scale=-1.0 / 16)
        ebi = f2.get()
        act(ebi.t[:, 0:NT], CS.t[:, 0:NT], AF.Exp, r=[CS], w=[ebi], scale=1.0 / 16)
        stt(QTg.t[:, 0:NT], QG.t[:, 0:NT], 32.0 ** -0.5, EBt.t[:, 0:NT], ALU.mult, ALU.mult, r=[QG, EBt], w=[QTg])
        tt(KTg.t[:, 0:NT], KG.t[:, 0:NT], ebi.t[:, 0:NT], ALU.mult, r=[KG, ebi], w=[KTg])
        KTm = Tl(AT.t[:, 0:4, :], AT.b)
        for h in range(4):
            ts(KTm.t[:, h, 0:NT], KTg.t[:, 0:NT], GT.t[:, 44 + h:45 + h], None, ALU.mult, None, r=[KTg, GT], w=[KTm])
        ekd = f2.get()
        for c in range(nch):
            act(ekd.t[:, C * c:C * c + C], CS.t[:, C * c:C * c + C], AF.Exp, r=[CS, NBL], w=[ekd], scale=1.0 / 16,
                bias=NBL.t[:, c:c + 1])
        tt(KDg.t[:, 0:NT], KG.t[:, 0:NT], ekd.t[:, 0:NT], ALU.mult, r=[KG, ekd], w=[KDg])
        st = SALL[l] if isp else SS
        if isp and g == 0:
            memset(st.t[:], 0.0, w=[st])
        BMASK = CMf.t[:, 11 * 128:13 * 128]
        for c in range(nch):
            if not isp:
                memset(st.t[:], 0.0, w=[st])
                for h in range(4):
                    dma("sp", st.t[32 * h:32 * h + 32, 64 * h:64 * h + 64], sgla[l, c, h], r=[], w=[st])
            tt(SBF16.t[:], st.t[:], BMASK, ALU.mult, r=[st, CMf], w=[SBF16])
            cs_ = slice(C * c, C * c + C)
            pk = psr()
            tr(pk.t[0:C, 0:128], KDg.t[:, cs_], IDF, r=[KDg, CMf], w=[pk])
            kdt = KDt.get()
            cp(kdt.t[0:C, :], pk.t[0:C, 0:128], r=[pk], w=[kdt])
            tt_, po_ = (C * c) // 128, (C * c) % 128
            if isp:
                vsrc, vt = VG, VG.t[:, tt_, :]
            else:
                dma("sp", VGLO.t[0:C, :], VG.t[po_:po_ + C, tt_, :], r=[VG], w=[VGLO])
                vsrc, vt = VGLO, VGLO.t[:, :]
            for h in range(4):
                pa = psr()
                mm(pa.t[0:C, 0:C], KTm.t[:, h, cs_], QTg.t[:, cs_], True, True, r=[KTm, QTg], w=[pa])
                am = ATM.get()
                tt(am.t[0:C, 0:C], pa.t[0:C, 0:C], MASKU01[0:C, 0:C], ALU.mult, r=[pa, CMf], w=[am])
                po = psr()
                pr = slice(64 * (h % 2), 64 * (h % 2) + 64)
                mm(po.t[pr, 0:C], vt[:, 64 * h:64 * h + 64], am.t[:, 0:C], True, False, r=[vsrc, am], w=[po])
                mm(po.t[pr, 0:C], SBF16.t[:, 64 * h:64 * h + 64], QTg.t[:, cs_], False, True, r=[SBF16, QTg], w=[po])
                cp(OBT.t[pr, h // 2, cs_], po.t[pr, 0:C], r=[po], w=[OBT], eng=("act" if h % 2 else "dve"))
            pu = psr()
            mm(pu.t[:, 0:256], kdt.t[:, :], vt, True, True, r=[kdt, vsrc], w=[pu])
            stt(st.t[:], st.t[:], EBt.t[:, C * c + C - 1:C * c + C], pu.t[:, 0:256], ALU.mult, ALU.add,
                r=[st, EBt, pu], w=[st])
            if not isp:
                for h in range(4):
                    dma("sp", o_gla_s[l, c, h], st.t[32 * h:32 * h + 32, 64 * h:64 * h + 64], r=[st], w=[])
        if isp and g == NG - 1:
            for h in range(4):
                dma("sp", o_gla_p[l, h], st.t[32 * h:32 * h + 32, 64 * h:64 * h + 64], r=[st], w=[])
        for c2 in range(2):
            sq = b1.get()
            act(sq.t[:, 0:NT], OBT.t[:, c2, 0:NT], AF.Square, r=[OBT], w=[sq])
            ps2 = psr()
            mm(ps2.t[:, 0:NT], BD64, sq.t[:, 0:NT], True, True, r=[sq, CMb], w=[ps2])
            rs = rstd_from(ps2.t[:, 0:NT], ps2, 0, 128, NT)
            tmp = f2.get()
            stt(tmp.t[:, 0:NT], OBT.t[:, c2, 0:NT], gcol(l, 38), rs.t[:, 0:NT], ALU.mult, ALU.mult,
                r=[OBT, rs, GT], w=[tmp])
            tt(OBB.t[:, c2, 0:NT], tmp.t[:, 0:NT], SRG.t[:, c2, 0:NT], ALU.mult, r=[tmp, SRG], w=[OBB])

    VGLO = S.sb([128, 256], BF, "VGLO")
    memset(VGLO.t[:], 0.0, w=[VGLO])
    for _t in KDt.tiles + ATM.tiles:
        memset(_t.t[:], 0.0, w=[_t])
    LATC = S.sb([128, 64], BF, "LATC")
    KRC = S.sb([96, 64], BF, "KRC")

    def mla_attention(kind, g, l, NT):
        isp = kind == "p"
        SC = 96.0 ** -0.5
        segs = [(0, 0, 512, g + 1)] if isp else [(1 + j, 64 * j, 64, NKBS) for j in range(NSS)]
        for (seq, c0, ncl, nkb) in segs:
            for h in range(8):
                po = psl()

                def step(kb_, vb_, i, w_, first, final, dmask):
                    ps = psr()
                    mm(ps.t[0:w_, 0:ncl], kb_.t[:, 128 * i:128 * i + w_], QTm.t[:, h, c0:c0 + ncl], True, True,
                       r=[kb_, QTm], w=[ps])
                    yield
                    pt = b1.get()
                    act(pt.t[0:w_, 0:ncl], ps.t[0:w_, 0:ncl], AF.Exp, r=[ps, NEG4], w=[pt], scale=SC,
                        bias=NEG4.t[0:w_, 0:1])
                    if dmask:
                        tt(pt.t[:, 0:512], pt.t[:, 0:512], MDb.t[:, 512 * i:512 * i + 512], ALU.mult,
                           r=[pt, MDb], w=[pt])
                    yield
                    mm(po.t[0:65, 0:ncl], vb_.t[0:w_, i, :], pt.t[0:w_, 0:ncl], first, final, r=[vb_, pt], w=[po])

                def step_blk(kb_, vb_, nsub, w_, first, final):
                    ps = psr()
                    for i in range(nsub):
                        mm(ps.t[0:w_, ncl * i:ncl * i + ncl], kb_.t[:, 128 * i:128 * i + w_], QTm.t[:, h, c0:c0 + ncl],
                           True, True, r=[kb_, QTm], w=[ps])
                    yield
                    pt = b1.get()
                    act(pt.t[0:w_, 0:ncl * nsub], ps.t[0:w_, 0:ncl * nsub], AF.Exp, r=[ps, NEG4], w=[pt], scale=SC,
                        bias=NEG4.t[0:w_, 0:1])
                    yield
                    for i in range(nsub):
                        mm(po.t[0:65, 0:ncl], vb_.t[0:w_, i, :], pt.t[0:w_, ncl * i:ncl * i + ncl],
                           first and i == 0, final and i == nsub - 1, r=[vb_, pt], w=[po])

                def all_steps():
                    for kbi in range(nkb):
                        last = kbi == nkb - 1
                        nkeys = 512 if (isp or not last) else 64
                        kb_ = MKB.get()
                        vb_ = MVB.get()
                        dma(LQ, kb_.t[:, 0:nkeys], c_mk[l, seq, :, h, 512 * kbi:512 * kbi + nkeys],
                            r=[CB(l, seq, kbi)], w=[kb_])
                        if nkeys == 512:
                            dma(LQ, vb_.t[:], c_mv[l, seq, h, 512 * kbi:512 * kbi + 512, :].rearrange(
                                "(t p) c -> p t c", p=128), r=[CB(l, seq, kbi)], w=[vb_])
                        else:
                            dma(LQ, vb_.t[0:64, 0, :], c_mv[l, seq, h, 512 * kbi:512 * kbi + 64, :],
                                r=[CB(l, seq, kbi)], w=[vb_])
                        nsub = (nkeys + 127) // 128
                        if not isp:
                            yield step_blk(kb_, vb_, nsub, min(128, nkeys), kbi == 0, last)
                            continue
                        for i in range(nsub):
                            w_ = min(128, nkeys - 128 * i)
                            yield step(kb_, vb_, i, w_, kbi == 0 and i == 0, last and i == nsub - 1, isp and last)

                run_interleaved(all_steps(), MLW if isp else 2)
                sq = b1.get()
                act(sq.t[0:65, 0:ncl], po.t[0:65, 0:ncl], AF.Square, r=[po], w=[sq])
                ps2 = psr()
                mm(ps2.t[0:64, 0:ncl], W65[0:65, 0:64], sq.t[0:65, 0:ncl], True, True, r=[sq, CMb], w=[ps2])
                t1 = f2.get()
                act(t1.t[0:64, 0:ncl], ps2.t[0:64, 0:ncl], AF.Ln, r=[ps2], w=[t1])
                t2 = f2.get()
                act(t2.t[0:64, 0:ncl], t1.t[0:64, 0:ncl], AF.Exp, r=[t1], w=[t2], scale=-0.5)
                stt(OCT.t[:, h, c0:c0 + ncl], po.t[0:64, 0:ncl], gcol(l, 39, 0, 64), t2.t[0:64, 0:ncl], ALU.mult,
                    ALU.mult, r=[po, t2, GT], w=[OCT])

    NEG4 = S.sb([128, 1], F32, "NEG4")
    memset(NEG4.t[:], -4.0, w=[NEG4])

    import os as _os
    _ph = _os.environ.get("KPHASES", "abcd")
    mark("prep_mem")
    if "a" in _ph:
        prep_mem_prompt()
    mark("prep_sample")
    if "b" in _ph:
        prep_sample_caches()
    if "c" in _ph:
        group("s", 0)
    if "d" in _ph:
        for g in range(NG):
            group("p", g)
    mark("end")
    S.finish()
    es.close()
    return nc


def host_consts(cfg):
    cm = np.zeros((128, 13, 128), np.float32)
    cm[:, 0] = np.eye(128)
    cm[:, 1] = 1.0 / 1024
    cm[:, 2] = 1.0 / 256
    cm[:, 3] = 1.0 / 128
    cm[0:64, 4, 0:64] = 1.0 / 64
    cm[64:128, 4, 64:128] = 1.0 / 64
    cm[0:64, 5, 0:64] = 1.0 / 64
    cm[64:96, 5, 64:96] = 1.0 / 32
    cm[0:64, 6, 0:64] = 1.0 / 64
    cm[64, 6, 0:64] = EPS
    q = np.arange(128)[:, None]
    k = np.arange(128)[None, :]
    cm[:, 7] = (k < q)
    cm[:, 8] = np.where(k < q, 0.0, -30000.0)
    cm[:, 9] = (q <= k)
    cm[:, 10] = 1.0
    bm = (np.arange(128)[:, None] // 32 == np.arange(256)[None, :] // 64).astype(np.float32)
    cm[:, 11] = bm[:, 0:128]
    cm[:, 12] = bm[:, 128:256]
    cmat = cm.reshape(128, 13 * 128)
    half = 16
    freqs = (10000.0 ** (-np.arange(half, dtype=np.float32) / half)).astype(np.float32)
    pos = np.arange(cfg.NPOS, dtype=np.float32)
    ang = (pos[None, :] * freqs[:, None]).astype(np.float32)
    c = np.cos(ang).astype(np.float32)
    s = np.sin(ang).astype(np.float32)
    ropec = np.ones((96, cfg.NPOS), np.float32)
    ropes = np.zeros((96, cfg.NPOS), np.float32)
    ropec[64:80] = c
    ropec[80:96] = c
    ropes[64:80] = -s
    ropes[80:96] = s
    md = np.zeros((128, 4, 512), np.float32)
    kk = np.arange(128)[:, None]
    qq = np.arange(512)[None, :]
    for i in range(4):
        md[:, i, :] = ((128 * i + kk) // 64 <= qq // 64)
    return cmat, ropec, ropes, md.reshape(128, 2048)


def host_gtab(inp):
    gt = np.ones((128, 2 * GL), np.float32)
    for l in range(2):
        b = GL * l
        gt[:, b + 0:b + 8] = inp["g_mix_norm"][l].reshape(8, 128).T
        gt[:, b + 8:b + 16] = inp["g_cross_norm"][l].reshape(8, 128).T
        gt[:, b + 16:b + 24] = inp["g_ffn_norm"][l].reshape(8, 128).T
        gt[:, b + 24:b + 32] = inp["g_mem_norm"][l].reshape(8, 128).T
        gt[:, b + 32:b + 34] = inp["g_cq"][l].reshape(2, 128).T
        gt[:, b + 34] = inp["g_ckv"][l]
        gt[0:64, b + 35] = inp["g_qn"][l]
        gt[64:96, b + 35] = inp["g_qr"][l]
        gt[0:64, b + 36] = inp["g_kn"][l]
        gt[64:96, b + 36] = inp["g_kr"][l]
        gt[0:64, b + 37] = inp["g_sb_out"][l]
        gt[64:128, b + 37] = inp["g_sb_out"][l]
        gt[0:64, b + 38] = inp["g_gla_out"][l]
        gt[64:128, b + 38] = inp["g_gla_out"][l]
        gt[0:64, b + 39] = inp["g_mla_out"][l]
        gt[:, b + 40] = inp["g_cqn"][l]
        gt[:, b + 41] = inp["g_ckn"][l]
        gt[:, b + 42] = inp["b_gla_gate"][l]
    for h in range(4):
        gt[:, 44 + h] = 0.0
        gt[32 * h:32 * h + 32, 44 + h] = 1.0
    return gt


_NC_CACHE = {}


def run(inp, cfg):
    key = (cfg.SEQ, cfg.PAST, cfg.NSS)
    if key not in _NC_CACHE:
        _NC_CACHE[key] = build(cfg)
    nc = _NC_CACHE[key]
    f = lambda a: np.ascontiguousarray(np.asarray(a, dtype=np.float32))
    cmat, ropec, ropes, md = host_consts(cfg)
    gt = host_gtab({k: np.asarray(v) for k, v in inp.items()})
    NSS = cfg.NSS
    shared = {k: f(inp[k]) for k in ("w_in", "w_gla_gate", "w_uq", "w_ukv", "w_out", "w_cq", "w_ck", "w_cv", "w_co",
                                     "w_gate", "w_up", "w_down")}
    shared.update(gtab=gt, cmat=cmat, ropec=ropec, ropes=ropes, mdiag=md)
    in_maps = []
    for c in range(8):
        sl = slice(NSS * c, NSS * c + NSS)
        m = dict(shared)
        m["xp"] = f(inp["x_prompt"][c // 2])
        m["xs"] = f(inp["x_sample"][sl]).reshape(NSS * 64, D)
        m["memp"] = f(inp["mem_prompt"][c // 2])
        m["csk"] = f(inp["cache_sb_k"][:, sl]).reshape(2, NSS, cfg.PAST, 256)
        m["csv"] = f(inp["cache_sb_v"][:, sl]).reshape(2, NSS, cfg.PAST, 256)
        m["sgla"] = f(inp["state_gla"][:, sl])
        m["clat"] = f(inp["cache_mla_latent"][:, sl])
        m["ckr"] = f(inp["cache_mla_krope"][:, sl])
        m["cmk"] = f(inp["cache_mem_k"][:, sl]).reshape(2, NSS, 256, 512)
        m["cmv"] = f(inp["cache_mem_v"][:, sl]).reshape(2, NSS, 256, 512)
        in_maps.append(m)
    res = run_bass_kernel_spmd(nc, in_maps, core_ids=list(range(8)))
    R = res.results
    B = 4
    SEQ = cfg.SEQ
    DB = 8 * NSS
    cat_p = lambda name, shp: np.stack([np.asarray(R[2 * b][name]).reshape(shp) for b in range(B)])
    y_p = cat_p("yp", (SEQ, D))
    y_s = np.concatenate([np.asarray(R[c]["ys"]).reshape(NSS, 64, D) for c in range(8)], 0)

    def pl(name, shp):
        return np.stack([np.asarray(R[2 * b][name]).reshape((2,) + shp) for b in range(B)], 1)

    def sl_(name, shp):
        return np.concatenate([np.asarray(R[c][name]).reshape((2, NSS) + shp) for c in range(8)], 1)

    outs = (y_p, y_s,
            pl("sbk_p", (SEQ, 4, 64)), pl("sbv_p", (SEQ, 4, 64)), pl("gla_p", (4, 32, 64)),
            pl("lat_p", (SEQ, 128)), pl("kr_p", (SEQ, 32)), pl("mk_p", (256, 4, 128)), pl("mv_p", (256, 4, 128)),
            sl_("sbk_s", (64, 4, 64)), sl_("sbv_s", (64, 4, 64)), sl_("gla_s", (4, 32, 64)),
            sl_("lat_s", (64, 128)), sl_("kr_s", (64, 32)))
    return tuple(np.ascontiguousarray(o.astype(np.float32)) for o in outs)


def kernel(**inputs):
    return run(inputs, Cfg())
```

```python
from contextlib import ExitStack
import numpy as np
import concourse.bass as bass
import concourse.mybir as mybir
from concourse.bass_utils import run_bass_kernel_spmd

F32 = mybir.dt.float32
BF = mybir.dt.bfloat16
AF = mybir.ActivationFunctionType
ALU = mybir.AluOpType
AX = mybir.AxisListType

D = 1024
NIN = 1968
DFF = 2816
EPS = 1e-6
GL = 48


class Cfg:
    def __init__(self, SEQ=4096, PAST=4096, NSS=4):
        self.SEQ = SEQ
        self.PAST = PAST
        self.NSS = NSS
        self.NG = SEQ // 512
        self.NK = max(SEQ, PAST + 512)
        self.NTOT = PAST + 64
        self.NPOS = max(SEQ, PAST + 64)


class Buf:
    __slots__ = ("w", "r", "name")

    def __init__(self, name=""):
        self.w = None
        self.r = {}
        self.name = name


class Tl:
    __slots__ = ("t", "b")

    def __init__(self, t, b):
        self.t = t
        self.b = b


class Sched:
    ROT = 30000
    KD = 8

    def __init__(self, nc, es):
        self.nc = nc
        self.es = es
        self.eng = {}
        for name, kind in (("pe", "c"), ("act", "c"), ("dve", "c"), ("pool", "d"), ("sp", "d")):
            self.eng[name] = dict(ops=[], n=0, kind=kind, known={})
        self.nt = 0

    def sb(self, shape, dt, name=None):
        self.nt += 1
        name = name or ("t%d" % self.nt)
        t = self.es.enter_context(self.nc.sbuf_tensor(name, list(shape), dt))
        return Tl(t, Buf(name))

    def ps(self, name):
        t = self.es.enter_context(self.nc.psum_tensor(name, [128, 512], F32))
        return Tl(t, Buf(name))

    def dram(self, name, shape, dt):
        return self.nc.dram_tensor(name, list(shape), dt, kind="Internal").ap()

    def _kv(self, dep):
        en, seq = dep
        if self.eng[en]["kind"] == "c":
            return (en, (seq - 1) // self.ROT), (seq - 1) % self.ROT + 1
        k = seq - 1
        return (en, k % self.KD), 16 * (k // self.KD + 1)

    def op(self, en, fn, r=(), w=()):
        e = self.eng[en]
        deps = set()
        for b in r:
            b = b.b if isinstance(b, Tl) else b
            if b.w is not None:
                deps.add(b.w)
        for b in w:
            b = b.b if isinstance(b, Tl) else b
            if b.w is not None:
                deps.add(b.w)
            for x in b.r.values():
                deps.update(x)
        e["n"] += 1
        seq = e["n"]
        me = (en, seq)
        if e["kind"] == "d" and seq > self.KD:
            deps.add((en, seq - self.KD))
        waits = []
        for d in deps:
            if d[0] == en and en == "pe":
                continue
            key, val = self._kv(d)
            if e["known"].get(key, 0) >= val:
                continue
            e["known"][key] = val
            waits.append((key, val))
        e["ops"].append((waits, fn, self._kv(me)))
        for b in r:
            b = b.b if isinstance(b, Tl) else b
            lst = b.r.setdefault(en, [])
            if e["kind"] == "c":
                lst[:] = [me]
            else:
                lst.append(me)
        for b in w:
            b = b.b if isinstance(b, Tl) else b
            b.w = me
            b.r = {}

    def finish(self):
        nc = self.nc
        fin = []
        for en in ("pool", "sp"):
            n = self.eng[en]["n"]
            for s in range(max(1, n - self.KD + 1), n + 1):
                fin.append(self._kv((en, s)))
        for en in ("pe", "act", "dve"):
            n = self.eng[en]["n"]
            if n:
                fin.append(self._kv((en, n)))
        keys = set()
        for e in self.eng.values():
            for waits, fn, kv in e["ops"]:
                keys.add(kv[0])
        sems = {}
        for k in sorted(keys):
            sems[k] = self.es.enter_context(nc.semaphore("s_%s_%d" % k))
        block = self.es.enter_context(nc.Block())
        engs = self.eng

        def replay(en, eobj, extra=()):
            inc = 1 if engs[en]["kind"] == "c" else 16
            for waits, fn, kv in engs[en]["ops"]:
                for k, v in waits:
                    eobj.wait_ge(sems[k], v)
                fn(eobj).then_inc(sems[kv[0]], inc)
            for k, v in extra:
                eobj.wait_ge(sems[k], v)

        @block.tensor
        def _(t):
            replay("pe", t)

        @block.scalar
        def _(a):
            replay("act", a)

        @block.vector
        def _(v):
            replay("dve", v)

        @block.gpsimd
        def _(g):
            replay("pool", g)

        @block.sync
        def _(s):
            replay("sp", s, fin)


class Pool:
    def __init__(self, S, shape, dt, n, name):
        self.tiles = [S.sb(shape, dt, "%s%d" % (name, i)) for i in range(n)]
        self.i = 0

    def get(self):
        t = self.tiles[self.i % len(self.tiles)]
        self.i += 1
        return t


def run_interleaved(gens, width, gap=0):
    active = []
    it = iter(gens)
    done = False
    since = gap
    while True:
        while not done and len(active) < width and (since >= gap or not active):
            g = next(it, None)
            if g is None:
                done = True
                break
            active.append(g)
            since = 0
            if gap > 0:
                break
        since += 1
        if not active:
            break
        for g in list(active):
            try:
                next(g)
            except StopIteration:
                active.remove(g)


def build(cfg):
    nc = bass.Bass("TRN2", target_bir_lowering=False)
    SEQ, PAST, NSS, NG, NK = cfg.SEQ, cfg.PAST, cfg.NSS, cfg.NG, cfg.NK
    NSEQ = 1 + NSS
    NTS = NSS * 64
    NKBS = PAST // 512 + 1

    def din(name, shape):
        return nc.dram_tensor(name, list(shape), F32, kind="ExternalInput").ap()

    def dout(name, shape):
        return nc.dram_tensor(name, list(shape), F32, kind="ExternalOutput").ap()

    xp = din("xp", [SEQ, D])
    xs = din("xs", [NTS, D])
    memp = din("memp", [256, D])
    csk = din("csk", [2, NSS, PAST, 256])
    csv = din("csv", [2, NSS, PAST, 256])
    sgla = din("sgla", [2, NSS, 4, 32, 64])
    clat = din("clat", [2, NSS, PAST, 128])
    ckr = din("ckr", [2, NSS, PAST, 32])
    cmk = din("cmk", [2, NSS, 256, 512])
    cmv = din("cmv", [2, NSS, 256, 512])
    w_in = din("w_in", [2, D, NIN])
    w_gg = din("w_gla_gate", [2, 16, 128])
    w_uq = din("w_uq", [2, 256, 768])
    w_ukv = din("w_ukv", [2, 128, 1024])
    w_out = din("w_out", [2, D, D])
    w_cq = din("w_cq", [2, D, 512])
    w_ck = din("w_ck", [2, D, 512])
    w_cv = din("w_cv", [2, D, 512])
    w_co = din("w_co", [2, 512, D])
    w_gate = din("w_gate", [2, D, DFF])
    w_up = din("w_up", [2, D, DFF])
    w_down = din("w_down", [2, DFF, D])
    gtab = din("gtab", [128, 2 * GL])
    cmat = din("cmat", [128, 13 * 128])
    ropec = din("ropec", [96, cfg.NPOS])
    ropes = din("ropes", [96, cfg.NPOS])
    mdiag = din("mdiag", [128, 4 * 512])

    yp = dout("yp", [SEQ, D])
    ys = dout("ys", [NTS, D])
    o_sbk_p = dout("sbk_p", [2, SEQ, 256])
    o_sbv_p = dout("sbv_p", [2, SEQ, 256])
    o_gla_p = dout("gla_p", [2, 4, 32, 64])
    o_lat_p = dout("lat_p", [2, SEQ, 128])
    o_kr_p = dout("kr_p", [2, SEQ, 32])
    o_mk_p = dout("mk_p", [2, 256, 512])
    o_mv_p = dout("mv_p", [2, 256, 512])
    o_sbk_s = dout("sbk_s", [2, NTS, 256])
    o_sbv_s = dout("sbv_s", [2, NTS, 256])
    o_gla_s = dout("gla_s", [2, NSS, 4, 32, 64])
    o_lat_s = dout("lat_s", [2, NTS, 128])
    o_kr_s = dout("kr_s", [2, NTS, 32])

    es = ExitStack()
    S = Sched(nc, es)
    c_sbk = S.dram("c_sbk", [2, NSEQ, 64, 4, NK], BF)
    c_sbv = S.dram("c_sbv", [2, NSEQ, NK, 256], BF)
    c_mk = S.dram("c_mk", [2, NSEQ, 96, 8, NK], BF)
    c_mv = S.dram("c_mv", [2, NSEQ, 8, NK, 65], BF)
    c_memk = S.dram("c_memk", [2, NSEQ, 128, 4, 256], BF)
    c_memv = S.dram("c_memv", [2, NSEQ, 256, 512], BF)
    cbuf = {}

    def CB(l, s, kb):
        k = (l, s, kb)
        if k not in cbuf:
            cbuf[k] = Buf("cb%d_%d_%d" % k)
        return cbuf[k]

    membuf = {(l, s): Buf("mem%d_%d" % (l, s)) for l in range(2) for s in range(NSEQ)}

    PSR = [S.ps("psr%d" % i) for i in range(6)]
    PSL = [S.ps("psl%d" % i) for i in range(2)]
    psi = [0, 0]

    def psr():
        psi[0] += 1
        return PSR[psi[0] % 6]

    def psl():
        psi[1] += 1
        return PSL[psi[1] % 2]

    GT = S.sb([128, 2 * GL], F32, "GT")
    NEGB = S.sb([128, 2], F32, "NEGB")
    CMf = S.sb([128, 13 * 128], F32, "CMf")
    CMb = S.sb([128, 13 * 128], BF, "CMb")
    MDb = S.sb([128, 4 * 512], BF, "MDb")
    ONESF = S.sb([128, 512], F32, "ONESF")
    WGG = S.sb([16, 2, 128], BF, "WGG")
    def cmf(i):
        return CMf.t[:, 128 * i:128 * (i + 1)]

    def cmb(i):
        return CMb.t[:, 128 * i:128 * (i + 1)]

    XT = S.sb([128, 8, 512], F32, "XT")
    HT = S.sb([128, 8, 512], BF, "HT")
    WPn = 3
    WP = [S.sb([128, 4224], BF, "WP%d" % i) for i in range(WPn)]
    wpi = [0]
    f2 = Pool(S, [128, 512], F32, 5, "f2_")
    b1 = Pool(S, [128, 512], BF, 5, "b1_")
    XIN = Pool(S, [128, 1024], F32, 2, "xin")
    QA = S.sb([64, 4, 512], BF, "QA")
    KA = S.sb([64, 4, 512], BF, "KA")
    VBF = S.sb([128, 4, 256], BF, "VBF")
    KBK = Pool(S, [64, 4, 512], BF, 2, "kbk")
    VBK = Pool(S, [128, 4, 256], BF, 2, "vbk")
    OSBT = S.sb([128, 1024], F32, "OSBT")
    OSB = Tl(OSBT.t[:, :].rearrange("p (a b) -> p a b", a=4), OSBT.b)
    OBT = Tl(OSBT.t[:, :].rearrange("p (a b) -> p a b", a=2), OSBT.b)
    NEGC = S.sb([128, 16], F32, "NEGC")
    NEGCB = [Buf("negc%d" % i) for i in range(16)]
    import os as _os2
    SBW = int(_os2.environ.get("KSBW", "4"))
    MLW = int(_os2.environ.get("KMLW", "4"))
    SBG = int(_os2.environ.get("KSBG", "2"))
    LQ = _os2.environ.get("KLQ", "pool")
    MLQ = _os2.environ.get("KMLQ", "sp")
    PLQ = _os2.environ.get("KPLQ", "pool")
    FT = Pool(S, [128, 513], F32, 4, "ft")
    QG = S.sb([128, 512], F32, "QG")
    KG = S.sb([128, 512], F32, "KG")
    AG = S.sb([16, 512], BF, "AG")
    SPG = S.sb([128, 512], F32, "SPG")
    CS = S.sb([128, 512], F32, "CS")
    EBt = S.sb([128, 512], F32, "EB")
    NBL = S.sb([128, 8], F32, "NBL")
    QTg = S.sb([128, 512], BF, "QTg")
    KTg = S.sb([128, 512], BF, "KTg")
    KDg = S.sb([128, 512], F32, "KDg")
    KDt = Pool(S, [128, 128], BF, 2, "kdt")
    VG = S.sb([128, 4, 256], BF, "VG")
    SALL = [S.sb([128, 256], F32, "SALL%d" % i) for i in range(2)]
    SS = S.sb([128, 256], F32, "SS")
    SBF16 = S.sb([128, 256], BF, "SBF16")
    ATM = Pool(S, [128, 128], BF, 3, "atm")
    SRG = S.sb([128, 2, 512], BF, "SRG")
    CQ = OBT
    LAT = S.sb([128, 512], F32, "LAT")
    LATB = S.sb([128, 512], BF, "LATB")
    KRN = S.sb([96, 512], F32, "KRN")
    ROT = Pool(S, [96, 512], F32, 2, "rot")
    KRB = S.sb([96, 512], BF, "KRB")
    QN = Pool(S, [96, 512], F32, 1, "qn")
    QTm = S.sb([96, 8, 512], BF, "QTm")
    KTn = Pool(S, [96, 512], BF, 2, "ktn")
    VAUG = S.sb([128, 4, 8 * 65], BF, "VAUG")
    COS = S.sb([96, 512], F32, "COS")
    SIN = S.sb([96, 512], F32, "SIN")
    MKB = Pool(S, [96, 512], BF, 2, "mkb")
    MVB = Pool(S, [128, 4, 65], BF, 2, "mvb")
    MEMK = Pool(S, [128, 4, 256], BF, 1, "memk")
    MEMV = Pool(S, [128, 2, 512], BF, 1, "memv")
    AT = S.sb([128, 22, 512], BF, "AT")
    SMALL = Pool(S, [128, 16], F32, 6, "small")
    OUTT = XIN

    class _V:
        def __init__(self, ap):
            self.t = ap
            self.b = AT.b
    OCR = Tl(AT.t[:, 0:4, :], AT.b)
    QCT = Tl(AT.t[:, 4:8, :], AT.b)
    OCT = Tl(AT.t[0:64, 8:16, :], AT.b)
    OAT = Tl(AT.t[:, 16:18, :], AT.b)
    OBB = Tl(AT.t[:, 18:20, :], AT.b)
    CQN = Tl(AT.t[:, 20:22, :], AT.b)

    def mm(out, lhsT, rhs, start, stop, r, w):
        S.op("pe", lambda e: e.matmul(out, lhsT=lhsT, rhs=rhs, start=start, stop=stop), r=r, w=w)

    def tr(out, in_, ident, r, w):
        S.op("pe", lambda e: e.transpose(out, in_, ident), r=r, w=w)

    def act(out, in_, func, r, w, bias=None, scale=None):
        kw = {}
        if bias is not None:
            kw["bias"] = bias
        if scale is not None:
            kw["scale"] = scale
        S.op("act", lambda e: e.activation(out, in_, func, **kw), r=r, w=w)

    def tt(out, in0, in1, op, r, w):
        S.op("dve", lambda e: e.tensor_tensor(out, in0, in1, op), r=r, w=w)

    def ts(out, in0, s1, s2, op0, op1, r, w):
        if op1 is None:
            S.op("dve", lambda e: e.tensor_scalar(out, in0, s1, None, op0), r=r, w=w)
        else:
            S.op("dve", lambda e: e.tensor_scalar(out, in0, s1, s2, op0, op1), r=r, w=w)

    def stt(out, in0, sc, in1, op0, op1, r, w):
        S.op("dve", lambda e: e.scalar_tensor_tensor(out, in0, sc, in1, op0, op1), r=r, w=w)

    def cp(out, in_, r, w, eng="dve"):
        if eng == "dve":
            S.op("dve", lambda e: e.tensor_copy(out, in_), r=r, w=w)
        else:
            S.op("act", lambda e: e.activation(out, in_, AF.Copy), r=r, w=w)

    def dma(q, out, in_, r, w):
        S.op(q, lambda e: e.dma_start(out=out, in_=in_), r=r, w=w)

    def memset(ap, val, w):
        S.op("dve", lambda e: e.memset(ap, val), r=(), w=w)

    def rstd_from(ps_ap, pb, P0, P1, N, eps=EPS):
        t1 = f2.get()
        act(t1.t[P0:P1, 0:N], ps_ap, AF.Ln, r=[pb], w=[t1], bias=EPSC[P0:P1, 0:1] if eps else None)
        t2 = f2.get()
        act(t2.t[P0:P1, 0:N], t1.t[P0:P1, 0:N], AF.Exp, r=[t1], w=[t2], scale=-0.5)
        return t2

    dma("sp", GT.t[:], gtab[:, :], r=[], w=[GT])
    dma("sp", CMf.t[:], cmat[:, :], r=[], w=[CMf])
    dma("pool", MDb.t[:], mdiag[:, :], r=[], w=[MDb])
    dma("pool", CMb.t[:], cmat[:, :], r=[], w=[CMb])
    memset(ONESF.t[:], 1.0, w=[ONESF])
    EPSt = S.sb([128, 1], F32, "EPSC")
    EPSC = EPSt.t
    memset(EPSC[:], EPS, w=[EPSt])
    for l in range(2):
        ts(NEGB.t[:, l:l + 1], GT.t[:, GL * l + 42:GL * l + 43], -1.0, None, ALU.mult, None, r=[GT], w=[NEGB])
        dma("pool", WGG.t[:, l, :], w_gg[l], r=[], w=[WGG])
    for t in FT.tiles:
        memset(t.t[:, 0:1], 0.0, w=[t])
    memset(VAUG.t[:], 1.0, w=[VAUG])
    IDF = cmf(0)
    ONES1024, ONES256, ONES128, BD64, BD96, W65 = cmb(1), cmb(2), cmb(3), cmb(4), cmb(5), cmb(6)
    MASKL01, MASKLNEG, MASKU01 = cmf(7), cmf(8), cmf(9)
    CONST = [CMf, CMb, EPSt]

    def gcol(l, c, P0=0, P1=128):
        return GT.t[P0:P1, GL * l + c:GL * l + c + 1]

    def wload(src_ap, nelem):
        wpi[0] += 1
        t = WP[wpi[0] % WPn]
        return t

    def wpiece(w2d, r0, nkc, c0, ncol, p=128):
        wpi[0] += 1
        t = WP[wpi[0] % WPn]
        view = t.t[0:p, 0:nkc * ncol].rearrange("p (a b) -> p a b", a=nkc)
        src = w2d[r0:r0 + nkc * p, c0:c0 + ncol].rearrange("(kc p) n -> p kc n", p=p)
        dma("pool", view, src, r=[], w=[t])
        return t, view

    def norm_x(l, gbase, NT):
        ps = psr()
        for c in range(8):
            sq = b1.get()
            act(sq.t[:, 0:NT], XT.t[:, c, 0:NT], AF.Square, r=[XT], w=[sq])
            mm(ps.t[:, 0:NT], ONES1024, sq.t[:, 0:NT], c == 0, c == 7, r=[sq, CMb], w=[ps])
        rs = rstd_from(ps.t[:, 0:NT], ps, 0, 128, NT)
        for c in range(8):
            stt(HT.t[:, c, 0:NT], XT.t[:, c, 0:NT], gcol(l, gbase + c), rs.t[:, 0:NT], ALU.mult, ALU.mult,
                r=[XT, rs, GT], w=[HT])

    def proj_fm(ps, pr0, M, wt, wv, col0, NT, KC=8, rhs=None, rb=None):
        rhs = rhs if rhs is not None else HT
        for kc in range(KC):
            mm(ps.t[pr0:pr0 + M, 0:NT], wv[:, kc, col0:col0 + M], rhs.t[:, kc, 0:NT], kc == 0, kc == KC - 1,
               r=[wt, rhs], w=[ps])

    def mla_kside(l, seq, k0, NT, latb, krb, wukv_t, wukv_v):
        kb = k0 // 512

        def head(h):
            ps = psr()
            mm(ps.t[0:64, 0:NT], wukv_v[:, 0, 128 * h:128 * h + 64], latb.t[:, 0:NT], True, True,
               r=[wukv_t, latb], w=[ps])
            yield
            sq = b1.get()
            act(sq.t[0:64, 0:NT], ps.t[0:64, 0:NT], AF.Square, r=[ps], w=[sq])
            yield
            ps2 = psr()
            mm(ps2.t[0:64, 0:NT], BD64[0:64, 0:64], sq.t[0:64, 0:NT], True, True, r=[sq, CMb], w=[ps2])
            yield
            t1 = f2.get()
            act(t1.t[0:64, 0:NT], ps2.t[0:64, 0:NT], AF.Ln, r=[ps2], w=[t1], bias=EPSC[0:64, 0:1])
            yield
            act(t1.t[0:64, 0:NT], t1.t[0:64, 0:NT], AF.Exp, r=[t1], w=[t1], scale=-0.5)
            yield
            kt = KTn.get()
            stt(kt.t[0:64, 0:NT], ps.t[0:64, 0:NT], gcol(l, 36, 0, 64), t1.t[0:64, 0:NT], ALU.mult, ALU.mult,
                r=[ps, t1, GT], w=[kt])
            cp(kt.t[64:96, 0:NT], krb.t[64:96, 0:NT], r=[krb], w=[kt])
            yield
            dma("sp", c_mk[l, seq, :, h, k0:k0 + NT], kt.t[0:96, 0:NT], r=[kt], w=[CB(l, seq, kb)])

        run_interleaved((head(h) for h in range(8)), 3, 2)
        ntt = (NT + 127) // 128
        for t in range(ntt):
            w_ = min(128, NT - 128 * t)
            ps = psr()
            vcols = wukv_v[:, 0, :].rearrange("p (h c) -> p h c", h=8)[:, :, 64:128]
            mm(ps.t[0:w_, 0:512].rearrange("p (h c) -> p h c", h=8), latb.t[:, 128 * t:128 * t + w_], vcols, True, True,
               r=[wukv_t, latb], w=[ps])
            cp(VAUG.t[0:w_, t, :].rearrange("p (h c) -> p h c", h=8)[:, :, 0:64],
               ps.t[0:w_, 0:512].rearrange("p (h c) -> p h c", h=8), r=[ps], w=[VAUG], eng="act")
        for t in range(ntt):
            w_ = min(128, NT - 128 * t)
            dma("sp", c_mv[l, seq, :, k0 + 128 * t:k0 + 128 * t + w_, :].rearrange("h p c -> p h c"),
                VAUG.t[0:w_, t, :].rearrange("p (h c) -> p h c", h=8), r=[VAUG], w=[CB(l, seq, kb)])

    def load_wukv(l):
        return wpiece(w_ukv[l], 0, 1, 0, 1024)

    def prep_mem_prompt():
        for t in range(2):
            xin = XIN.get()
            dma("sp", xin.t[:], memp[128 * t:128 * t + 128, :], r=[], w=[xin])
            for half in range(2):
                ps = psr()
                for c4 in range(4):
                    c = 4 * half + c4
                    tr(ps.t[:, 128 * c4:128 * c4 + 128], xin.t[:, 128 * c:128 * c + 128], IDF, r=[xin, CMf], w=[ps])
                cp(XT.t[:, 4 * half:4 * half + 4, 128 * t:128 * t + 128],
                   ps.t[:, :].rearrange("p (a b) -> p a b", a=4), r=[ps], w=[XT])
        import os as _os
        _sub = int(_os.environ.get("KSUB", "9"))
        if _sub < 1:
            return
        for l in range(2):
            norm_x(l, 24, 256)
            if _sub < 2:
                continue
            wt, wv = wpiece(w_ck[l], 0, 8, 0, 512)
            for h in range(4 if _sub in (3, 4, 9) else 0):
                ps = psr()
                proj_fm(ps, 0, 128, wt, wv, 128 * h, 256)
                sq = b1.get()
                act(sq.t[:, 0:256], ps.t[:, 0:256], AF.Square, r=[ps], w=[sq])
                ps2 = psr()
                mm(ps2.t[:, 0:256], ONES128, sq.t[:, 0:256], True, True, r=[sq, CMb], w=[ps2])
                rs = rstd_from(ps2.t[:, 0:256], ps2, 0, 128, 256)
                kf = f2.get()
                stt(kf.t[:, 0:256], ps.t[:, 0:256], gcol(l, 41), rs.t[:, 0:256], ALU.mult, ALU.mult,
                    r=[ps, rs, GT], w=[kf])
                kb_ = b1.get()
                cp(kb_.t[:, 0:256], kf.t[:, 0:256], r=[kf], w=[kb_])
                dma("sp", c_memk[l, 0, :, h, :], kb_.t[:, 0:256], r=[kb_], w=[membuf[(l, 0)]])
                ps3 = psr()
                for t in range(2):
                    tr(ps3.t[:, 128 * t:128 * t + 128], kf.t[:, 128 * t:128 * t + 128], IDF, r=[kf, CMf], w=[ps3])
                ko = f2.get()
                cp(ko.t[:, 0:256], ps3.t[:, 0:256], r=[ps3], w=[ko], eng="act")
                dma("sp", o_mk_p[l, :, 128 * h:128 * h + 128].rearrange("(t p) c -> p t c", p=128),
                    ko.t[:, 0:256].rearrange("p (t c) -> p t c", t=2), r=[ko], w=[])
            wt, wv = wpiece(w_cv[l], 0, 8, 0, 512)
            for t in range(2 if _sub >= 4 else 0):
                ps = psr()
                for kc in range(8):
                    mm(ps.t[:, 0:512], HT.t[:, kc, 128 * t:128 * t + 128], wv[:, kc, :], kc == 0, kc == 7,
                       r=[HT, wt], w=[ps])
                vo = f2.get()
                cp(vo.t[:, :], ps.t[:, :], r=[ps], w=[vo])
                dma("sp", o_mv_p[l, 128 * t:128 * t + 128, :], vo.t[:, :], r=[vo], w=[])
                vb = b1.get()
                cp(vb.t[:, :], vo.t[:, :], r=[vo], w=[vb], eng="act")
                dma("sp", c_memv[l, 0, 128 * t:128 * t + 128, :], vb.t[:, :], r=[vb], w=[membuf[(l, 0)]])

    def prep_sample_caches():
        for l in range(2):
            wukv_t, wukv_v = load_wukv(l)
            for j in range(NSS):
                seq = 1 + j
                dma("pool", c_sbv[l, seq, 0:PAST, :], csv[l, j], r=[], w=[CB(l, seq, kb) for kb in range(PAST // 512)])
                dma("pool", c_memv[l, seq], cmv[l, j], r=[], w=[membuf[(l, seq)]])
                for t in range(2):
                    xin = XIN.get()
                    dma("sp", xin.t[:, 0:512], cmk[l, j, 128 * t:128 * t + 128, :], r=[], w=[xin])
                    ps = psr()
                    for h in range(4):
                        tr(ps.t[:, 128 * h:128 * h + 128], xin.t[:, 128 * h:128 * h + 128], IDF, r=[xin, CMf], w=[ps])
                    kb_ = b1.get()
                    cp(kb_.t[:, :], ps.t[:, :], r=[ps], w=[kb_])
                    dma("sp", c_memk[l, seq, :, :, 128 * t:128 * t + 128],
                        kb_.t[:, :].rearrange("p (h k) -> p h k", h=4), r=[kb_], w=[membuf[(l, seq)]])
                for kb in range(PAST // 512):
                    k0 = 512 * kb
                    xin = XIN.get()
                    dma(PLQ, xin.t[:, :].rearrange("p (t c) -> p t c", t=4),
                        csk[l, j, k0:k0 + 512, :].rearrange("(t p) c -> p t c", p=128), r=[], w=[xin])
                    for hp in range(2):
                        ps = psr()
                        for t in range(4):
                            tr(ps.t[:, 128 * t:128 * t + 128], xin.t[:, 256 * t + 128 * hp:256 * t + 128 * hp + 128],
                               IDF, r=[xin, CMf], w=[ps])
                        kb_ = b1.get()
                        cp(kb_.t[:, :], ps.t[:, :], r=[ps], w=[kb_], eng=("act" if hp else "dve"))
                        for hh in range(2):
                            dma("sp", c_sbk[l, seq, :, 2 * hp + hh, k0:k0 + 512], kb_.t[64 * hh:64 * hh + 64, :],
                                r=[kb_], w=[CB(l, seq, kb)])
                    xin = XIN.get()
                    dma(PLQ, xin.t[:, 0:512].rearrange("p (t c) -> p t c", t=4),
                        clat[l, j, k0:k0 + 512, :].rearrange("(t p) c -> p t c", p=128), r=[], w=[xin])
                    dma(PLQ, xin.t[:, 512:640].rearrange("p (t c) -> p t c", t=4),
                        ckr[l, j, k0:k0 + 512, :].rearrange("(t p) c -> p t c", p=128), r=[], w=[xin])
                    ps = psr()
                    for t in range(4):
                        tr(ps.t[:, 128 * t:128 * t + 128], xin.t[:, 128 * t:128 * t + 128], IDF, r=[xin, CMf], w=[ps])
                    cp(LATB.t[:, :], ps.t[:, :], r=[ps], w=[LATB])
                    ps = psr()
                    for t in range(4):
                        tr(ps.t[0:32, 128 * t:128 * t + 128], xin.t[:, 512 + 32 * t:512 + 32 * t + 32], IDF,
                           r=[xin, CMf], w=[ps])
                    kr32 = b1.get()
                    cp(kr32.t[0:32, :], ps.t[0:32, :], r=[ps], w=[kr32], eng="act")
                    dma("sp", KRB.t[64:96, :], kr32.t[0:32, :], r=[kr32], w=[KRB])
                    mla_kside(l, seq, k0, 512, LATB, KRB, wukv_t, wukv_v)

    MARKS = []

    def mark(lbl):
        MARKS.append((lbl, S.eng["pe"]["n"], S.eng["act"]["n"], S.eng["dve"]["n"], S.eng["sp"]["n"]))
    nc._marks = MARKS

    def group(kind, g):
        mark("group_%s%d" % (kind, g))
        if kind == "p":
            NT, NTT = 512, 4
            xsrc, ydst = xp, yp
            t0 = 512 * g
            pos0 = t0
            segs = [(0, 0, 512)]
        else:
            NT, NTT = NTS, NTS // 128
            xsrc, ydst = xs, ys
            t0 = 0
            pos0 = PAST
        for t in range(NTT):
            xin = XIN.get()
            dma("sp", xin.t[:], xsrc[t0 + 128 * t:t0 + 128 * t + 128, :], r=[], w=[xin])
            for half in range(2):
                ps = psr()
                for c4 in range(4):
                    c = 4 * half + c4
                    tr(ps.t[:, 128 * c4:128 * c4 + 128], xin.t[:, 128 * c:128 * c + 128], IDF, r=[xin, CMf], w=[ps])
                cp(XT.t[:, 4 * half:4 * half + 4, 128 * t:128 * t + 128],
                   ps.t[:, :].rearrange("p (a b) -> p a b", a=4), r=[ps], w=[XT], eng=("act" if half else "dve"))
        if kind == "p":
            dma("sp", COS.t[:, 0:NT], ropec[:, pos0:pos0 + NT], r=[], w=[COS])
            dma("sp", SIN.t[:, 0:NT], ropes[:, pos0:pos0 + NT], r=[], w=[SIN])
        else:
            for j in range(NSS):
                dma("sp", COS.t[:, 64 * j:64 * j + 64], ropec[:, PAST:PAST + 64], r=[], w=[COS])
                dma("sp", SIN.t[:, 64 * j:64 * j + 64], ropes[:, PAST:PAST + 64], r=[], w=[SIN])

        for l in range(2):
            layer(kind, g, l, NT, NTT, t0)

        for t in range(NTT):
            ot = OUTT.get()
            for half in range(2):
                ps = psr()
                for c4 in range(4):
                    c = 4 * half + c4
                    tr(ps.t[:, 128 * c4:128 * c4 + 128], XT.t[:, c, 128 * t:128 * t + 128], IDF, r=[XT, CMf], w=[ps])
                cp(ot.t[:, 512 * half:512 * half + 512], ps.t[:, :], r=[ps], w=[ot], eng=("act" if half else "dve"))
            dma("sp", ydst[t0 + 128 * t:t0 + 128 * t + 128, :], ot.t[:], r=[ot], w=[])

    def layer(kind, g, l, NT, NTT, t0):
        isp = kind == "p"
        if isp:
            o_sbk, o_sbv, o_lat, o_kr = o_sbk_p, o_sbv_p, o_lat_p, o_kr_p
            k0new = 512 * g
            seqs = [0]
        else:
            o_sbk, o_sbv, o_lat, o_kr = o_sbk_s, o_sbv_s, o_lat_s, o_kr_s
            k0new = PAST
            seqs = list(range(1, 1 + NSS))
        mark("%s%d_l%d_A" % (kind, g, l))
        norm_x(l, 0, NT)
        wt1, wv1 = wpiece(w_in[l], 0, 8, 0, 512)
        for h in range(4):
            ps = psr()
            proj_fm(ps, 0, 64, wt1, wv1, 64 * h, NT)
            cp(QA.t[:, h, 0:NT], ps.t[0:64, 0:NT], r=[ps], w=[QA], eng=("act" if h % 2 else "dve"))
        for h in range(4):
            ps = psr()
            proj_fm(ps, 0, 64, wt1, wv1, 256 + 64 * h, NT)
            cp(KA.t[:, h, 0:NT], ps.t[0:64, 0:NT], r=[ps], w=[KA], eng=("act" if h % 2 else "dve"))
        wt2, wv2 = wpiece(w_in[l], 0, 8, 512, 512)
        for t in range(NTT):
            psk = psr()
            for kc in range(8):
                mm(psk.t[:, 0:256], HT.t[:, kc, 128 * t:128 * t + 128], wv1[:, kc, 256:512], kc == 0, kc == 7,
                   r=[HT, wt1], w=[psk])
            psv = psr()
            for kc in range(8):
                mm(psv.t[:, 0:256], HT.t[:, kc, 128 * t:128 * t + 128], wv2[:, kc, 0:256], kc == 0, kc == 7,
                   r=[HT, wt2], w=[psv])
            kv = f2.get()
            cp(kv.t[:, 0:256], psk.t[:, 0:256], r=[psk], w=[kv])
            cp(kv.t[:, 256:512], psv.t[:, 0:256], r=[psv], w=[kv], eng="act")
            cp(VBF.t[:, t, :], kv.t[:, 256:512], r=[kv], w=[VBF])
            dma("sp", o_sbk[l, t0 + 128 * t:t0 + 128 * t + 128, :], kv.t[:, 0:256], r=[kv], w=[])
            dma("sp", o_sbv[l, t0 + 128 * t:t0 + 128 * t + 128, :], kv.t[:, 256:512], r=[kv], w=[])
        if isp:
            dma("sp", c_sbk[l, 0, :, :, k0new:k0new + 512], KA.t[:, :, :], r=[KA], w=[CB(l, 0, g)])
            dma("sp", c_sbv[l, 0, k0new:k0new + 512, :].rearrange("(t p) c -> p t c", p=128), VBF.t[:, :, :],
                r=[VBF], w=[CB(l, 0, g)])
        else:
            for j in range(NSS):
                dma("sp", c_sbk[l, 1 + j, :, :, PAST:PAST + 64], KA.t[:, :, 64 * j:64 * j + 64], r=[KA],
                    w=[CB(l, 1 + j, PAST // 512)])
                tt_, po = (64 * j) // 128, (64 * j) % 128
                dma("sp", c_sbv[l, 1 + j, PAST:PAST + 64, :], VBF.t[po:po + 64, tt_, :], r=[VBF],
                    w=[CB(l, 1 + j, PAST // 512)])
        ps = psr()
        proj_fm(ps, 0, 128, wt2, wv2, 256, NT)
        cp(QG.t[:, 0:NT], ps.t[:, 0:NT], r=[ps], w=[QG])
        ps = psr()
        proj_fm(ps, 0, 128, wt2, wv2, 384, NT)
        cp(KG.t[:, 0:NT], ps.t[:, 0:NT], r=[ps], w=[KG], eng="act")
        wt3, wv3 = wpiece(w_in[l], 0, 8, 1024, 528)
        for t in range(NTT):
            ps = psr()
            for kc in range(8):
                mm(ps.t[:, 0:256], HT.t[:, kc, 128 * t:128 * t + 128], wv3[:, kc, 0:256], kc == 0, kc == 7,
                   r=[HT, wt3], w=[ps])
            cp(VG.t[:, t, :], ps.t[:, 0:256], r=[ps], w=[VG], eng=("act" if t % 2 else "dve"))
        ps = psr()
        proj_fm(ps, 0, 16, wt3, wv3, 256, NT)
        cp(AG.t[:, 0:NT], ps.t[0:16, 0:NT], r=[ps], w=[AG])
        for c in range(2):
            ps = psr()
            proj_fm(ps, 0, 128, wt3, wv3, 272 + 128 * c, NT)
            act(SRG.t[:, c, 0:NT], ps.t[:, 0:NT], AF.Silu, r=[ps], w=[SRG])
        wt4, wv4 = wpiece(w_in[l], 0, 8, 1552, 416)
        psq = psr()
        for c in range(2):
            ps = psr()
            proj_fm(ps, 0, 128, wt4, wv4, 128 * c, NT)
            cp(CQ.t[:, c, 0:NT], ps.t[:, 0:NT], r=[ps], w=[CQ])
            sq = b1.get()
            act(sq.t[:, 0:NT], CQ.t[:, c, 0:NT], AF.Square, r=[CQ], w=[sq])
            mm(psq.t[:, 0:NT], ONES256, sq.t[:, 0:NT], c == 0, c == 1, r=[sq, CMb], w=[psq])
        rs = rstd_from(psq.t[:, 0:NT], psq, 0, 128, NT)
        for c in range(2):
            stt(CQN.t[:, c, 0:NT], CQ.t[:, c, 0:NT], gcol(l, 32 + c), rs.t[:, 0:NT], ALU.mult, ALU.mult,
                r=[CQ, rs, GT], w=[CQN])
        ps = psr()
        proj_fm(ps, 0, 128, wt4, wv4, 256, NT)
        sq = b1.get()
        act(sq.t[:, 0:NT], ps.t[:, 0:NT], AF.Square, r=[ps], w=[sq])
        ps2 = psr()
        mm(ps2.t[:, 0:NT], ONES128, sq.t[:, 0:NT], True, True, r=[sq, CMb], w=[ps2])
        rs = rstd_from(ps2.t[:, 0:NT], ps2, 0, 128, NT)
        stt(LAT.t[:, 0:NT], ps.t[:, 0:NT], gcol(l, 34), rs.t[:, 0:NT], ALU.mult, ALU.mult, r=[ps, rs, GT], w=[LAT])
        cp(LATB.t[:, 0:NT], LAT.t[:, 0:NT], r=[LAT], w=[LATB], eng="act")
        ps3 = psr()
        for t in range(NTT):
            tr(ps3.t[:, 128 * t:128 * t + 128], LAT.t[:, 128 * t:128 * t + 128], IDF, r=[LAT, CMf], w=[ps3])
        lo = f2.get()
        cp(lo.t[:, 0:NT], ps3.t[:, 0:NT], r=[ps3], w=[lo])
        dma("sp", o_lat[l, t0:t0 + NT, :].rearrange("(t p) c -> p t c", p=128),
            lo.t[:, 0:NT].rearrange("p (t c) -> p t c", t=NTT), r=[lo], w=[])
        ps = psr()
        proj_fm(ps, 64, 32, wt4, wv4, 384, NT)
        sq = b1.get()
        act(sq.t[64:96, 0:NT], ps.t[64:96, 0:NT], AF.Square, r=[ps], w=[sq])
        ps2 = psr()
        mm(ps2.t[64:96, 0:NT], BD96[64:96, 64:96], sq.t[64:96, 0:NT], True, True, r=[sq, CMb], w=[ps2])
        rs = rstd_from(ps2.t[64:96, 0:NT], ps2, 64, 96, NT)
        stt(KRN.t[64:96, 0:NT], ps.t[64:96, 0:NT], gcol(l, 36, 64, 96), rs.t[64:96, 0:NT], ALU.mult, ALU.mult,
            r=[ps, rs, GT], w=[KRN])
        rot = ROT.get()
        dma("sp", rot.t[64:80, 0:NT], KRN.t[80:96, 0:NT], r=[KRN], w=[rot])
        dma("sp", rot.t[80:96, 0:NT], KRN.t[64:80, 0:NT], r=[KRN], w=[rot])
        t1 = f2.get()
        tt(t1.t[64:96, 0:NT], KRN.t[64:96, 0:NT], COS.t[64:96, 0:NT], ALU.mult, r=[KRN, COS], w=[t1])
        t2 = f2.get()
        tt(t2.t[64:96, 0:NT], rot.t[64:96, 0:NT], SIN.t[64:96, 0:NT], ALU.mult, r=[rot, SIN], w=[t2])
        tt(KRN.t[64:96, 0:NT], t1.t[64:96, 0:NT], t2.t[64:96, 0:NT], ALU.add, r=[t1, t2], w=[KRN])
        cp(KRB.t[64:96, 0:NT], KRN.t[64:96, 0:NT], r=[KRN], w=[KRB], eng="act")
        kr0 = f2.get()
        dma("sp", kr0.t[0:32, 0:NT], KRN.t[64:96, 0:NT], r=[KRN], w=[kr0])
        ps3 = psr()
        for t in range(NTT):
            tr(ps3.t[:, 32 * t:32 * t + 32], kr0.t[0:32, 128 * t:128 * t + 128], IDF[0:32, 0:32], r=[kr0, CMf], w=[ps3])
        ko = f2.get()
        cp(ko.t[:, 0:32 * NTT], ps3.t[:, 0:32 * NTT], r=[ps3], w=[ko])
        dma("sp", o_kr[l, t0:t0 + NT, :].rearrange("(t p) c -> p t c", p=128),
            ko.t[:, 0:32 * NTT].rearrange("p (t c) -> p t c", t=NTT), r=[ko], w=[])
        mark("%s%d_l%d_mlaq" % (kind, g, l))
        wtq, wvq = wpiece(w_uq[l], 0, 2, 0, 768)
        def qhead(h):
            ps = psr()
            proj_fm(ps, 0, 96, wtq, wvq, 96 * h, NT, KC=2, rhs=CQN)
            yield
            sq = b1.get()
            act(sq.t[0:96, 0:NT], ps.t[0:96, 0:NT], AF.Square, r=[ps], w=[sq])
            yield
            ps2 = psr()
            mm(ps2.t[0:96, 0:NT], BD96[0:96, 0:96], sq.t[0:96, 0:NT], True, True, r=[sq, CMb], w=[ps2])
            yield
            rs = f2.get()
            act(rs.t[0:96, 0:NT], ps2.t[0:96, 0:NT], AF.Ln, r=[ps2], w=[rs], bias=EPSC[0:96, 0:1])
            yield
            act(rs.t[0:96, 0:NT], rs.t[0:96, 0:NT], AF.Exp, r=[rs], w=[rs], scale=-0.5)
            yield
            stt(QTm.t[0:64, h, 0:NT], ps.t[0:64, 0:NT], gcol(l, 35, 0, 64), rs.t[0:64, 0:NT], ALU.mult, ALU.mult,
                r=[ps, rs, GT], w=[QTm])
            qn = f2.get()
            stt(qn.t[64:96, 0:NT], ps.t[64:96, 0:NT], gcol(l, 35, 64, 96), rs.t[64:96, 0:NT], ALU.mult, ALU.mult,
                r=[ps, rs, GT], w=[qn])
            yield
            rot = ROT.get()
            dma("sp", rot.t[64:80, 0:NT], qn.t[80:96, 0:NT], r=[qn], w=[rot])
            dma("sp", rot.t[80:96, 0:NT], qn.t[64:80, 0:NT], r=[qn], w=[rot])
            yield
            t1 = f2.get()
            tt(t1.t[64:96, 0:NT], qn.t[64:96, 0:NT], COS.t[64:96, 0:NT], ALU.mult, r=[qn, COS], w=[t1])
            t2 = f2.get()
            tt(t2.t[64:96, 0:NT], rot.t[64:96, 0:NT], SIN.t[64:96, 0:NT], ALU.mult, r=[rot, SIN], w=[t2])
            tt(QTm.t[64:96, h, 0:NT], t1.t[64:96, 0:NT], t2.t[64:96, 0:NT], ALU.add, r=[t1, t2], w=[QTm])

        run_interleaved((qhead(h) for h in range(8)), 2, 4)
        wukv_t, wukv_v = load_wukv(l)
        if isp:
            mla_kside(l, 0, k0new, 512, LATB, KRB, wukv_t, wukv_v)
        else:
            for j in range(NSS):
                mla_kside_cols(l, 1 + j, PAST, 64 * j, 64, wukv_t, wukv_v)
        import os as _os
        _cut = int(_os.environ.get("KCUT", "99"))
        if _cut <= 1:
            return
        mark("%s%d_l%d_SB" % (kind, g, l))
        sb_attention(kind, g, l, NT)
        if _cut <= 2:
            return
        mark("%s%d_l%d_GLA" % (kind, g, l))
        gla(kind, g, l, NT, NTT)
        if _cut <= 3:
            return
        mark("%s%d_l%d_MLA" % (kind, g, l))
        mla_attention(kind, g, l, NT)
        mark("%s%d_l%d_Wout" % (kind, g, l))
        if _cut <= 4:
            return
        for half in range(2):
            wta, wva = wpiece(w_out[l], 0, 4, 512 * half, 512)
            wtm, wvm = wpiece(w_out[l], 512, 8, 512 * half, 512, p=64)
            for o4 in range(4):
                oc = 4 * half + o4
                ps = psr()
                cols = slice(128 * o4, 128 * o4 + 128)
                for c in range(2):
                    mm(ps.t[:, 0:NT], wva[:, c, cols], OAT.t[:, c, 0:NT], c == 0, False, r=[wta, OAT], w=[ps])
                for c in range(2):
                    mm(ps.t[:, 0:NT], wva[:, 2 + c, cols], OBB.t[:, c, 0:NT], False, False, r=[wta, OBB], w=[ps])
                for h in range(8):
                    mm(ps.t[:, 0:NT], wvm[:, h, cols], OCT.t[:, h, 0:NT], False, h == 7, r=[wtm, OCT], w=[ps])
                tt(XT.t[:, oc, 0:NT], ps.t[:, 0:NT], XT.t[:, oc, 0:NT], ALU.add, r=[ps, XT], w=[XT])
        mark("%s%d_l%d_cross" % (kind, g, l))
        norm_x(l, 8, NT)
        wt, wv = wpiece(w_cq[l], 0, 8, 0, 512)
        for h in range(4):
            ps = psr()
            proj_fm(ps, 0, 128, wt, wv, 128 * h, NT)
            sq = b1.get()
            act(sq.t[:, 0:NT], ps.t[:, 0:NT], AF.Square, r=[ps], w=[sq])
            ps2 = psr()
            mm(ps2.t[:, 0:NT], ONES128, sq.t[:, 0:NT], True, True, r=[sq, CMb], w=[ps2])
            rs = rstd_from(ps2.t[:, 0:NT], ps2, 0, 128, NT)
            stt(QCT.t[:, h, 0:NT], ps.t[:, 0:NT], gcol(l, 40), rs.t[:, 0:NT], ALU.mult, ALU.mult,
                r=[ps, rs, GT], w=[QCT])
        csegs = [(0, 0, NT)] if isp else [(1 + j, 64 * j, 64) for j in range(NSS)]
        for (seq, c0, ncl) in csegs:
            mk = MEMK.get()
            mv = MEMV.get()
            dma("sp", mk.t[:], c_memk[l, seq], r=[membuf[(l, seq)]], w=[mk])
            dma("sp", mv.t[:], c_memv[l, seq].rearrange("(t p) c -> p t c", p=128), r=[membuf[(l, seq)]], w=[mv])
            for h in range(4):
                po = psl()
                pd = psr()
                for t in range(2):
                    ps = psr()
                    mm(ps.t[:, 0:ncl], mk.t[:, h, 128 * t:128 * t + 128], QCT.t[:, h, c0:c0 + ncl], True, True,
                       r=[mk, QCT], w=[ps])
                    pt = b1.get()
                    act(pt.t[:, 0:ncl], ps.t[:, 0:ncl], AF.Exp, r=[ps], w=[pt], scale=128.0 ** -0.5)
                    mm(po.t[:, 0:ncl], mv.t[:, t, 128 * h:128 * h + 128], pt.t[:, 0:ncl], t == 0, t == 1,
                       r=[mv, pt], w=[po])
                    mm(pd.t[:, 0:ncl], cmb(10), pt.t[:, 0:ncl], t == 0, t == 1, r=[CMb, pt], w=[pd])
                rd = f2.get()
                S.op("dve", lambda e, o=rd.t[:, 0:ncl], i=pd.t[:, 0:ncl]: e.reciprocal(o, i), r=[pd], w=[rd])
                tt(OCR.t[:, h, c0:c0 + ncl], po.t[:, 0:ncl], rd.t[:, 0:ncl], ALU.mult, r=[po, rd], w=[OCR])
        wt, wv = wpiece(w_co[l], 0, 4, 0, 1024)
        for oc in range(8):
            ps = psr()
            for h in range(4):
                mm(ps.t[:, 0:NT], wv[:, h, 128 * oc:128 * oc + 128], OCR.t[:, h, 0:NT], h == 0, h == 3,
                   r=[wt, OCR], w=[ps])
            tt(XT.t[:, oc, 0:NT], ps.t[:, 0:NT], XT.t[:, oc, 0:NT], ALU.add, r=[ps, XT], w=[XT])
        mark("%s%d_l%d_FFN" % (kind, g, l))
        norm_x(l, 16, NT)
        for pc in range(6):
            nf = 4 if pc < 5 else 2
            wtg, wvg = wpiece(w_gate[l], 0, 8, 512 * pc, 128 * nf)
            wtu, wvu = wpiece(w_up[l], 0, 8, 512 * pc, 128 * nf)
            for fi in range(nf):
                f = 4 * pc + fi
                pg = psr()
                proj_fm(pg, 0, 128, wtg, wvg, 128 * fi, NT)
                pu = psr()
                proj_fm(pu, 0, 128, wtu, wvu, 128 * fi, NT)
                sg = f2.get()
                act(sg.t[:, 0:NT], pg.t[:, 0:NT], AF.Silu, r=[pg], w=[sg])
                tt(AT.t[:, f, 0:NT], pu.t[:, 0:NT], sg.t[:, 0:NT], ALU.mult, r=[pu, sg], w=[AT])
        for oc in range(8):
            wt, wv = wpiece(w_down[l], 0, 22, 128 * oc, 128)
            ps = psr()
            for f in range(22):
                mm(ps.t[:, 0:NT], wv[:, f, :], AT.t[:, f, 0:NT], f == 0, f == 21, r=[wt, AT], w=[ps])
            tt(XT.t[:, oc, 0:NT], ps.t[:, 0:NT], XT.t[:, oc, 0:NT], ALU.add, r=[ps, XT], w=[XT])

    def mla_kside_cols(l, seq, k0, c0, n, wukv_t, wukv_v):
        lb = LATC
        cp(lb.t[:, 0:n], LATB.t[:, c0:c0 + n], r=[LATB], w=[lb])
        kb2 = KRC
        cp(kb2.t[64:96, 0:n], KRB.t[64:96, c0:c0 + n], r=[KRB], w=[kb2], eng="act")
        mla_kside(l, seq, k0, n, lb, kb2, wukv_t, wukv_v)

    def sb_attention(kind, g, l, NT):
        isp = kind == "p"
        if isp:
            units = [(0, 128 * j, 128, j) for j in range(4)]
            nkb = g + 1
        else:
            units = [(1 + j, 64 * j, 64, j) for j in range(NSS)]
            nkb = NKBS
        memset(NEGC.t[:], 0.0, w=[NEGC] + NEGCB)
        seq_list = [0] if isp else list(range(1, 1 + NSS))
        zpi = [0, 0]
        ZPOOL = [PSR[0], PSR[1], PSR[2], PSL[0]]
        TPOOL = [PSR[3], PSR[4], PSR[5], PSL[1]]

        def unit(kbk, vbk, qc0, nq, slot, h, N, diag, last):
            col = 4 * slot + h
            ncb = NEGCB[col]
            zpi[0] += 1
            ps = ZPOOL[zpi[0] % 4]
            mm(ps.t[0:nq, 0:N], QA.t[:, h, qc0:qc0 + nq], kbk.t[:, h, 0:N], True, True, r=[QA, kbk], w=[ps])
            yield
            e1 = f2.get()
            act(e1.t[0:nq, 0:N], ps.t[0:nq, 0:N], AF.Exp, r=[ps], w=[e1], scale=0.125)
            yield
            act(e1.t[0:nq, 0:N], e1.t[0:nq, 0:N], AF.Ln, r=[e1], w=[e1], bias=ONESF.t[0:nq, 0:1])
            yield
            if diag:
                tt(e1.t[0:nq, N - nq:N], e1.t[0:nq, N - nq:N], MASKL01[0:nq, 0:nq], ALU.mult, r=[e1, CMf], w=[e1])
            ft = FT.get()
            S.op("dve", lambda e, o=ft.t[0:nq, 1:N + 1], d0=ONESF.t[0:nq, 0:N], d1=e1.t[0:nq, 0:N]:
                 e.tensor_tensor_scan(out=o, data0=d0, data1=d1, initial=0.0, op0=ALU.mult, op1=ALU.add),
                 r=[e1, ONESF], w=[ft])
            tt(NEGC.t[0:nq, col:col + 1], NEGC.t[0:nq, col:col + 1], ft.t[0:nq, N:N + 1], ALU.subtract,
               r=[ncb, ft], w=[ncb])
            yield
            stt(e1.t[0:nq, 0:N], ps.t[0:nq, 0:N], 0.125, ft.t[0:nq, 0:N], ALU.mult, ALU.add, r=[ps, ft], w=[e1])
            if diag:
                tt(e1.t[0:nq, N - nq:N], e1.t[0:nq, N - nq:N], MASKLNEG[0:nq, 0:nq], ALU.add, r=[e1, CMf], w=[e1])
            yield
            act(e1.t[0:nq, 0:N], e1.t[0:nq, 0:N], AF.Exp, r=[e1, ncb], w=[e1], bias=NEGC.t[0:nq, col:col + 1])
            yield
            nsub = (N + 127) // 128
            zpi[1] += 1
            pst = TPOOL[zpi[1] % 4]
            for i in range(nsub):
                w_ = min(128, N - 128 * i)
                tr(pst.t[0:w_, 128 * i:128 * i + nq], e1.t[0:nq, 128 * i:128 * i + w_], IDF[0:nq, 0:nq],
                   r=[e1, CMf], w=[pst])
            yield
            wT = b1.get()
            wmax = min(128, N)
            if nq == 128:
                cp(wT.t[0:wmax, 0:128 * nsub], pst.t[0:wmax, 0:128 * nsub], r=[pst], w=[wT], eng="act")
            else:
                cp(wT.t[0:wmax, 0:128 * nsub].rearrange("p (a b) -> p a b", a=nsub)[:, :, 0:nq],
                   pst.t[0:wmax, 0:128 * nsub].rearrange("p (a b) -> p a b", a=nsub)[:, :, 0:nq], r=[pst], w=[wT],
                   eng="act")
            yield
            po = ps
            for i in range(nsub):
                w_ = min(128, N - 128 * i)
                mm(po.t[0:nq, 0:64], wT.t[0:w_, 128 * i:128 * i + nq], vbk.t[0:w_, i, 64 * h:64 * h + 64],
                   i == 0, i == nsub - 1, r=[wT, vbk], w=[po])
            yield
            if last:
                cp(OSB.t[0:nq, slot, 64 * h:64 * h + 64], po.t[0:nq, 0:64], r=[po], w=[OSB])
            else:
                tt(OSB.t[0:nq, slot, 64 * h:64 * h + 64], po.t[0:nq, 0:64],
                   OSB.t[0:nq, slot, 64 * h:64 * h + 64], ALU.add, r=[po, OSB], w=[OSB])

        def all_units():
            for seq in seq_list:
                us = [u for u in units if u[0] == seq]
                for kbi in range(nkb - 1, -1, -1):
                    kbk = KBK.get()
                    vbk = VBK.get()
                    last = kbi == nkb - 1
                    nkeys = 512 if (isp or not last) else 64
                    dma(LQ, kbk.t[:, :, 0:nkeys], c_sbk[l, seq, :, :, 512 * kbi:512 * kbi + nkeys],
                        r=[CB(l, seq, kbi)], w=[kbk])
                    if nkeys == 512:
                        dma(LQ, vbk.t[:], c_sbv[l, seq, 512 * kbi:512 * kbi + 512, :].rearrange(
                            "(t p) c -> p t c", p=128), r=[CB(l, seq, kbi)], w=[vbk])
                    else:
                        dma(LQ, vbk.t[0:64, 0, :], c_sbv[l, seq, 512 * kbi:512 * kbi + 64, :],
                            r=[CB(l, seq, kbi)], w=[vbk])
                    for (sq_, qc0, nq, slot) in us:
                        N = 128 * (slot + 1) if (isp and last) else nkeys
                        for h in range(4):
                            yield unit(kbk, vbk, qc0, nq, slot, h, N, last, last)

        run_interleaved(all_units(), SBW, SBG)
        for (sq_, qc0, nq, slot) in units:
            sqt = f2.get()
            act(sqt.t[0:nq, 0:256], OSB.t[0:nq, slot, :], AF.Square, r=[OSB], w=[sqt])
            ms = SMALL.get()
            S.op("dve", lambda e, o=ms.t[0:nq, 0:4], i=sqt.t[0:nq, 0:256].rearrange("p (h c) -> p h c", h=4):
                 e.tensor_reduce(out=o, in_=i, axis=AX.X, op=ALU.add), r=[sqt], w=[ms])
            l1 = SMALL.get()
            act(l1.t[0:nq, 0:4], ms.t[0:nq, 0:4], AF.Ln, r=[ms], w=[l1], bias=EPSC[0:nq, 0:1], scale=1.0 / 64)
            r1 = SMALL.get()
            act(r1.t[0:nq, 0:4], l1.t[0:nq, 0:4], AF.Exp, r=[l1], w=[r1], scale=-0.5)
            on = f2.get()
            for h in range(4):
                ts(on.t[0:nq, 64 * h:64 * h + 64], OSB.t[0:nq, slot, 64 * h:64 * h + 64], r1.t[0:nq, h:h + 1], None,
                   ALU.mult, None, r=[OSB, r1], w=[on])
            for c in range(2):
                ps = psr()
                tr(ps.t[:, 0:nq], on.t[0:nq, 128 * c:128 * c + 128], IDF[0:nq, 0:nq], r=[on, CMf], w=[ps])
                ts(OAT.t[:, c, qc0:qc0 + nq], ps.t[:, 0:nq], gcol(l, 37), None, ALU.mult, None, r=[ps, GT], w=[OAT])

    def gla(kind, g, l, NT, NTT):
        isp = kind == "p"
        C = 128 if isp else 64
        nch = NT // C
        ps = psr()
        mm(ps.t[:, 0:NT], WGG.t[:, l, :], AG.t[:, 0:NT], True, True, r=[WGG, AG], w=[ps])
        e1 = f2.get()
        act(e1.t[:, 0:NT], ps.t[:, 0:NT], AF.Exp, r=[ps, NEGB], w=[e1], scale=-1.0, bias=NEGB.t[:, l:l + 1])
        act(SPG.t[:, 0:NT], e1.t[:, 0:NT], AF.Ln, r=[e1], w=[SPG], bias=ONESF.t[:, 0:1])
        for c in range(nch):
            S.op("dve", lambda e, o=CS.t[:, C * c:C * c + C], d0=ONESF.t[:, 0:C], d1=SPG.t[:, C * c:C * c + C]:
                 e.tensor_tensor_scan(out=o, data0=d0, data1=d1, initial=0.0, op0=ALU.mult, op1=ALU.add),
                 r=[SPG, ONESF], w=[CS])
            ts(NBL.t[:, c:c + 1], CS.t[:, C * c + C - 1:C * c + C], -1.0 / 16, None, ALU.mult, None, r=[CS], w=[NBL])
        act(EBt.t[:, 0:NT], CS.t[:, 0:NT], AF.Exp, r=[CS], w=[EBt], scale=-1.0 / 16)
        ebi = f2.get()
        act(ebi.t[:, 0:NT], CS.t[:, 0:NT], AF.Exp, r=[CS], w=[ebi], scale=1.0 / 16)
        stt(QTg.t[:, 0:NT], QG.t[:, 0:NT], 32.0 ** -0.5, EBt.t[:, 0:NT], ALU.mult, ALU.mult, r=[QG, EBt], w=[QTg])
        tt(KTg.t[:, 0:NT], KG.t[:, 0:NT], ebi.t[:, 0:NT], ALU.mult, r=[KG, ebi], w=[KTg])
        KTm = Tl(AT.t[:, 0:4, :], AT.b)
        for h in range(4):
            ts(KTm.t[:, h, 0:NT], KTg.t[:, 0:NT], GT.t[:, 44 + h:45 + h], None, ALU.mult, None, r=[KTg, GT], w=[KTm])
        ekd = f2.get()
        for c in range(nch):
            act(ekd.t[:, C * c:C * c + C], CS.t[:, C * c:C * c + C], AF.Exp, r=[CS, NBL], w=[ekd], scale=1.0 / 16,
                bias=NBL.t[:, c:c + 1])
        tt(KDg.t[:, 0:NT], KG.t[:, 0:NT], ekd.t[:, 0:NT], ALU.mult, r=[KG, ekd], w=[KDg])
        st = SALL[l] if isp else SS
        if isp and g == 0:
            memset(st.t[:], 0.0, w=[st])
        BMASK = CMf.t[:, 11 * 128:13 * 128]
        for c in range(nch):
            if not isp:
                memset(st.t[:], 0.0, w=[st])
                for h in range(4):
                    dma("sp", st.t[32 * h:32 * h + 32, 64 * h:64 * h + 64], sgla[l, c, h], r=[], w=[st])
            tt(SBF16.t[:], st.t[:], BMASK, ALU.mult, r=[st, CMf], w=[SBF16])
            cs_ = slice(C * c, C * c + C)
            pk = psr()
            tr(pk.t[0:C, 0:128], KDg.t[:, cs_], IDF, r=[KDg, CMf], w=[pk])
            kdt = KDt.get()
            cp(kdt.t[0:C, :], pk.t[0:C, 0:128], r=[pk], w=[kdt])
            tt_, po_ = (C * c) // 128, (C * c) % 128
            if isp:
                vsrc, vt = VG, VG.t[:, tt_, :]
            else:
                dma("sp", VGLO.t[0:C, :], VG.t[po_:po_ + C, tt_, :], r=[VG], w=[VGLO])
                vsrc, vt = VGLO, VGLO.t[:, :]
            for h in range(4):
                pa = psr()
                mm(pa.t[0:C, 0:C], KTm.t[:, h, cs_], QTg.t[:, cs_], True, True, r=[KTm, QTg], w=[pa])
                am = ATM.get()
                tt(am.t[0:C, 0:C], pa.t[0:C, 0:C], MASKU01[0:C, 0:C], ALU.mult, r=[pa, CMf], w=[am])
                po = psr()
                pr = slice(64 * (h % 2), 64 * (h % 2) + 64)
                mm(po.t[pr, 0:C], vt[:, 64 * h:64 * h + 64], am.t[:, 0:C], True, False, r=[vsrc, am], w=[po])
                mm(po.t[pr, 0:C], SBF16.t[:, 64 * h:64 * h + 64], QTg.t[:, cs_], False, True, r=[SBF16, QTg], w=[po])
                cp(OBT.t[pr, h // 2, cs_], po.t[pr, 0:C], r=[po], w=[OBT], eng=("act" if h % 2 else "dve"))
            pu = psr()
            mm(pu.t[:, 0:256], kdt.t[:, :], vt, True, True, r=[kdt, vsrc], w=[pu])
            stt(st.t[:], st.t[:], EBt.t[:, C * c + C - 1:C * c + C], pu.t[:, 0:256], ALU.mult, ALU.add,
                r=[st, EBt, pu], w=[st])
            if not isp:
                for h in range(4):
                    dma("sp", o_gla_s[l, c, h], st.t[32 * h:32 * h + 32, 64 * h:64 * h + 64], r=[st], w=[])
        if isp and g == NG - 1:
            for h in range(4):
                dma("sp", o_gla_p[l, h], st.t[32 * h:32 * h + 32, 64 * h:64 * h + 64], r=[st], w=[])
        for c2 in range(2):
            sq = b1.get()
            act(sq.t[:, 0:NT], OBT.t[:, c2, 0:NT], AF.Square, r=[OBT], w=[sq])
            ps2 = psr()
            mm(ps2.t[:, 0:NT], BD64, sq.t[:, 0:NT], True, True, r=[sq, CMb], w=[ps2])
            rs = rstd_from(ps2.t[:, 0:NT], ps2, 0, 128, NT)
            tmp = f2.get()
            stt(tmp.t[:, 0:NT], OBT.t[:, c2, 0:NT], gcol(l, 38), rs.t[:, 0:NT], ALU.mult, ALU.mult,
                r=[OBT, rs, GT], w=[tmp])
            tt(OBB.t[:, c2, 0:NT], tmp.t[:, 0:NT], SRG.t[:, c2, 0:NT], ALU.mult, r=[tmp, SRG], w=[OBB])

    VGLO = S.sb([128, 256], BF, "VGLO")
    memset(VGLO.t[:], 0.0, w=[VGLO])
    for _t in KDt.tiles + ATM.tiles:
        memset(_t.t[:], 0.0, w=[_t])
    LATC = S.sb([128, 64], BF, "LATC")
    KRC = S.sb([96, 64], BF, "KRC")

    def mla_attention(kind, g, l, NT):
        isp = kind == "p"
        SC = 96.0 ** -0.5
        segs = [(0, 0, 512, g + 1)] if isp else [(1 + j, 64 * j, 64, NKBS) for j in range(NSS)]
        for (seq, c0, ncl, nkb) in segs:
            for h in range(8):
                po = psl()

                def step(kb_, vb_, i, w_, first, final, dmask):
                    ps = psr()
                    mm(ps.t[0:w_, 0:ncl], kb_.t[:, 128 * i:128 * i + w_], QTm.t[:, h, c0:c0 + ncl], True, True,
                       r=[kb_, QTm], w=[ps])
                    yield
                    pt = b1.get()
                    act(pt.t[0:w_, 0:ncl], ps.t[0:w_, 0:ncl], AF.Exp, r=[ps, NEG4], w=[pt], scale=SC,
                        bias=NEG4.t[0:w_, 0:1])
                    if dmask:
                        tt(pt.t[:, 0:512], pt.t[:, 0:512], MDb.t[:, 512 * i:512 * i + 512], ALU.mult,
                           r=[pt, MDb], w=[pt])
                    yield
                    mm(po.t[0:65, 0:ncl], vb_.t[0:w_, i, :], pt.t[0:w_, 0:ncl], first, final, r=[vb_, pt], w=[po])

                def step_blk(kb_, vb_, nsub, w_, first, final):
                    ps = psr()
                    for i in range(nsub):
                        mm(ps.t[0:w_, ncl * i:ncl * i + ncl], kb_.t[:, 128 * i:128 * i + w_], QTm.t[:, h, c0:c0 + ncl],
                           True, True, r=[kb_, QTm], w=[ps])
                    yield
                    pt = b1.get()
                    act(pt.t[0:w_, 0:ncl * nsub], ps.t[0:w_, 0:ncl * nsub], AF.Exp, r=[ps, NEG4], w=[pt], scale=SC,
                        bias=NEG4.t[0:w_, 0:1])
                    yield
                    for i in range(nsub):
                        mm(po.t[0:65, 0:ncl], vb_.t[0:w_, i, :], pt.t[0:w_, ncl * i:ncl * i + ncl],
                           first and i == 0, final and i == nsub - 1, r=[vb_, pt], w=[po])

                def all_steps():
                    for kbi in range(nkb):
                        last = kbi == nkb - 1
                        nkeys = 512 if (isp or not last) else 64
                        kb_ = MKB.get()
                        vb_ = MVB.get()
                        dma(MLQ, kb_.t[:, 0:nkeys], c_mk[l, seq, :, h, 512 * kbi:512 * kbi + nkeys],
                            r=[CB(l, seq, kbi)], w=[kb_])
                        if nkeys == 512:
                            dma(MLQ, vb_.t[:], c_mv[l, seq, h, 512 * kbi:512 * kbi + 512, :].rearrange(
                                "(t p) c -> p t c", p=128), r=[CB(l, seq, kbi)], w=[vb_])
                        else:
                            dma(MLQ, vb_.t[0:64, 0, :], c_mv[l, seq, h, 512 * kbi:512 * kbi + 64, :],
                                r=[CB(l, seq, kbi)], w=[vb_])
                        nsub = (nkeys + 127) // 128
                        if not isp:
                            yield step_blk(kb_, vb_, nsub, min(128, nkeys), kbi == 0, last)
                            continue
                        for i in range(nsub):
                            w_ = min(128, nkeys - 128 * i)
                            yield step(kb_, vb_, i, w_, kbi == 0 and i == 0, last and i == nsub - 1, isp and last)

                run_interleaved(all_steps(), MLW if isp else 2)
                sq = b1.get()
                act(sq.t[0:65, 0:ncl], po.t[0:65, 0:ncl], AF.Square, r=[po], w=[sq])
                ps2 = psr()
                mm(ps2.t[0:64, 0:ncl], W65[0:65, 0:64], sq.t[0:65, 0:ncl], True, True, r=[sq, CMb], w=[ps2])
                t1 = f2.get()
                act(t1.t[0:64, 0:ncl], ps2.t[0:64, 0:ncl], AF.Ln, r=[ps2], w=[t1])
                t2 = f2.get()
                act(t2.t[0:64, 0:ncl], t1.t[0:64, 0:ncl], AF.Exp, r=[t1], w=[t2], scale=-0.5)
                stt(OCT.t[:, h, c0:c0 + ncl], po.t[0:64, 0:ncl], gcol(l, 39, 0, 64), t2.t[0:64, 0:ncl], ALU.mult,
                    ALU.mult, r=[po, t2, GT], w=[OCT])

    NEG4 = S.sb([128, 1], F32, "NEG4")
    memset(NEG4.t[:], -4.0, w=[NEG4])

    import os as _os
    _ph = _os.environ.get("KPHASES", "abcd")
    mark("prep_mem")
    if "a" in _ph:
        prep_mem_prompt()
    mark("prep_sample")
    if "b" in _ph:
        prep_sample_caches()
    if "c" in _ph:
        group("s", 0)
    if "d" in _ph:
        for g in range(NG):
            group("p", g)
    mark("end")
    S.finish()
    es.close()
    return nc


def host_consts(cfg):
    cm = np.zeros((128, 13, 128), np.float32)
    cm[:, 0] = np.eye(128)
    cm[:, 1] = 1.0 / 1024
    cm[:, 2] = 1.0 / 256
    cm[:, 3] = 1.0 / 128
    cm[0:64, 4, 0:64] = 1.0 / 64
    cm[64:128, 4, 64:128] = 1.0 / 64
    cm[0:64, 5, 0:64] = 1.0 / 64
    cm[64:96, 5, 64:96] = 1.0 / 32
    cm[0:64, 6, 0:64] = 1.0 / 64
    cm[64, 6, 0:64] = EPS
    q = np.arange(128)[:, None]
    k = np.arange(128)[None, :]
    cm[:, 7] = (k < q)
    cm[:, 8] = np.where(k < q, 0.0, -30000.0)
    cm[:, 9] = (q <= k)
    cm[:, 10] = 1.0
    bm = (np.arange(128)[:, None] // 32 == np.arange(256)[None, :] // 64).astype(np.float32)
    cm[:, 11] = bm[:, 0:128]
    cm[:, 12] = bm[:, 128:256]
    cmat = cm.reshape(128, 13 * 128)
    half = 16
    freqs = (10000.0 ** (-np.arange(half, dtype=np.float32) / half)).astype(np.float32)
    pos = np.arange(cfg.NPOS, dtype=np.float32)
    ang = (pos[None, :] * freqs[:, None]).astype(np.float32)
    c = np.cos(ang).astype(np.float32)
    s = np.sin(ang).astype(np.float32)
    ropec = np.ones((96, cfg.NPOS), np.float32)
    ropes = np.zeros((96, cfg.NPOS), np.float32)
    ropec[64:80] = c
    ropec[80:96] = c
    ropes[64:80] = -s
    ropes[80:96] = s
    md = np.zeros((128, 4, 512), np.float32)
    kk = np.arange(128)[:, None]
    qq = np.arange(512)[None, :]
    for i in range(4):
        md[:, i, :] = ((128 * i + kk) // 64 <= qq // 64)
    return cmat, ropec, ropes, md.reshape(128, 2048)


def host_gtab(inp):
    gt = np.ones((128, 2 * GL), np.float32)
    for l in range(2):
        b = GL * l
        gt[:, b + 0:b + 8] = inp["g_mix_norm"][l].reshape(8, 128).T
        gt[:, b + 8:b + 16] = inp["g_cross_norm"][l].reshape(8, 128).T
        gt[:, b + 16:b + 24] = inp["g_ffn_norm"][l].reshape(8, 128).T
        gt[:, b + 24:b + 32] = inp["g_mem_norm"][l].reshape(8, 128).T
        gt[:, b + 32:b + 34] = inp["g_cq"][l].reshape(2, 128).T
        gt[:, b + 34] = inp["g_ckv"][l]
        gt[0:64, b + 35] = inp["g_qn"][l]
        gt[64:96, b + 35] = inp["g_qr"][l]
        gt[0:64, b + 36] = inp["g_kn"][l]
        gt[64:96, b + 36] = inp["g_kr"][l]
        gt[0:64, b + 37] = inp["g_sb_out"][l]
        gt[64:128, b + 37] = inp["g_sb_out"][l]
        gt[0:64, b + 38] = inp["g_gla_out"][l]
        gt[64:128, b + 38] = inp["g_gla_out"][l]
        gt[0:64, b + 39] = inp["g_mla_out"][l]
        gt[:, b + 40] = inp["g_cqn"][l]
        gt[:, b + 41] = inp["g_ckn"][l]
        gt[:, b + 42] = inp["b_gla_gate"][l]
    for h in range(4):
        gt[:, 44 + h] = 0.0
        gt[32 * h:32 * h + 32, 44 + h] = 1.0
    return gt


_NC_CACHE = {}


def run(inp, cfg):
    key = (cfg.SEQ, cfg.PAST, cfg.NSS)
    if key not in _NC_CACHE:
        _NC_CACHE[key] = build(cfg)
    nc = _NC_CACHE[key]
    f = lambda a: np.ascontiguousarray(np.asarray(a, dtype=np.float32))
    cmat, ropec, ropes, md = host_consts(cfg)
    gt = host_gtab({k: np.asarray(v) for k, v in inp.items()})
    NSS = cfg.NSS
    shared = {k: f(inp[k]) for k in ("w_in", "w_gla_gate", "w_uq", "w_ukv", "w_out", "w_cq", "w_ck", "w_cv", "w_co",
                                     "w_gate", "w_up", "w_down")}
    shared.update(gtab=gt, cmat=cmat, ropec=ropec, ropes=ropes, mdiag=md)
    in_maps = []
    for c in range(8):
        sl = slice(NSS * c, NSS * c + NSS)
        m = dict(shared)
        m["xp"] = f(inp["x_prompt"][c // 2])
        m["xs"] = f(inp["x_sample"][sl]).reshape(NSS * 64, D)
        m["memp"] = f(inp["mem_prompt"][c // 2])
        m["csk"] = f(inp["cache_sb_k"][:, sl]).reshape(2, NSS, cfg.PAST, 256)
        m["csv"] = f(inp["cache_sb_v"][:, sl]).reshape(2, NSS, cfg.PAST, 256)
        m["sgla"] = f(inp["state_gla"][:, sl])
        m["clat"] = f(inp["cache_mla_latent"][:, sl])
        m["ckr"] = f(inp["cache_mla_krope"][:, sl])
        m["cmk"] = f(inp["cache_mem_k"][:, sl]).reshape(2, NSS, 256, 512)
        m["cmv"] = f(inp["cache_mem_v"][:, sl]).reshape(2, NSS, 256, 512)
        in_maps.append(m)
    res = run_bass_kernel_spmd(nc, in_maps, core_ids=list(range(8)))
    R = res.results
    B = 4
    SEQ = cfg.SEQ
    DB = 8 * NSS
    cat_p = lambda name, shp: np.stack([np.asarray(R[2 * b][name]).reshape(shp) for b in range(B)])
    y_p = cat_p("yp", (SEQ, D))
    y_s = np.concatenate([np.asarray(R[c]["ys"]).reshape(NSS, 64, D) for c in range(8)], 0)

    def pl(name, shp):
        return np.stack([np.asarray(R[2 * b][name]).reshape((2,) + shp) for b in range(B)], 1)

    def sl_(name, shp):
        return np.concatenate([np.asarray(R[c][name]).reshape((2, NSS) + shp) for c in range(8)], 1)

    outs = (y_p, y_s,
            pl("sbk_p", (SEQ, 4, 64)), pl("sbv_p", (SEQ, 4, 64)), pl("gla_p", (4, 32, 64)),
            pl("lat_p", (SEQ, 128)), pl("kr_p", (SEQ, 32)), pl("mk_p", (256, 4, 128)), pl("mv_p", (256, 4, 128)),
            sl_("sbk_s", (64, 4, 64)), sl_("sbv_s", (64, 4, 64)), sl_("gla_s", (4, 32, 64)),
            sl_("lat_s", (64, 128)), sl_("kr_s", (64, 32)))
    return tuple(np.ascontiguousarray(o.astype(np.float32)) for o in outs)


def kernel(**inputs):
    return run(inputs, Cfg())
```
